# Optimizing a Trainium2 kernel written in Bass

```python
import math
import jax, jax.numpy as jnp
from jax import lax
import numpy as np

D_MODEL = 1024
BATCH = 4
SEQ = 4096
DEPTH = 2

N_A_LAYERS = DEPTH // 2
N_B_LAYERS = DEPTH - N_A_LAYERS
EPS = 1e-6

EXPAND = 2
CONV_D = EXPAND * D_MODEL
CONV_WIDTH = 3

N_HEADS = 16
HEAD_DIM = D_MODEL // N_HEADS
N_KV_GROUPS = 4
HEADS_PER_GROUP = N_HEADS // N_KV_GROUPS
ATTN_D = N_HEADS * HEAD_DIM
KV_D = N_KV_GROUPS * HEAD_DIM
N_BRANCH = 3
N_KV_STREAMS = 6
ROT_DIM = HEAD_DIM // 4
ROPE_THETA = 500000.0
CMP_BLOCK = 32
CMP_STRIDE = 16
CMP_R = CMP_BLOCK // CMP_STRIDE
CMP_HIDDEN = 4 * HEAD_DIM
SLC_BLOCK = 64
SLC_RATIO = SLC_BLOCK // CMP_STRIDE
N_SELECT = 16
N_LOCAL = 2
WINDOW = 512
Q_BLOCK = 64
FORCE_SCORE = 1e4

kernel_name = "yoco_shortconv_nsa_hybrid"


def rmsnorm(x, g):
    xf = x.astype(jnp.float32)
    y = xf * lax.rsqrt(jnp.mean(xf * xf, axis=-1, keepdims=True) + EPS)
    return (y * g.astype(jnp.float32)).astype(x.dtype)


def partial_rope(x):
    S = x.shape[1]
    pos = jnp.arange(S, dtype=jnp.float32)
    inv = ROPE_THETA ** (-jnp.arange(0, ROT_DIM, 2, dtype=jnp.float32) / ROT_DIM)
    ang = pos[:, None] * inv[None, :]
    cos = jnp.cos(ang)[None, :, None, :]
    sin = jnp.sin(ang)[None, :, None, :]
    xr = x[..., :ROT_DIM].astype(jnp.float32)
    x1, x2 = xr[..., :ROT_DIM // 2], xr[..., ROT_DIM // 2:]
    rot = jnp.concatenate([x1 * cos - x2 * sin, x2 * cos + x1 * sin], axis=-1).astype(x.dtype)
    return jnp.concatenate([rot, x[..., ROT_DIM:]], axis=-1)


def masked_softmax(s, mask, axis=-1):
    s = jnp.where(mask, s.astype(jnp.float32), -jnp.inf)
    m = jnp.max(s, axis=axis, keepdims=True)
    m = jnp.where(jnp.isfinite(m), m, 0.0)
    e = jnp.where(mask, jnp.exp(s - m), 0.0)
    d = jnp.sum(e, axis=axis, keepdims=True)
    return e / jnp.maximum(d, 1e-30)


def short_conv_layer(x, norm_g, w_in, conv_w, w_out):
    S = x.shape[1]
    h = rmsnorm(x, norm_g)
    proj = h @ w_in
    b, c, u, z = jnp.split(proj, 4, axis=-1)
    vp = jnp.pad(c * u, ((0, 0), (CONV_WIDTH - 1, 0), (0, 0)))
    conv = conv_w[0] * vp[:, 0:S]
    for k in range(1, CONV_WIDTH):
        conv = conv + conv_w[k] * vp[:, k:k + S]
    y = b * conv * jax.nn.silu(z)
    return x + y @ w_out


def compress_blocks(kv, pos_emb, w1, w2):
    B, S, G, D = kv.shape
    chunks = kv.reshape(B, S // CMP_STRIDE, CMP_STRIDE, G, D)
    n_cmp = S // CMP_STRIDE - CMP_R + 1
    blocks = jnp.concatenate([chunks[:, r:r + n_cmp] for r in range(CMP_R)], axis=2)
    blocks = blocks + pos_emb[None, None, :, None, :]
    flat = blocks.transpose(0, 1, 3, 2, 4).reshape(B, n_cmp, G, CMP_BLOCK * D)
    return jax.nn.gelu(flat @ w1) @ w2


def shared_kv(h, norm_g, w_kv, cmp_pos_k, cmp_w1_k, cmp_w2_k, cmp_pos_v, cmp_w1_v, cmp_w2_v):
    B, S, _ = h.shape
    kv = (rmsnorm(h, norm_g) @ w_kv).reshape(B, S, N_KV_STREAMS, N_KV_GROUPS, HEAD_DIM)
    k_c, v_c, k_s, v_s, k_w, v_w = [kv[:, :, i] for i in range(N_KV_STREAMS)]
    k_cmp = compress_blocks(k_c, cmp_pos_k, cmp_w1_k, cmp_w2_k)
    v_cmp = compress_blocks(v_c, cmp_pos_v, cmp_w1_v, cmp_w2_v)
    return (k_cmp, v_cmp, partial_rope(k_s), v_s, partial_rope(k_w), v_w)


def nsa_attention(q, q_rot, gates, k_cmp, v_cmp, k_slc, v_slc, k_win, v_win):
    B, S, H, D = q.shape
    G, I = N_KV_GROUPS, HEADS_PER_GROUP
    n_blk = S // Q_BLOCK
    n_cmp = k_cmp.shape[1]
    n_sb = S // SLC_BLOCK
    n_sel = min(N_SELECT, n_sb)
    scale = HEAD_DIM ** -0.5
    cmp_end = jnp.arange(n_cmp) * CMP_STRIDE + CMP_BLOCK - 1
    kb_slc = k_slc.reshape(B, n_sb, SLC_BLOCK, G, D).transpose(0, 3, 1, 2, 4)
    vb_slc = v_slc.reshape(B, n_sb, SLC_BLOCK, G, D).transpose(0, 3, 1, 2, 4)
    k_win_p = jnp.pad(k_win, ((0, 0), (WINDOW, 0), (0, 0), (0, 0)))
    v_win_p = jnp.pad(v_win, ((0, 0), (WINDOW, 0), (0, 0), (0, 0)))
    bi = jnp.arange(B)[:, None, None, None]
    gi = jnp.arange(G)[None, None, :, None]
    blk_ids = jnp.arange(n_sb)

    def to_blocks(a):
        return a.reshape(B, n_blk, Q_BLOCK, G, I, a.shape[-1]).transpose(1, 0, 2, 3, 4, 5)

    def block_fn(args):
        blk, qc, qr, g = args
        start = blk * Q_BLOCK
        t = start + jnp.arange(Q_BLOCK)

        s_c = jnp.einsum('bqgid,bngd->bqgin', qc, k_cmp) * scale
        m_c = (cmp_end[None, :] <= t[:, None])[None, :, None, None, :]
        p_c = masked_softmax(s_c, m_c)
        o_c = jnp.einsum('bqgin,bngd->bqgid', p_c.astype(v_cmp.dtype), v_cmp)

        p_grp = jnp.pad(p_c.sum(axis=3), ((0, 0), (0, 0), (0, 0), (CMP_R - 1, CMP_R - 1)))
        imp = 0.0
        for m in range(SLC_RATIO):
            for n in range(CMP_R):
                off = CMP_R - 1 + m - n
                imp = imp + p_grp[..., off:off + SLC_RATIO * (n_sb - 1) + 1:SLC_RATIO]
        cur = t // SLC_BLOCK
        valid = blk_ids[None, :] <= cur[:, None]
        forced = (blk_ids[None, :] == 0) | (valid & (blk_ids[None, :] > cur[:, None] - N_LOCAL))
        score = jnp.where(forced[None, :, None, :], FORCE_SCORE, imp)
        score = jnp.where(valid[None, :, None, :], score, -jnp.inf)
        top_s, idx = lax.top_k(score, n_sel)
        sel_ok = jnp.isfinite(top_s)

        kg = kb_slc[bi, gi, idx]
        vg = vb_slc[bi, gi, idx]
        s_s = jnp.einsum('bqgid,bqgnld->bqginl', qr, kg) * scale
        kpos = idx[..., None] * SLC_BLOCK + jnp.arange(SLC_BLOCK)
        m_s = sel_ok[..., None] & (kpos <= t[None, :, None, None, None])
        p_s = masked_softmax(s_s.reshape(B, Q_BLOCK, G, I, n_sel * SLC_BLOCK),
                             m_s.reshape(B, Q_BLOCK, G, 1, n_sel * SLC_BLOCK))
        p_s = p_s.reshape(B, Q_BLOCK, G, I, n_sel, SLC_BLOCK).astype(vg.dtype)
        o_s = jnp.einsum('bqginl,bqgnld->bqgid', p_s, vg)

        kw = lax.dynamic_slice_in_dim(k_win_p, start, WINDOW + Q_BLOCK, axis=1)
        vw = lax.dynamic_slice_in_dim(v_win_p, start, WINDOW + Q_BLOCK, axis=1)
        kpos_w = start - WINDOW + jnp.arange(WINDOW + Q_BLOCK)
        m_w = ((kpos_w[None, :] <= t[:, None]) & (kpos_w[None, :] > t[:, None] - WINDOW)
               & (kpos_w[None, :] >= 0))
        s_w = jnp.einsum('bqgid,bkgd->bqgik', qr, kw) * scale
        p_w = masked_softmax(s_w, m_w[None, :, None, None, :]).astype(vw.dtype)
        o_w = jnp.einsum('bqgik,bkgd->bqgid', p_w, vw)

        return g[..., 0:1] * o_c + g[..., 1:2] * o_s + g[..., 2:3] * o_w

    xs = (jnp.arange(n_blk), to_blocks(q), to_blocks(q_rot), to_blocks(gates))
    out = lax.map(block_fn, xs)
    return out.transpose(1, 0, 2, 3, 4, 5).reshape(B, S, H, D)


def nsa_layer(x, kvs, norm_g, w_in, w_out):
    B, S, _ = x.shape
    h = rmsnorm(x, norm_g)
    proj = h @ w_in
    q = proj[..., :ATTN_D].reshape(B, S, N_HEADS, HEAD_DIM)
    gates = jax.nn.sigmoid(proj[..., ATTN_D:ATTN_D + N_HEADS * N_BRANCH]
                           .reshape(B, S, N_HEADS, N_BRANCH))
    z = proj[..., ATTN_D + N_HEADS * N_BRANCH:]
    o = nsa_attention(q, partial_rope(q), gates, *kvs)
    y = o.reshape(B, S, ATTN_D) * jax.nn.silu(z)
    return x + y @ w_out


def setup_inputs(seed: int = 0) -> dict:
    key = jax.random.key(seed)
    ks = jax.random.split(key, 20)
    f = jnp.float32
    nrm = lambda k, shape, s: jax.random.normal(k, shape, f) * s
    gain = lambda k, shape: 1.0 + 0.01 * jax.random.normal(k, shape, f)
    b_in_cols = ATTN_D + N_HEADS * N_BRANCH + ATTN_D
    return {
        "x": jax.random.normal(ks[0], (BATCH, SEQ, D_MODEL), f),
        "a_norm": gain(ks[1], (N_A_LAYERS, D_MODEL)),
        "a_w_in": nrm(ks[2], (N_A_LAYERS, D_MODEL, 4 * CONV_D), D_MODEL ** -0.5),
        "a_conv_w": nrm(ks[3], (N_A_LAYERS, CONV_WIDTH, CONV_D), CONV_WIDTH ** -0.5),
        "a_w_out": nrm(ks[4], (N_A_LAYERS, CONV_D, D_MODEL), CONV_D ** -0.5),
        "kv_norm": gain(ks[5], (D_MODEL,)),
        "w_kv": nrm(ks[6], (D_MODEL, N_KV_STREAMS * KV_D), D_MODEL ** -0.5),
        "cmp_pos_k": nrm(ks[7], (CMP_BLOCK, HEAD_DIM), 0.1),
        "cmp_w1_k": nrm(ks[8], (CMP_BLOCK * HEAD_DIM, CMP_HIDDEN), (CMP_BLOCK * HEAD_DIM) ** -0.5),
        "cmp_w2_k": nrm(ks[9], (CMP_HIDDEN, HEAD_DIM), CMP_HIDDEN ** -0.5),
        "cmp_pos_v": nrm(ks[10], (CMP_BLOCK, HEAD_DIM), 0.1),
        "cmp_w1_v": nrm(ks[11], (CMP_BLOCK * HEAD_DIM, CMP_HIDDEN), (CMP_BLOCK * HEAD_DIM) ** -0.5),
        "cmp_w2_v": nrm(ks[12], (CMP_HIDDEN, HEAD_DIM), CMP_HIDDEN ** -0.5),
        "b_norm": gain(ks[13], (N_B_LAYERS, D_MODEL)),
        "b_w_in": nrm(ks[14], (N_B_LAYERS, D_MODEL, b_in_cols), D_MODEL ** -0.5),
        "b_w_out": nrm(ks[15], (N_B_LAYERS, ATTN_D, D_MODEL), ATTN_D ** -0.5),
        "final_norm": gain(ks[16], (D_MODEL,)),
    }


def reference(x, a_norm, a_w_in, a_conv_w, a_w_out, kv_norm, w_kv, cmp_pos_k, cmp_w1_k,
              cmp_w2_k, cmp_pos_v, cmp_w1_v, cmp_w2_v, b_norm, b_w_in, b_w_out, final_norm):
    h = x
    kvs = None
    for layer in range(DEPTH):
        if layer < N_A_LAYERS:
            h = short_conv_layer(h, a_norm[layer], a_w_in[layer], a_conv_w[layer], a_w_out[layer])
        else:
            if layer == N_A_LAYERS:
                kvs = shared_kv(h, kv_norm, w_kv, cmp_pos_k, cmp_w1_k, cmp_w2_k,
                                cmp_pos_v, cmp_w1_v, cmp_w2_v)
            j = layer - N_A_LAYERS
            h = nsa_layer(h, kvs, b_norm[j], b_w_in[j], b_w_out[j])
    return rmsnorm(h, final_norm)
```

```python
import numpy as np
import concourse.bass as bass
import concourse.mybir as mybir
from concourse.bass_utils import run_bass_kernel_spmd
from contextlib import ExitStack

F32 = mybir.dt.float32
BF16 = mybir.dt.bfloat16
AF = mybir.ActivationFunctionType
ALU = mybir.AluOpType

D = 1024
S = 4096
CONV_D = 2048
EPS = 1e-6
NCORES = 8
_DBG = {"pe_sync": 0}


class Sched:
    def __init__(self, nc, n_dma_sems=10, same_engine_sync=False):
        self.nc = nc
        self.eng = {"pe": nc.tensor, "act": nc.scalar, "dve": nc.vector,
                    "pool": nc.gpsimd, "sp": nc.sync}
        self.ops = []
        self.last_w = {}
        self.readers = {}
        self.same_engine_sync = same_engine_sync
        self.n_dma_sems = n_dma_sems
        self.last_on = {}
        self.dma_all = []
        self.n_emitted = 0
        self.esem = None

    def _add(self, kind, eng, fn, reads, writes):
        idx = len(self.ops)
        reads = list(reads)
        writes = list(writes)
        for k in list(reads):
            if isinstance(k, tuple) and k and k[0] == "bank":
                reads.remove(k)
                if k not in writes:
                    writes.append(k)
        deps = set()
        for k in reads + writes:
            if k in self.last_w:
                deps.add(self.last_w[k])
        for k in writes:
            r = self.readers.get(k)
            if r:
                deps.update(r["c"].values())
                deps.update(r["d"])
        for k in writes:
            self.last_w[k] = idx
            self.readers[k] = {"c": {}, "d": []}
        for k in reads:
            if k in writes:
                continue
            r = self.readers.setdefault(k, {"c": {}, "d": []})
            if kind == "dma":
                r["d"].append(idx)
            else:
                r["c"][eng] = idx
        deps.discard(idx)
        self.ops.append({"kind": kind, "eng": eng, "fn": fn, "deps": deps, "need_inc": False})
        if kind == "dma":
            self.dma_all.append(idx)
        else:
            self.last_on[eng] = idx
        return idx

    def op(self, eng, fn, reads=(), writes=()):
        return self._add("c", eng, fn, reads, writes)

    def dma(self, queue, fn, reads=(), writes=()):
        return self._add("dma", queue, fn, reads, writes)

    def barrier(self):
        deps = set(self.last_on.values()) | set(self.dma_all)
        for e in ("pe", "act", "dve", "pool", "sp"):
            idx = len(self.ops)
            self.ops.append({"kind": "c", "eng": e, "fn": None, "deps": set(deps), "need_inc": False, "bar": True})
        self.dma_all = []

    def flush(self):
        self.barrier()
        nc = self.nc
        ops = self.ops
        start = self.n_emitted
        pe_sync = _DBG.get("pe_sync")

        def implicit(od, o):
            return (od["eng"] == o["eng"] and o["kind"] == "c" and od["kind"] == "c"
                    and (not self.same_engine_sync or (o["eng"] == "pe" and not pe_sync)))
        for o in ops[start:]:
            for d in o["deps"]:
                od = ops[d]
                if d < start or od["kind"] == "dma" or implicit(od, o):
                    continue
                od["need_inc"] = True
        if self.esem is None:
            self.esem = {e: nc.alloc_semaphore(name="sem_" + e) for e in ("pe", "act", "dve", "pool")}
            self.ecount = {e: 0 for e in self.esem}
            self.dsem = {}
            self.dcount = {}
            for q in ("sp", "act", "pool"):
                self.dsem[q] = [nc.alloc_semaphore(name="dma_%s_%d" % (q, i)) for i in range(self.n_dma_sems)]
                self.dcount[q] = 0
            self.waited = {e: {} for e in self.eng}
        esem, ecount, dsem, dcount, waited = self.esem, self.ecount, self.dsem, self.dcount, self.waited
        K = self.n_dma_sems
        for o in ops[start:]:
            e = o["eng"]
            engine = self.eng[e]
            waits = {}
            for d in o["deps"]:
                od = ops[d]
                if d < start and not o.get("bar"):
                    continue
                if od["kind"] == "dma":
                    key = ("d", od["eng"], od["slot"])
                    waits[key] = max(waits.get(key, 0), od["val"])
                else:
                    if implicit(od, o) or "val" not in od:
                        continue
                    key = ("c", od["eng"])
                    waits[key] = max(waits.get(key, 0), od["val"])
            if o["kind"] == "dma":
                j = dcount[e]
                slot = j % K
                if j >= K:
                    key = ("d", e, slot)
                    waits[key] = max(waits.get(key, 0), 16 * (j // K))
            for key, val in waits.items():
                if waited[e].get(key, 0) >= val:
                    continue
                waited[e][key] = val
                sem = dsem[key[1]][key[2]] if key[0] == "d" else esem[key[1]]
                engine.wait_ge(sem, val)
            if o["fn"] is None:
                continue
            inst = o["fn"]()
            if o["kind"] == "dma":
                j = dcount[e]
                o["slot"] = j % K
                o["val"] = 16 * (j // K + 1)
                inst.then_inc(dsem[e][o["slot"]], 16)
                dcount[e] = j + 1
            elif o["need_inc"]:
                ecount[e] += 1
                o["val"] = ecount[e]
                inst.then_inc(esem[e], 1)
        self.n_emitted = len(ops)
        self.stats = {"ops": len(ops), "incs": dict(ecount), "dmas": dict(dcount)}

    def finish(self):
        self.flush()


NEG = -30000.0


def build_program(stages=("A", "B", "C"), dbg=False, same_engine_sync=True):
    nc = bass.Bass("TRN2", target_bir_lowering=False)
    sc = Sched(nc, same_engine_sync=same_engine_sync)
    scratch_kind = "ExternalOutput" if dbg else "Internal"

    def dram(name, shape, dtype, kind):
        return nc.dram_tensor(name, list(shape), dtype, kind=kind).ap()

    def din(name, shape, dtype=F32):
        return dram(name, shape, dtype, "ExternalInput")

    xb = din("xb", [S, D])
    xh = din("xh", [4, D])
    w_in_r = din("w_in_r", [16, 128, 4096])
    a_gT = din("a_gT", [128, 8])
    a_cw = din("a_cw", [128, 48])
    w_out_r = din("w_out_r", [128, 16 * 1024])
    ident_d = din("ident", [128, 128])
    x1d = dram("x1", [S, D], F32, scratch_kind)
    wkv_r = din("wkv_r", [128, 8 * 1536])
    wbin_r = din("wbin_r", [128, 8 * 2096])
    gkvT_d = din("gkvT", [128, 8])
    gbT_d = din("gbT", [128, 8])
    cos_d = din("cosT", [128, 32 * 8])
    sin_d = din("sinT", [128, 32 * 8])
    kT4 = dram("kT4", [16, 64, S], BF16, scratch_kind)
    vtok = dram("vtok", [S, 520], BF16, scratch_kind)
    qT = dram("qT", [32, 64, 2048], BF16, scratch_kind)
    szd = dram("szd", [2048, 1024], F32, scratch_kind)
    gated = dram("gated", [2048, 48], F32, scratch_kind)
    w1k_d = din("w1k", [128, 16 * 256])
    w1v_d = din("w1v", [128, 16 * 256])
    w2k_d = din("w2k", [128, 2 * 128])
    w2v_d = din("w2v", [128, 2 * 128])
    posk_d = din("poskT", [128, 16])
    posv_d = din("posvT", [128, 16])
    E_d = din("Emat", [64, S])
    kbias_d = din("kbias", [1, S])
    cmask_d = din("cmask", [128, 2 * 2048])
    Aaug_d = din("Aaug", [128, 2 * 65])
    fmul_d = din("fmul", [128, 16 * 64])
    fadd_d = din("fadd", [128, 16 * 64])
    tri_d = din("tri", [128, 128])
    upp_d = din("upp", [128, 128])
    woutb_d = din("woutb", [128, 8 * 1024])
    gfin_d = din("gfin", [128, 1024])
    outd = dram("out", [2048, D], F32, "ExternalOutput")

    banks = [nc.alloc_psum_tensor("bank%d" % i, [128, 512], F32) for i in range(8)]
    bankctr = [0]

    def nextbank():
        b = bankctr[0] % 8
        bankctr[0] += 1
        return b

    def bk(i):
        return ("bank", i)

    def bfview(i):
        return banks[i].bitcast(BF16)

    ident = nc.alloc_sbuf_tensor("ident_sb", [128, 128], BF16)
    sc.dma("pool", lambda: nc.gpsimd.dma_start(out=ident[:], in_=ident_d[:, :]), writes=["ident"])

    def rms_front(stk, tag, xin, xhat, ss, sq, rstd, col, rows=128):
        sc.op("act", lambda: nc.scalar.activation(
            out=xhat[0:rows, :], in_=xin[0:rows, :], func=AF.Square, accum_out=ss[0:rows, col:col + 1]),
            reads=[(tag, "xin")], writes=[(tag, "xhat"), (tag, "ss", col)])
        sc.op("act", lambda: nc.scalar.activation(
            out=sq[0:rows, col:col + 1], in_=ss[0:rows, col:col + 1], func=AF.Sqrt, scale=1.0 / D, bias=EPS),
            reads=[(tag, "ss", col)], writes=[(tag, "sq", col)])
        sc.op("dve", lambda: nc.vector.reciprocal(out=rstd[0:rows, col:col + 1], in_=sq[0:rows, col:col + 1]),
              reads=[(tag, "sq", col)], writes=[(tag, "rstd", col)])
        sc.op("dve", lambda: nc.vector.tensor_scalar(
            out=xhat[0:rows, :], in0=xin[0:rows, :], scalar1=rstd[0:rows, col:col + 1], scalar2=None, op0=ALU.mult),
            reads=[(tag, "xin"), (tag, "rstd", col)], writes=[(tag, "xhat")])

    if "A" in stages:
        with ExitStack() as stk:
            def sb(name, shape, dtype):
                return stk.enter_context(nc.sbuf_tensor(name, list(shape), dtype))
            gT = sb("gT", [128, 8], F32)
            cw = sb("cw", [128, 48], F32)
            woutT = sb("woutT", [128, 16, 1024], BF16)
            hT = sb("hT", [128, 8, 2048], BF16)
            hTh = sb("hTh", [128, 8, 4], BF16)
            yT = sb("yT", [128, 16, 2048], BF16)
            wblk = [sb("wblk%d" % i, [128, 4096], BF16) for i in range(2)]
            xin = [sb("xin%d" % i, [128, 1024], F32) for i in range(2)]
            xhat = [sb("xhat%d" % i, [128, 1024], BF16) for i in range(2)]
            ss = sb("ss", [128, 17], F32)
            sq = sb("sq", [128, 17], F32)
            rstd = sb("rstd", [128, 17], F32)
            vhalo = sb("vhalo", [128, 16, 2], F32)
            hcs = sb("hcs", [128, 4], F32)
            csb = [sb("csb%d" % i, [128, 512], F32) for i in range(2)]
            vbuf = [sb("vbuf%d" % i, [128, 514], F32) for i in range(2)]
            tbuf = [sb("tbuf%d" % i, [128, 512], F32) for i in range(2)]
            szb = [sb("szb%d" % i, [128, 512], F32) for i in range(2)]
            x1t = [sb("x1t%d" % i, [128, 1024], F32) for i in range(2)]

            sc.dma("sp", lambda: nc.sync.dma_start(out=gT[:], in_=a_gT[:, :]), writes=["gT"])
            sc.dma("sp", lambda: nc.sync.dma_start(out=cw[:], in_=a_cw[:, :]), writes=["cw"])
            for q in range(8):
                sc.dma("pool", lambda q=q: nc.gpsimd.dma_start(
                    out=woutT[:, 2 * q:2 * q + 2, :],
                    in_=w_out_r[:, 2048 * q:2048 * (q + 1)].rearrange("p (a n) -> p a n", a=2)),
                    writes=[("woutT", q)])

            sc.op("pool", lambda: nc.gpsimd.memset(xin[1][:], 0.0), writes=[("A1", "xin")])
            sc.dma("sp", lambda: nc.sync.dma_start(out=xin[1][0:4, :], in_=xh[:, :]), reads=[("A1", "xin")], writes=[("A1", "xin")])
            rms_front(stk, "A1", xin[1], xhat[1], ss, sq, rstd, 16)
            bi = nextbank()

            def trh(bi=bi):
                pst = bfview(bi)
                inst = None
                for kc in range(8):
                    inst = nc.tensor.transpose(out=pst[:, kc * 128:(kc + 1) * 128],
                                               in_=xhat[1][:, kc * 128:(kc + 1) * 128], identity=ident[:])
                return inst
            sc.op("pe", trh, reads=[("A1", "xhat"), "ident"], writes=[bk(bi)])
            sc.op("dve", lambda bi=bi: nc.vector.tensor_tensor(
                out=hTh[:, :, :], in0=bfview(bi)[:, :].rearrange("p (k t) -> p k t", k=8)[:, :, 0:4],
                in1=gT[:, :, None].broadcast_to([128, 8, 4]), op=ALU.mult),
                reads=[bk(bi), "gT"], writes=["hTh"])

            nblk = 0
            nchunk = 0
            for hf in range(2):
                tok0 = hf * 2048
                for tt in range(16):
                    s = tt % 2
                    tg = "A%d" % s
                    r0 = tok0 + tt * 128
                    sc.dma("sp", lambda s=s, r0=r0: nc.sync.dma_start(out=xin[s][:], in_=xb[r0:r0 + 128, :]),
                           writes=[(tg, "xin")])
                    rms_front(stk, tg, xin[s], xhat[s], ss, sq, rstd, tt)
                    bi = nextbank()

                    def tr(s=s, bi=bi):
                        pst = bfview(bi)
                        inst = None
                        for kc in range(8):
                            inst = nc.tensor.transpose(out=pst[:, kc * 128:(kc + 1) * 128],
                                                       in_=xhat[s][:, kc * 128:(kc + 1) * 128], identity=ident[:])
                        return inst
                    sc.op("pe", tr, reads=[(tg, "xhat"), "ident"], writes=[bk(bi)])
                    sc.op("dve", lambda tt=tt, bi=bi: nc.vector.tensor_tensor(
                        out=hT[:, :, tt * 128:(tt + 1) * 128],
                        in0=bfview(bi)[:, :].rearrange("p (k t) -> p k t", k=8),
                        in1=gT[:, :, None].broadcast_to([128, 8, 128]), op=ALU.mult),
                        reads=[bk(bi), "gT"], writes=[("hT", tt)])

                for cb in range(16):
                    wb = nblk % 2
                    nblk += 1
                    for hh in range(2):
                        sc.dma("pool", lambda cb=cb, wb=wb, hh=hh: nc.gpsimd.dma_start(
                            out=wblk[wb][:, 2048 * hh:2048 * (hh + 1)], in_=w_in_r[cb, :, 2048 * hh:2048 * (hh + 1)]),
                            writes=[("wblk", wb, hh)])
                    wkeys = [("wblk", wb, 0), ("wblk", wb, 1)]
                    for tc in range(4):
                        sl = nchunk % 2
                        nchunk += 1
                        c0 = tc * 512
                        bb, bc, bu, bz = 4 * sl, 4 * sl + 1, 4 * sl + 2, 4 * sl + 3
                        if tc == 0:
                            for j, bi in ((1, bb), (2, bz)):
                                def mmh(j=j, bi=bi, wb=wb):
                                    inst = None
                                    for kc in range(8):
                                        off = kc * 512 + j * 128
                                        inst = nc.tensor.matmul(banks[bi][:, 0:4], lhsT=wblk[wb][:, off:off + 128],
                                                                rhs=hTh[:, kc, :], start=(kc == 0), stop=(kc == 7))
                                    return inst
                                sc.op("pe", mmh, reads=wkeys + ["hTh"], writes=[bk(bi)])
                            sc.op("act", lambda bb=bb: nc.scalar.copy(out=hcs[:], in_=banks[bb][:, 0:4]),
                                  reads=[bk(bb)], writes=["hcs"])
                            sc.op("dve", lambda bz=bz, cb=cb, hf=hf: nc.vector.tensor_tensor(
                                out=vhalo[:, cb, :], in0=hcs[:, 2 * hf:2 * hf + 2], in1=banks[bz][:, 2 * hf:2 * hf + 2],
                                op=ALU.mult), reads=["hcs", bk(bz)], writes=[("vhalo", cb)])
                        for j in (1, 2, 0, 3):
                            bi = 4 * sl + j

                            def mm(j=j, bi=bi, wb=wb, c0=c0):
                                inst = None
                                for kc in range(8):
                                    off = kc * 512 + j * 128
                                    inst = nc.tensor.matmul(banks[bi][:, :], lhsT=wblk[wb][:, off:off + 128],
                                                            rhs=hT[:, kc, c0:c0 + 512],
                                                            start=(kc == 0), stop=(kc == 7))
                                return inst
                            sc.op("pe", mm, reads=wkeys + [("hT", 4 * tc + i) for i in range(4)], writes=[bk(bi)])
                        sc.op("act", lambda sl=sl, bc=bc: nc.scalar.copy(out=csb[sl][:], in_=banks[bc][:, :]),
                              reads=[bk(bc)], writes=[("csb", sl)])
                        sc.op("pool", lambda sl=sl, cb=cb: nc.gpsimd.tensor_copy(out=vbuf[sl][:, 0:2], in_=vhalo[:, cb, :]),
                              reads=[("vhalo", cb)], writes=[("vbufh", sl)])
                        sc.op("dve", lambda sl=sl, bu=bu: nc.vector.tensor_tensor(
                            out=vbuf[sl][:, 2:514], in0=csb[sl][:], in1=banks[bu][:, :], op=ALU.mult),
                            reads=[("csb", sl), bk(bu)], writes=[("vbuf", sl)])
                        sc.op("pool", lambda sl=sl, cb=cb: nc.gpsimd.tensor_copy(out=vhalo[:, cb, :], in_=vbuf[sl][:, 512:514]),
                              reads=[("vbuf", sl)], writes=[("vhalo", cb)])
                        sc.op("dve", lambda sl=sl, cb=cb: nc.vector.tensor_scalar(
                            out=tbuf[sl][:], in0=vbuf[sl][:, 0:512], scalar1=cw[:, 3 * cb:3 * cb + 1], scalar2=None,
                            op0=ALU.mult),
                            reads=[("vbuf", sl), ("vbufh", sl), "cw"], writes=[("tbuf", sl)])
                        sc.op("dve", lambda sl=sl, cb=cb: nc.vector.scalar_tensor_tensor(
                            out=tbuf[sl][:], in0=vbuf[sl][:, 1:513], scalar=cw[:, 3 * cb + 1:3 * cb + 2], in1=tbuf[sl][:],
                            op0=ALU.mult, op1=ALU.add),
                            reads=[("vbuf", sl), ("vbufh", sl), ("tbuf", sl)], writes=[("tbuf", sl)])
                        sc.op("dve", lambda sl=sl, cb=cb: nc.vector.scalar_tensor_tensor(
                            out=tbuf[sl][:], in0=vbuf[sl][:, 2:514], scalar=cw[:, 3 * cb + 2:3 * cb + 3], in1=tbuf[sl][:],
                            op0=ALU.mult, op1=ALU.add),
                            reads=[("vbuf", sl), ("tbuf", sl)], writes=[("tbuf", sl)])
                        sc.op("act", lambda sl=sl, bz=bz: nc.scalar.activation(out=szb[sl][:], in_=banks[bz][:, :], func=AF.Silu),
                              reads=[bk(bz)], writes=[("szb", sl)])
                        sc.op("dve", lambda sl=sl, bb=bb: nc.vector.tensor_tensor(
                            out=tbuf[sl][:], in0=tbuf[sl][:], in1=banks[bb][:, :], op=ALU.mult),
                            reads=[("tbuf", sl), bk(bb)], writes=[("tbuf", sl)])
                        sc.op("dve", lambda sl=sl, cb=cb, c0=c0: nc.vector.tensor_tensor(
                            out=yT[:, cb, c0:c0 + 512], in0=tbuf[sl][:], in1=szb[sl][:], op=ALU.mult),
                            reads=[("tbuf", sl), ("szb", sl)], writes=[("yT", cb, tc)])

                for tt in range(16):
                    s = tt % 2
                    tg = "A%d" % s
                    r0 = tok0 + tt * 128
                    sc.dma("sp", lambda s=s, r0=r0: nc.sync.dma_start(out=xin[s][:], in_=xb[r0:r0 + 128, :]),
                           writes=[(tg, "xin")])
                    for nh in range(2):
                        bi = nextbank()

                        def mm2(bi=bi, tt=tt, nh=nh):
                            inst = None
                            for cb in range(16):
                                inst = nc.tensor.matmul(banks[bi][:, :], lhsT=yT[:, cb, tt * 128:(tt + 1) * 128],
                                                        rhs=woutT[:, cb, nh * 512:(nh + 1) * 512],
                                                        start=(cb == 0), stop=(cb == 15))
                            return inst
                        sc.op("pe", mm2, reads=[("yT", cb, tt // 4) for cb in range(16)] + [("woutT", q) for q in range(8)],
                              writes=[bk(bi)])
                        sc.op("dve", lambda s=s, bi=bi, nh=nh: nc.vector.tensor_tensor(
                            out=x1t[s][:, nh * 512:(nh + 1) * 512], in0=banks[bi][:, :],
                            in1=xin[s][:, nh * 512:(nh + 1) * 512], op=ALU.add),
                            reads=[bk(bi), (tg, "xin")], writes=[("x1t", s, nh)])
                    sc.dma("sp", lambda s=s, r0=r0: nc.sync.dma_start(out=x1d[r0:r0 + 128, :], in_=x1t[s][:]),
                           reads=[("x1t", s, 0), ("x1t", s, 1)], writes=[("x1d", r0 // 128)])
            sc.flush()

    if "B" in stages:
        with ExitStack() as stk:
            def sb(name, shape, dtype):
                return stk.enter_context(nc.sbuf_tensor(name, list(shape), dtype))
            wkv = sb("wkv", [128, 8, 1536], BF16)
            wbin = sb("wbin", [128, 8, 2096], BF16)
            gkvT = sb("gkvT_sb", [128, 8], F32)
            gbT = sb("gbT_sb", [128, 8], F32)
            cosT = sb("cosT_sb", [128, 32, 8], F32)
            sinT = sb("sinT_sb", [128, 32, 8], F32)
            xin = [sb("bxin%d" % i, [128, 1024], F32) for i in range(2)]
            xhat = [sb("bxhat%d" % i, [128, 1024], BF16) for i in range(2)]
            ss = sb("bss", [128, 32], F32)
            sq = sb("bsq", [128, 32], F32)
            rstd = sb("brstd", [128, 32], F32)
            hkT = [sb("hkT%d" % i, [128, 8, 128], BF16) for i in range(2)]
            hqT = [sb("hqT%d" % i, [128, 8, 128], BF16) for i in range(2)]
            kcv = [sb("kcv%d" % i, [128, 512], BF16) for i in range(2)]
            kk = [sb("kk%d" % i, [128, 8, 64], BF16) for i in range(2)]
            rt = [sb("rt%d" % i, [128, 8, 8], F32) for i in range(4)]
            kTst = [sb("kTst%d" % i, [128, 8, 512], BF16) for i in range(2)]
            vst = [sb("vst%d" % i, [128, 4, 8, 65], BF16) for i in range(2)]
            qr = [sb("qr%d" % i, [128, 16, 64], BF16) for i in range(2)]
            qc = [sb("qc%d" % i, [128, 16, 64], BF16) for i in range(2)]
            qTst = [sb("qTst%d" % i, [128, 16, 512], BF16) for i in range(2)]
            szst = [sb("szst%d" % i, [128, 1024], F32) for i in range(2)]
            gst = [sb("gst%d" % i, [128, 48], F32) for i in range(2)]

            for q in range(4):
                sc.dma("pool", lambda q=q: nc.gpsimd.dma_start(
                    out=wkv[:, 2 * q:2 * q + 2, :],
                    in_=wkv_r[:, 3072 * q:3072 * (q + 1)].rearrange("p (a n) -> p a n", a=2)),
                    writes=[("wkv", q)])
            for q in range(8):
                sc.dma("pool", lambda q=q: nc.gpsimd.dma_start(
                    out=wbin[:, q, :].rearrange("p (a n) -> p a n", a=2),
                    in_=wbin_r[:, 2096 * q:2096 * (q + 1)].rearrange("p (a n) -> p a n", a=2)), writes=[("wbin", q)])
            wkv_keys = [("wkv", q) for q in range(4)]
            wbin_keys = [("wbin", q) for q in range(8)]
            sc.dma("sp", lambda: nc.sync.dma_start(out=gkvT[:], in_=gkvT_d[:, :]), writes=["gkvT"])
            sc.dma("sp", lambda: nc.sync.dma_start(out=gbT[:], in_=gbT_d[:, :]), writes=["gbT"])
            sc.dma("sp", lambda: nc.sync.dma_start(out=cosT[:], in_=cos_d[:, :].rearrange("p (t f) -> p t f", f=8)),
                   writes=["cosT"])
            sc.dma("sp", lambda: nc.sync.dma_start(out=sinT[:], in_=sin_d[:, :].rearrange("p (t f) -> p t f", f=8)),
                   writes=["sinT"])
            for i in range(2):
                sc.op("pool", lambda i=i: nc.gpsimd.memset(vst[i][:], 1.0), writes=[("vst", i)])

            def rope(src_bank, col0, nh, dst, tt, tagk, wkey):
                psv = banks[src_bank][:, col0:col0 + nh * 64].rearrange("p (h d) -> p h d", d=64)
                cs = cosT[:, tt:tt + 1, :].broadcast_to([128, nh, 8])
                sn = sinT[:, tt:tt + 1, :].broadcast_to([128, nh, 8])
                rd = [bk(src_bank), "cosT", "sinT"]
                for t_i, (lo, tab) in enumerate(((0, cs), (8, sn), (8, cs), (0, sn))):
                    sc.op("dve", lambda t_i=t_i, lo=lo, tab=tab: nc.vector.tensor_tensor(
                        out=rt[t_i][:, 0:nh, :], in0=psv[:, :, lo:lo + 8], in1=tab, op=ALU.mult),
                        reads=rd, writes=[("rt", t_i)])
                if _DBG.get("r1"):
                    return []
                sc.op("dve", lambda: nc.vector.tensor_tensor(
                    out=dst[:, 0:nh, 0:8], in0=rt[0][:, 0:nh, :], in1=rt[1][:, 0:nh, :], op=ALU.subtract),
                    reads=[("rt", 0), ("rt", 1)], writes=[wkey + ("a",)])
                sc.op("dve", lambda: nc.vector.tensor_tensor(
                    out=dst[:, 0:nh, 8:16], in0=rt[2][:, 0:nh, :], in1=rt[3][:, 0:nh, :], op=ALU.add),
                    reads=[("rt", 2), ("rt", 3)], writes=[wkey + ("b",)])
                if _DBG.get("r2"):
                    return []
                sc.op("dve", lambda: nc.vector.tensor_copy(out=dst[:, 0:nh, 16:64], in_=psv[:, :, 16:64]),
                      reads=[bk(src_bank)], writes=[wkey + ("c",)])
                return [wkey + ("a",), wkey + ("b",), wkey + ("c",)]

            for ch in range(_DBG.get('b_nch', 8)):
                own = ch < 4 and not _DBG.get('b_noq')
                cs_ = ch % 2
                for i in range(4):
                    tt = 4 * ch + i
                    s = tt % 2
                    tg = "B%d" % s
                    r0 = tt * 128
                    sc.dma("sp", lambda s=s, r0=r0: nc.sync.dma_start(out=xin[s][:], in_=x1d[r0:r0 + 128, :]),
                           reads=[("x1d", tt)], writes=[(tg, "xin")])
                    rms_front(stk, tg, xin[s], xhat[s], ss, sq, rstd, tt)
                    bi = nextbank()

                    def tr(s=s, bi=bi):
                        pst = bfview(bi)
                        inst = None
                        for kc in range(8):
                            inst = nc.tensor.transpose(out=pst[:, kc * 128:(kc + 1) * 128],
                                                       in_=xhat[s][:, kc * 128:(kc + 1) * 128], identity=ident[:])
                        return inst
                    sc.op("pe", tr, reads=[(tg, "xhat"), "ident"], writes=[bk(bi)])
                    sc.op("dve", lambda s=s, bi=bi: nc.vector.tensor_tensor(
                        out=hkT[s][:, :, :], in0=bfview(bi)[:, :].rearrange("p (k t) -> p k t", k=8),
                        in1=gkvT[:, :, None].broadcast_to([128, 8, 128]), op=ALU.mult),
                        reads=[bk(bi), "gkvT"], writes=[("hkT", s)])
                    if own:
                        sc.op("dve", lambda s=s, bi=bi: nc.vector.tensor_tensor(
                            out=hqT[s][:, :, :], in0=bfview(bi)[:, :].rearrange("p (k t) -> p k t", k=8),
                            in1=gbT[:, :, None].broadcast_to([128, 8, 128]), op=ALU.mult),
                            reads=[bk(bi), "gbT"], writes=[("hqT", s)])

                    def proj(bi, act, wt, c0, ncols):
                        def f():
                            inst = None
                            for kc in range(8):
                                inst = nc.tensor.matmul(banks[bi][:, 0:ncols], lhsT=act[:, kc, :],
                                                        rhs=wt[:, kc, c0:c0 + ncols], start=(kc == 0), stop=(kc == 7))
                            return inst
                        return f
                    b0 = nextbank()
                    sc.op("pe", proj(b0, hkT[s], wkv, 0, 512), reads=[("hkT", s)] + wkv_keys, writes=[bk(b0)])
                    b1 = nextbank()
                    sc.op("pe", proj(b1, hkT[s], wkv, 512, 512), reads=[("hkT", s)] + wkv_keys, writes=[bk(b1)])
                    b2 = nextbank()
                    sc.op("pe", proj(b2, hkT[s], wkv, 1024, 512), reads=[("hkT", s)] + wkv_keys, writes=[bk(b2)])
                    if _DBG.get("b_stop1"):
                        continue
                    sc.op("act", lambda s=s, b0=b0: nc.scalar.copy(out=kcv[s][:], in_=banks[b0][:, :]),
                          reads=[bk(b0)], writes=[("kcv", s)])
                    if _DBG.get("b_stop2"):
                        continue
                    kkeys = rope(b1, 0, 8, kk[s], tt, "kk", ("kk", s))
                    if _DBG.get("b_stop3"):
                        continue
                    sc.op("act", lambda s=s, b2=b2, i=i, cs_=cs_: nc.scalar.copy(
                        out=vst[cs_][:, i, :, 0:64], in_=banks[b2][:, :].rearrange("p (a d) -> p a d", d=64)),
                        reads=[bk(b2)], writes=[("vst", cs_)])
                    if _DBG.get("b_stop4"):
                        continue
                    for half, src, skeys in ((0, kcv[s], [("kcv", s)]), (1, kk[s], kkeys)):
                        bt = nextbank()

                        def trk(bt=bt, src=src, half=half):
                            pst = bfview(bt)
                            sv = src[:, :] if half == 0 else src[:, :, :].rearrange("p a d -> p (a d)")
                            inst = None
                            for a in range(4):
                                inst = nc.tensor.transpose(out=pst[:, a * 128:(a + 1) * 128],
                                                           in_=sv[:, a * 128:(a + 1) * 128], identity=ident[:])
                            return inst
                        sc.op("pe", trk, reads=skeys + ["ident"], writes=[bk(bt)])
                        sc.op("act" if half == 0 else "dve", (lambda bt=bt, half=half, i=i, cs_=cs_: (
                            nc.scalar.copy if half == 0 else nc.vector.tensor_copy)(
                            out=kTst[cs_][:, 4 * half:4 * half + 4, i * 128:(i + 1) * 128],
                            in_=bfview(bt)[:, 0:512].rearrange("p (a t) -> p a t", a=4))),
                            reads=[bk(bt)], writes=[("kTst", cs_, half, i)])
                    if own:
                        for qh in range(2):
                            bq = nextbank()
                            sc.op("pe", proj(bq, hqT[s], wbin, 512 * qh, 512), reads=[("hqT", s)] + wbin_keys, writes=[bk(bq)])
                            if not _DBG.get("q_noqc"):
                                sc.op("act", lambda s=s, bq=bq, qh=qh: nc.scalar.copy(
                                    out=qc[s][:, 8 * qh:8 * qh + 8, :], in_=banks[bq][:, :].rearrange("p (h d) -> p h d", d=64)),
                                    reads=[bk(bq)], writes=[("qc", s, qh)])
                            if not _DBG.get("q_norope"):
                                rope(bq, 0, 8, qr[s][:, 8 * qh:8 * qh + 8, :], tt, "q", ("qr", s, qh))
                        for zh in range(0 if not _DBG.get("q_noz") else 2, 2):
                            bz = nextbank()
                            sc.op("pe", proj(bz, hqT[s], wbin, 1024 + 512 * zh, 512), reads=[("hqT", s)] + wbin_keys, writes=[bk(bz)])
                            sc.op("act", lambda s=s, bz=bz, zh=zh: nc.scalar.activation(
                                out=szst[s][:, 512 * zh:512 * (zh + 1)], in_=banks[bz][:, :], func=AF.Silu),
                                reads=[bk(bz)], writes=[("szst", s, zh)])
                        if _DBG.get("q_stopg"):
                            continue
                        bg = nextbank()
                        sc.op("pe", proj(bg, hqT[s], wbin, 2048, 48), reads=[("hqT", s)] + wbin_keys, writes=[bk(bg)])
                        sc.op("act", lambda s=s, bg=bg: nc.scalar.activation(out=gst[s][:], in_=banks[bg][:, 0:48], func=AF.Sigmoid),
                              reads=[bk(bg)], writes=[("gst", s)])
                        sc.dma("sp", lambda s=s, r0=r0: nc.sync.dma_start(out=szd[r0:r0 + 128, :], in_=szst[s][:]),
                               reads=[("szst", s, 0), ("szst", s, 1)], writes=[("szd", tt)])
                        sc.dma("sp", lambda s=s, r0=r0: nc.sync.dma_start(out=gated[r0:r0 + 128, :], in_=gst[s][:]),
                               reads=[("gst", s)], writes=[("gated", tt)])
                        if _DBG.get("q_stoptr"):
                            continue
                        for which, src, skeys in ((0, qc[s], [("qc", s, 0), ("qc", s, 1)]),
                                                  (1, qr[s], [("qr", s, qh, x) for qh in range(2) for x in "abc"])):
                            bt = nextbank()

                            def trq(bt=bt, src=src):
                                pst = bfview(bt)
                                sv = src[:, :, :].rearrange("p h d -> p (h d)")
                                inst = None
                                for a in range(8):
                                    inst = nc.tensor.transpose(out=pst[:, a * 128:(a + 1) * 128],
                                                               in_=sv[:, a * 128:(a + 1) * 128], identity=ident[:])
                                return inst
                            sc.op("pe", trq, reads=skeys + ["ident"], writes=[bk(bt)])
                            a0 = 8 * which
                            sc.op("dve" if which == 0 else "act", (lambda bt=bt, which=which, a0=a0, i=i, cs_=cs_: (
                                nc.vector.tensor_copy if which == 0 else nc.scalar.copy)(
                                out=qTst[cs_][:, a0:a0 + 8, i * 128:(i + 1) * 128],
                                in_=bfview(bt)[:, :].rearrange("p (a t) -> p a t", a=8))),
                                reads=[bk(bt)], writes=[("qTst", cs_, which, i)])
                c0 = ch * 512
                if _DBG.get('b_nostore'):
                    continue
                sc.dma("sp", lambda cs_=cs_, c0=c0: nc.sync.dma_start(
                    out=kT4[:, :, c0:c0 + 512].rearrange("(a gl) d t -> (gl d) a t", gl=2), in_=kTst[cs_][:, :, :]),
                    reads=[("kTst", cs_, h_, i_) for h_ in range(2) for i_ in range(4)], writes=[("kT4", ch)])
                sc.dma("sp", lambda cs_=cs_, c0=c0: nc.sync.dma_start(
                    out=vtok[c0:c0 + 512, :].rearrange("(i p) f -> p i f", p=128),
                    in_=vst[cs_][:, :, :, :].rearrange("p i a d -> p i (a d)")),
                    reads=[("vst", cs_)], writes=[("vtok", ch)])
                if own:
                    sc.dma("sp", lambda cs_=cs_, c0=c0: nc.sync.dma_start(
                        out=qT[:, :, c0:c0 + 512].rearrange("(a hl) d t -> (hl d) a t", hl=2), in_=qTst[cs_][:, :, :]),
                        reads=[("qTst", cs_, w_, i_) for w_ in range(2) for i_ in range(4)],
                        writes=[("qT", ch)])
            sc.flush()

    if "C" in stages:
        with ExitStack() as stk:
            def sb(name, shape, dtype):
                return stk.enter_context(nc.sbuf_tensor(name, list(shape), dtype))
            kcmpT = sb("kcmpT", [128, 4, 256], BF16)
            vcmp = sb("vcmp", [128, 2, 4, 64], BF16)
            AI = sb("AI", [128, 2, 64], BF16)
            sc.dma("pool", lambda: nc.gpsimd.dma_start(out=AI[:], in_=Aaug_d[:, :].rearrange("p (a c) -> p a c", a=2)[:, :, 0:64]),
                   writes=["AI"])
            all_kT4 = [("kT4", ch) for ch in range(8)]

            with ExitStack() as stk2:
                def sb2(name, shape, dtype):
                    return stk2.enter_context(nc.sbuf_tensor(name, list(shape), dtype))
                kcT = sb2("kcT", [128, 4, 4128], BF16)
                w1 = sb2("w1", [128, 16, 256], BF16)
                w2 = sb2("w2", [128, 2, 128], BF16)
                posT = sb2("posT", [128, 16], BF16)
                hidT = sb2("hidT", [128, 2, 4, 256], BF16)
                pb = sb2("pb", [128, 2], F32)
                ub = sb2("ub", [128, 512], F32)
                tb = sb2("tb", [128, 512], F32)
                sg = sb2("sg", [128, 512], F32)
                for st in range(2):
                    for g in range(4):
                        sc.dma("sp", lambda st=st, g=g: nc.sync.dma_start(out=kcT[0:64, g, 0:4096], in_=kT4[4 * st + g, :, :]),
                               reads=all_kT4, writes=[("kcT", g)])
                        sc.dma("sp", lambda st=st, g=g: nc.sync.dma_start(out=kcT[0:64, g, 4096:4128], in_=kT4[4 * st + g, :, 0:32]),
                               reads=all_kT4, writes=[("kcTw", g)])
                        sc.dma("sp", lambda st=st, g=g: nc.sync.dma_start(out=kcT[64:128, g, 0:4095], in_=kT4[4 * st + g, :, 1:4096]),
                               reads=all_kT4, writes=[("kcT2", g)])
                        sc.dma("sp", lambda st=st, g=g: nc.sync.dma_start(out=kcT[64:128, g, 4095:4127], in_=kT4[4 * st + g, :, 0:32]),
                               reads=all_kT4, writes=[("kcT2w", g)])
                    w1d = w1k_d if st == 0 else w1v_d
                    w2d = w2k_d if st == 0 else w2v_d
                    posd = posk_d if st == 0 else posv_d
                    for q in range(4):
                        sc.dma("pool", lambda q=q, w1d=w1d: nc.gpsimd.dma_start(
                            out=w1[:, 4 * q:4 * q + 4, :], in_=w1d[:, 1024 * q:1024 * (q + 1)].rearrange("p (l n) -> p l n", n=256)),
                            writes=[("w1", q)])
                    sc.dma("pool", lambda w2d=w2d: nc.gpsimd.dma_start(out=w2[:], in_=w2d[:, :].rearrange("p (a n) -> p a n", a=2)),
                           writes=["w2"])
                    sc.dma("pool", lambda posd=posd: nc.gpsimd.dma_start(out=posT[:], in_=posd[:, :]), writes=["posT"])
                    w1keys = [("w1", q) for q in range(4)]
                    kckeys = [(nm, g) for g in range(4) for nm in ("kcT", "kcTw", "kcT2", "kcT2w")]
                    for hc in range(2):
                        bi = nextbank()

                        def mpb(bi=bi, hc=hc):
                            inst = None
                            for l in range(16):
                                inst = nc.tensor.matmul(banks[bi][:, 0:1], lhsT=w1[:, l, hc * 128:(hc + 1) * 128],
                                                        rhs=posT[:, l:l + 1], start=(l == 0), stop=(l == 15))
                            return inst
                        sc.op("pe", mpb, reads=w1keys + ["posT"], writes=[bk(bi)])
                        sc.op("act", lambda bi=bi, hc=hc: nc.scalar.copy(out=pb[:, hc:hc + 1], in_=banks[bi][:, 0:1]),
                              reads=[bk(bi)], writes=[("pb", hc)])
                        for gp in range(2):
                            bh = nextbank()

                            def mh(bh=bh, hc=hc, gp=gp):
                                inst = None
                                for l in range(16):
                                    inst = nc.tensor.matmul(banks[bh][:, :], lhsT=w1[:, l, hc * 128:(hc + 1) * 128],
                                                            rhs=kcT[:, 2 * gp:2 * gp + 2, 2 * l:2 * l + 4096:16],
                                                            start=(l == 0), stop=(l == 15))
                                return inst
                            sc.op("pe", mh, reads=w1keys + kckeys, writes=[bk(bh)])
                            sc.op("act", lambda bh=bh, hc=hc: nc.scalar.activation(
                                out=ub[:], in_=banks[bh][:, :], func=AF.Identity, bias=pb[:, hc:hc + 1]),
                                reads=[bk(bh), ("pb", hc)], writes=["ub"])
                            sc.op("dve", lambda: nc.vector.tensor_tensor(out=tb[:], in0=ub[:], in1=ub[:], op=ALU.mult),
                                  reads=["ub"], writes=["tb"])
                            sc.op("dve", lambda: nc.vector.tensor_scalar(out=tb[:], in0=tb[:], scalar1=0.044715, scalar2=1.0,
                                                                         op0=ALU.mult, op1=ALU.add),
                                  reads=["tb"], writes=["tb"])
                            sc.op("dve", lambda: nc.vector.tensor_tensor(out=tb[:], in0=tb[:], in1=ub[:], op=ALU.mult),
                                  reads=["tb", "ub"], writes=["tb"])
                            sc.op("act", lambda: nc.scalar.activation(out=sg[:], in_=tb[:], func=AF.Sigmoid, scale=1.5957691216057308),
                                  reads=["tb"], writes=["sg"])
                            sc.op("dve", lambda hc=hc, gp=gp: nc.vector.tensor_tensor(
                                out=hidT[:, hc, 2 * gp:2 * gp + 2, :], in0=ub[:, :].rearrange("p (a m) -> p a m", a=2),
                                in1=sg[:, :].rearrange("p (a m) -> p a m", a=2), op=ALU.mult),
                                reads=["ub", "sg"], writes=[("hidT", hc, gp)])
                    hkeys = [("hidT", hc, gp) for hc in range(2) for gp in range(2)]
                    if st == 0:
                        for gp in range(2):
                            bo = nextbank()

                            def mk(bo=bo, gp=gp):
                                inst = None
                                for hc in range(2):
                                    inst = nc.tensor.matmul(banks[bo][:, :], lhsT=w2[:, hc, :],
                                                            rhs=hidT[:, hc, 2 * gp:2 * gp + 2, :], start=(hc == 0), stop=(hc == 1))
                                return inst
                            sc.op("pe", mk, reads=hkeys + ["w2"], writes=[bk(bo)])
                            sc.op("act", lambda bo=bo, gp=gp: nc.scalar.copy(
                                out=kcmpT[:, 2 * gp:2 * gp + 2, :], in_=banks[bo][:, :].rearrange("p (a m) -> p a m", a=2)),
                                reads=[bk(bo)], writes=[("kcmpT", gp)])
                    else:
                        for nt in range(2):
                            bo = nextbank()

                            def mv(bo=bo, nt=nt):
                                inst = None
                                for g in range(4):
                                    for hc in range(2):
                                        inst = nc.tensor.matmul(banks[bo][:, g * 64:(g + 1) * 64],
                                                                lhsT=hidT[:, hc, g, nt * 128:(nt + 1) * 128],
                                                                rhs=w2[:, hc, 0:64], start=(g == 0 and hc == 0), stop=(hc == 1),
                                                                skip_group_check=True)
                                return inst
                            sc.op("pe", mv, reads=hkeys + ["w2"], writes=[bk(bo)])
                            sc.op("act", lambda bo=bo, nt=nt: nc.scalar.copy(
                                out=vcmp[:, nt, :, :], in_=banks[bo][:, 0:256].rearrange("p (g d) -> p g d", g=4)),
                                reads=[bk(bo)], writes=[("vcmp", nt)])
                sc.flush()

            Ksel = sb("Ksel", [128, 4, 4096], BF16)
            KwT = sb("KwT", [128, 4, 4096], BF16)
            Vall = sb("Vall", [128, 32, 520], BF16)
            cmask = sb("cmask_sb", [128, 2, 2048], BF16)
            fmul = sb("fmul_sb", [128, 16, 64], F32)
            fadd = sb("fadd_sb", [128, 16, 64], F32)
            tri = sb("tri_sb", [128, 128], BF16)
            upp = sb("upp_sb", [128, 128], BF16)
            woutb = sb("woutb_sb", [128, 8, 1024], BF16)
            gfin = sb("gfin_sb", [128, 1024], F32)
            Qsel = [sb("Qsel%d" % i, [128, 16, 128], BF16) for i in range(2)]
            Qwin = [sb("Qwin%d" % i, [128, 16, 128], BF16) for i in range(2)]
            Qc = [sb("Qc%d" % i, [128, 16, 128], BF16) for i in range(2)]
            gts = [sb("gts%d" % i, [128, 16, 3], F32) for i in range(2)]
            szs = sb("szs", [128, 1024], F32)
            x1s = sb("x1s", [128, 1024], F32)
            oacc2 = [sb("oacc%d" % i, [128, 16, 64], F32) for i in range(2)]
            otmp = [sb("otmp%d" % i, [128, 4, 64], F32) for i in range(2)]
            NPT = 4
            PTb = [sb("PTb%d" % i, [128, 512], BF16) for i in range(NPT)]
            rs = [sb("rs%d" % i, [128, 4], F32) for i in range(3)]
            rc = [sb("rc%d" % i, [128, 4], F32) for i in range(3)]
            wg = [sb("wg%d" % i, [128, 4], F32) for i in range(3)]
            impb = sb("impb", [128, 64], F32)
            scr = sb("scr", [128, 64], F32)
            scr2 = sb("scr2", [128, 64], F32)
            m8a = sb("m8a", [128, 8], F32)
            m8b = sb("m8b", [128, 8], F32)
            thr = sb("thr", [128, 1], F32)
            negb4 = [sb("negb%d" % i, [128, 128], BF16) for i in range(4)]
            ybf = sb("ybf", [128, 1024], BF16)
            yT = sb("cyT", [128, 8, 128], BF16)
            x2 = sb("x2", [128, 1024], F32)
            fss = sb("fss", [128, 16], F32)
            fsq = sb("fsq", [128, 16], F32)
            frs = sb("frs", [128, 16], F32)
            junk = sb("junk", [128, 1024], BF16)
            osb = sb("osb", [128, 1024], F32)
            identF = sb("identF", [128, 128], F32)
            accT = [sb("accT%d" % i, [128, 512], F32) for i in range(2)]
            sc.dma("sp", lambda: nc.sync.dma_start(out=identF[:], in_=ident_d[:, :]), writes=["identF"])
            for i in range(2):
                sc.op("pool", lambda i=i: nc.gpsimd.memset(accT[i][:], 0.0), writes=[("accT", i)])

            for g in range(4):
                sc.dma("sp", lambda g=g: nc.sync.dma_start(out=Ksel[0:64, g, :], in_=kT4[8 + g, :, :]),
                       reads=all_kT4, writes=[("Ksel", g)])
                sc.op("pool", lambda g=g: nc.gpsimd.memset(KwT[64:128, g, :], 0.0), writes=[("KwTz", g)])
                sc.dma("sp", lambda g=g: nc.sync.dma_start(out=KwT[0:64, g, :], in_=kT4[12 + g, :, :]),
                       reads=all_kT4, writes=[("KwT", g)])
                for hh in range(2):
                    sc.dma("pool", lambda g=g, hh=hh: nc.gpsimd.dma_start(
                        out=Ksel[64:128, g, 2048 * hh:2048 * (hh + 1)], in_=E_d[:, 2048 * hh:2048 * (hh + 1)]),
                        writes=[("KselE", g, hh)])
                    sc.dma("pool", lambda g=g, hh=hh: nc.gpsimd.dma_start(
                        out=KwT[64:65, g, 2048 * hh:2048 * (hh + 1)], in_=kbias_d[:, 2048 * hh:2048 * (hh + 1)]),
                        reads=[("KwTz", g)], writes=[("KwTb", g, hh)])
            for ch in range(8):
                sc.dma("sp", lambda ch=ch: nc.sync.dma_start(
                    out=Vall[:, 4 * ch:4 * ch + 4, :], in_=vtok[512 * ch:512 * (ch + 1), :].rearrange("(i p) f -> p i f", p=128)),
                    reads=[("vtok", ch)], writes=[("Vall", ch)])
            for hh in range(2):
                sc.dma("pool", lambda hh=hh: nc.gpsimd.dma_start(out=cmask[:, hh, :], in_=cmask_d[:, 2048 * hh:2048 * (hh + 1)]),
                       writes=[("cmask", hh)])
            sc.dma("sp", lambda: nc.sync.dma_start(out=fmul[:], in_=fmul_d[:, :].rearrange("p (t j) -> p t j", j=64)), writes=["fmul"])
            sc.dma("sp", lambda: nc.sync.dma_start(out=fadd[:], in_=fadd_d[:, :].rearrange("p (t j) -> p t j", j=64)), writes=["fadd"])
            sc.dma("pool", lambda: nc.gpsimd.dma_start(out=tri[:], in_=tri_d[:, :]), writes=["tri"])
            sc.dma("pool", lambda: nc.gpsimd.dma_start(out=upp[:], in_=upp_d[:, :]), writes=["upp"])
            for q in range(8):
                sc.dma("pool", lambda q=q: nc.gpsimd.dma_start(out=woutb[:, q, :], in_=woutb_d[:, 1024 * q:1024 * (q + 1)]),
                       writes=[("woutb", q)])
            sc.dma("sp", lambda: nc.sync.dma_start(out=gfin[:], in_=gfin_d[:, :]), writes=["gfin"])
            for i in range(2):
                sc.op("pool", lambda i=i: nc.gpsimd.memset(Qwin[i][64:128, :, :], 0.0), writes=[("Qwin1", i)])
                sc.op("pool", lambda i=i: nc.gpsimd.memset(Qwin[i][64:65, :, :], 1.0), reads=[("Qwin1", i)], writes=[("Qwin1", i)])
                sc.op("pool", lambda i=i: nc.gpsimd.memset(Qc[i][64:128, :, :], 0.0), writes=[("Qc0", i)])
            for i in range(4):
                sc.op("pool", lambda i=i: nc.gpsimd.memset(negb4[i][:, 0:64], 0.0), writes=[("negb0", i)])
            ksel_keys = lambda g: [("Ksel", g), ("KselE", g, 0), ("KselE", g, 1)]
            kw_keys = lambda g: [("KwT", g), ("KwTz", g), ("KwTb", g, 0), ("KwTb", g, 1)]
            sbank = [0]
            ptc = [0]

            def emit_score(u):
                bS = sbank[0] % 3
                sbank[0] += 1
                pi = ptc[0] % NPT
                ptc[0] += 1
                lhsT, rhs, rkeys, mask, mkeys = u["score"]
                sc.op("pe", lambda: nc.tensor.matmul(banks[bS][:, :], lhsT=lhsT, rhs=rhs, start=True, stop=True),
                      reads=rkeys, writes=[bk(bS)])
                sc.op("act", lambda: nc.scalar.activation(out=PTb[pi][:], in_=banks[bS][:, :], func=AF.Exp, scale=0.125),
                      reads=[bk(bS)], writes=[("PT", pi)])
                if mask is not None:
                    sc.op("dve", lambda: nc.vector.tensor_tensor(
                        out=PTb[pi][:, :].rearrange("p (h q) -> p h q", h=4),
                        in0=PTb[pi][:, :].rearrange("p (h q) -> p h q", h=4),
                        in1=mask, op=ALU.mult), reads=[("PT", pi)] + list(mkeys), writes=[("PT", pi)])
                return pi

            def emit_pv(u, pi):
                for pv_ in u["pvs"]:
                    if pv_[0] == "vstat":
                        _, accb, lhsT_, rkeys, first, last = pv_
                        sc.op("pe", lambda accb=accb, lhsT_=lhsT_, first=first, last=last: nc.tensor.matmul(
                            banks[accb][0:65, :], lhsT=lhsT_, rhs=PTb[pi][:, :], start=first, stop=last),
                            reads=[("PT", pi)] + rkeys, writes=[bk(accb)])
                        continue
                    (accb, col0, width, rhs_, rkeys, first, last) = pv_

                    def f(accb=accb, col0=col0, width=width, rhs_=rhs_, first=first, last=last):
                        inst = None
                        for h in range(4):
                            inst = nc.tensor.matmul(banks[accb][:, col0 + h * width:col0 + (h + 1) * width],
                                                    lhsT=PTb[pi][:, h * 128:(h + 1) * 128], rhs=rhs_,
                                                    start=(first and h == 0), stop=last, skip_group_check=True)
                        return inst
                    sc.op("pe", f, reads=[("PT", pi)] + rkeys, writes=[bk(accb)])

            def untranspose(accb, ai, then):
                sc.op("dve", lambda: nc.vector.tensor_copy(out=accT[ai][0:65, :], in_=banks[accb][0:65, :]),
                      reads=[bk(accb), ("accT", ai)], writes=[("accT", ai)])

                def tail():
                    def tr4():
                        inst = None
                        for h in range(4):
                            inst = nc.tensor.transpose(out=banks[accb][:, h * 128:(h + 1) * 128],
                                                       in_=accT[ai][:, h * 128:(h + 1) * 128], identity=identF[:])
                        return inst
                    sc.op("pe", tr4, reads=[("accT", ai), "identF"], writes=[bk(accb)])
                    then()
                deferred.append([_DBG.get("dunt", 2), tail])

            def branch_out(accb, g, gidx, gt, slot, stride=65, ob=0):
                oacc = oacc2[ob]
                av = banks[accb][:, 0:4 * stride].rearrange("p (h c) -> p h c", c=stride)
                r_, c_, w_ = rs[slot], rc[slot], wg[slot]
                sc.op("dve", lambda: nc.vector.tensor_scalar(out=r_[:], in0=av[:, :, 64], scalar1=1e-30, scalar2=None,
                                                             op0=ALU.max), reads=[bk(accb)], writes=[("rs", slot)])
                sc.op("dve", lambda: nc.vector.reciprocal(out=c_[:], in_=r_[:]), reads=[("rs", slot)], writes=[("rc", slot)])
                sc.op("dve", lambda: nc.vector.tensor_tensor(out=w_[:], in0=c_[:], in1=gt[:, 4 * g:4 * g + 4, gidx], op=ALU.mult),
                      reads=[("rc", slot), "gts"], writes=[("wg", slot)])
                ot = otmp[slot - 1]
                sc.op("dve", lambda: nc.vector.tensor_tensor(
                    out=ot[:], in0=av[:, :, 0:64], in1=w_[:, :, None].broadcast_to([128, 4, 64]), op=ALU.mult),
                    reads=[bk(accb), ("wg", slot)], writes=[("otmp", slot)])
                sc.op("pool", lambda: nc.gpsimd.tensor_tensor(
                    out=oacc[:, 4 * g:4 * g + 4, :], in0=oacc[:, 4 * g:4 * g + 4, :], in1=ot[:], op=ALU.add),
                    reads=[("otmp", slot), ("oacc", ob, g)], writes=[("oacc", ob, g)])

            def select_chain(qs, g, qb):
                gt = gts[qb]
                oacc = oacc2[qb]
                negb = negb4[g]
                iv = banks[3][:, 0:256].rearrange("p (h c) -> p h c", c=64)
                r_, c_, w_ = rs[0], rc[0], wg[0]
                sc.op("dve", lambda: nc.vector.reduce_sum(out=r_[:], in_=iv, axis=mybir.AxisListType.X),
                      reads=[bk(3)], writes=[("rs", 0)])
                sc.op("dve", lambda: nc.vector.tensor_scalar(out=r_[:], in0=r_[:], scalar1=0.5, scalar2=1e-30,
                                                             op0=ALU.mult, op1=ALU.max), reads=[("rs", 0)], writes=[("rs", 0)])
                sc.op("dve", lambda: nc.vector.reciprocal(out=c_[:], in_=r_[:]), reads=[("rs", 0)], writes=[("rc", 0)])
                sc.op("dve", lambda: nc.vector.tensor_scalar(out=impb[:], in0=iv[:, 0, 0:64], scalar1=c_[:, 0:1],
                                                             scalar2=None, op0=ALU.mult),
                      reads=[bk(3), ("rc", 0)], writes=["impb"])
                for h in range(1, 4):
                    sc.op("dve", lambda h=h: nc.vector.scalar_tensor_tensor(
                        out=impb[:], in0=iv[:, h, 0:64], scalar=c_[:, h:h + 1], in1=impb[:], op0=ALU.mult, op1=ALU.add),
                        reads=[bk(3), ("rc", 0), "impb"], writes=["impb"])
                sc.op("dve", lambda: nc.vector.tensor_tensor(out=w_[:], in0=c_[:], in1=gt[:, 4 * g:4 * g + 4, 0], op=ALU.mult),
                      reads=[("rc", 0), "gts"], writes=[("wg", 0)])
                sc.op("dve", lambda: nc.vector.tensor_tensor(
                    out=oacc[:, 4 * g:4 * g + 4, :], in0=banks[3][:, 256:512].rearrange("p (h c) -> p h c", c=64),
                    in1=w_[:, :, None].broadcast_to([128, 4, 64]), op=ALU.mult),
                    reads=[bk(3), ("wg", 0)], writes=[("oacc", qb, g)])
                sc.op("dve", lambda: nc.vector.tensor_tensor(out=scr[:], in0=impb[:], in1=fmul[:, qs, :], op=ALU.mult),
                      reads=["impb", "fmul"], writes=["scr"])
                sc.op("dve", lambda: nc.vector.tensor_tensor(out=scr[:], in0=scr[:], in1=fadd[:, qs, :], op=ALU.add),
                      reads=["scr", "fadd"], writes=["scr"])
                sc.op("dve", lambda: nc.vector.max(out=m8a[:], in_=scr[:]), reads=["scr"], writes=["m8a"])
                sc.op("dve", lambda: nc.vector.match_replace(out=scr2[:], in_to_replace=m8a[:], in_values=scr[:], imm_value=-2e9),
                      reads=["scr", "m8a"], writes=["scr2"])
                sc.op("dve", lambda: nc.vector.max(out=m8b[:], in_=scr2[:]), reads=["scr2"], writes=["m8b"])
                sc.op("dve", lambda: nc.vector.tensor_scalar(out=thr[:], in0=m8b[:, 7:8], scalar1=-1e8, scalar2=None, op0=ALU.max),
                      reads=["m8b"], writes=["thr"])
                sc.op("dve", lambda: nc.vector.tensor_scalar(out=negb[:, 64:128], in0=scr[:], scalar1=thr[:, 0:1], scalar2=NEG,
                                                             op0=ALU.is_lt, op1=ALU.mult),
                      reads=["scr", "thr"], writes=[("negb", g)])

                def tail():
                    sc.op("pe", lambda: nc.tensor.transpose(out=bfview(7)[:, 0:128], in_=negb[:, :], identity=ident[:]),
                          reads=[("negb", g), ("negb0", g), "ident"], writes=[bk(7)])
                    sc.op("dve", lambda: nc.vector.tensor_copy(
                        out=Qsel[qb][64:128, 4 * g:4 * g + 4, :], in_=bfview(7)[64:128, None, 0:128].broadcast_to([64, 4, 128])),
                        reads=[bk(7)], writes=[("QselB", qb, g)])
                deferred.append([_DBG.get("dsel", 5), tail])

            def load_q(qs):
                qb = qs % 2
                q0 = qs * 128
                sc.dma("sp", lambda: nc.sync.dma_start(
                    out=Qsel[qb][0:64, :, :], in_=qT[16:32, :, q0:q0 + 128].rearrange("a d t -> d a t")),
                    reads=[("qT", qs // 4)], writes=[("QselQ", qb)])
                sc.dma("sp", lambda: nc.sync.dma_start(
                    out=Qwin[qb][0:64, :, :], in_=qT[16:32, :, q0:q0 + 128].rearrange("a d t -> d a t")),
                    reads=[("qT", qs // 4)], writes=[("QwinQ", qb)])
                sc.dma("sp", lambda: nc.sync.dma_start(
                    out=Qc[qb][0:64, :, :], in_=qT[0:16, :, q0:q0 + 128].rearrange("a d t -> d a t")),
                    reads=[("qT", qs // 4)], writes=[("Qc", qb)])
                sc.dma("sp", lambda: nc.sync.dma_start(
                    out=gts[qb][:, :, :], in_=gated[q0:q0 + 128, :].rearrange("p (h c) -> p h c", c=3)),
                    reads=[("gated", qs)], writes=["gts"])

            def load_tail(qs):
                q0 = qs * 128
                sc.dma("sp", lambda: nc.sync.dma_start(out=szs[:], in_=szd[q0:q0 + 128, :]),
                       reads=[("szd", qs)], writes=["szs"])
                sc.dma("sp", lambda: nc.sync.dma_start(out=x1s[:], in_=x1d[q0:q0 + 128, :]),
                       reads=[("x1d", qs)], writes=["x1s"])

            mhalf = sb("mhalf", [128, 1], F32)
            sc.op("pool", lambda: nc.gpsimd.memset(mhalf[:], -0.5), writes=["mhalf"])

            def epilogue(qs):
                q0 = qs * 128
                okeys = [("oacc", qs % 2, g) for g in range(4)]
                oacc = oacc2[qs % 2]
                sc.op("dve", lambda: nc.vector.tensor_tensor(out=ybf[:], in0=oacc[:, :, :].rearrange("p h d -> p (h d)"),
                                                             in1=szs[:], op=ALU.mult),
                      reads=okeys + ["szs"], writes=["ybf"])

                def p1():
                    def try_():
                        inst = None
                        for kc in range(8):
                            inst = nc.tensor.transpose(out=bfview(7)[:, kc * 128:(kc + 1) * 128],
                                                       in_=ybf[:, kc * 128:(kc + 1) * 128], identity=ident[:])
                        return inst
                    sc.op("pe", try_, reads=["ybf", "ident"], writes=[bk(7)])

                def p2():
                    sc.op("dve", lambda: nc.vector.tensor_copy(out=yT[:, :, :], in_=bfview(7)[:, :].rearrange("p (k t) -> p k t", k=8)),
                          reads=[bk(7)], writes=["cyT"])

                def p3():
                    for nh in range(2):
                        def mo(nh=nh):
                            inst = None
                            for kc in range(8):
                                inst = nc.tensor.matmul(banks[4 + nh][:, :], lhsT=yT[:, kc, :],
                                                        rhs=woutb[:, kc, nh * 512:(nh + 1) * 512],
                                                        start=(kc == 0), stop=(kc == 7))
                            return inst
                        sc.op("pe", mo, reads=["cyT"] + [("woutb", q) for q in range(8)], writes=[bk(4 + nh)])

                def p4():
                    for nh in range(2):
                        sc.op("dve", lambda nh=nh: nc.vector.tensor_tensor(
                            out=x2[:, nh * 512:(nh + 1) * 512], in0=banks[4 + nh][:, :], in1=x1s[:, nh * 512:(nh + 1) * 512],
                            op=ALU.add), reads=[bk(4 + nh), "x1s"], writes=[("x2", nh)])
                    sc.op("dve", lambda: nc.vector.scalar_tensor_tensor(
                        out=osb[:], in0=x2[:], scalar=1.0, in1=x2[:], op0=ALU.mult, op1=ALU.mult, accum_out=fss[:, qs:qs + 1]),
                        reads=[("x2", 0), ("x2", 1)], writes=["osb", "fss"])
                    sc.op("pool", lambda: nc.gpsimd.tensor_scalar(out=fsq[:, qs:qs + 1], in0=fss[:, qs:qs + 1], scalar1=1.0 / D,
                                                                  scalar2=EPS, op0=ALU.mult, op1=ALU.add),
                          reads=["fss"], writes=["fsq"])
                    sc.op("pool", lambda: nc.gpsimd.tensor_tensor(out=frs[:, qs:qs + 1], in0=fsq[:, qs:qs + 1], in1=mhalf[:],
                                                                  op=ALU.pow), reads=["fsq", "mhalf"], writes=["frs"])

                def p5():
                    sc.op("dve", lambda: nc.vector.scalar_tensor_tensor(
                        out=osb[:], in0=x2[:], scalar=frs[:, qs:qs + 1], in1=gfin[:], op0=ALU.mult, op1=ALU.mult),
                        reads=[("x2", 0), ("x2", 1), "frs", "gfin", "osb"], writes=["osb"])
                    sc.dma("sp", lambda: nc.sync.dma_start(out=outd[q0:q0 + 128, :], in_=osb[:]),
                           reads=["osb"], writes=[("out", qs)])
                for dly, fn in zip(_DBG.get("depi", (3, 5, 7, 10, 13)), (p1, p2, p3, p4, p5)):
                    deferred.append([dly, fn])

            stream = []

            def cmp_units(qs, g):
                qb = qs % 2
                q0 = qs * 128
                for nt in range(2):
                    stream.append(("unit", {
                        "score": (kcmpT[:, g, nt * 128:(nt + 1) * 128], Qc[qb][:, 4 * g:4 * g + 4, :],
                                  [("kcmpT", g // 2), ("Qc", qb), ("Qc0", qb)],
                                  cmask[:, nt:nt + 1, q0:q0 + 128].broadcast_to([128, 4, 128]), [("cmask", nt)]),
                        "pvs": [(3, 0, 64, AI[:, nt, :], ["AI"], nt == 0, nt == 1),
                                (3, 256, 64, vcmp[:, nt, g, :], [("vcmp", nt)], False, nt == 1)],
                        "post": ((lambda qs=qs, g=g, qb=qb: select_chain(qs, g, qb)) if nt == 1 else None)}))

            load_q(0)
            for g in range(4):
                cmp_units(0, g)
            for qs in range(16):
                qb = qs % 2
                q0 = qs * 128
                for g in range(4):
                    for k in (1, 2, 3, 0, 4):
                        kt = (qs - 4 + k) % 32
                        m = None
                        if k == 0:
                            m = upp[:, None, :].broadcast_to([128, 4, 128])
                        elif k == 4:
                            m = tri[:, None, :].broadcast_to([128, 4, 128])
                        stream.append(("unit", {
                            "score": (KwT[:, g, kt * 128:(kt + 1) * 128], Qwin[qb][:, 4 * g:4 * g + 4, :],
                                      kw_keys(g) + [("QwinQ", qb), ("Qwin1", qb)], m, ["tri", "upp"]),
                            "pvs": [(6, 0, 65, Vall[:, kt, (4 + g) * 65:(5 + g) * 65], [("Vall", kt // 4)], k == 1, k == 4)],
                            "post": ((lambda g=g, qb=qb: branch_out(6, g, 2, gts[qb], 2, ob=qb)) if k == 4 else None)}))
                    if g == 0 and qs + 1 < 16:
                        stream.append(("call", lambda qs=qs: load_q(qs + 1)))
                    if g == 3:
                        stream.append(("call", lambda qs=qs: load_tail(qs)))
                for g in range(4):
                    if qs + 1 < 16:
                        cmp_units(qs + 1, g)
                    klist = list(range(0, qs + 1)) + list(range(16, 32))
                    for idx, kt in enumerate(klist):
                        last = idx == len(klist) - 1
                        post = None
                        sb_ = 4 + g % 2
                        if last:
                            if g < 3:
                                post = (lambda g=g, qb=qb, sb_=sb_: untranspose(
                                    sb_, g % 2, lambda: branch_out(sb_, g, 1, gts[qb], 1, stride=128, ob=qb)))
                            else:
                                post = (lambda g=g, qb=qb, qs=qs, sb_=sb_: untranspose(
                                    sb_, g % 2, lambda: (branch_out(sb_, g, 1, gts[qb], 1, stride=128, ob=qb), epilogue(qs))))
                        stream.append(("unit", {
                            "score": (Ksel[:, g, kt * 128:(kt + 1) * 128], Qsel[qb][:, 4 * g:4 * g + 4, :],
                                      ksel_keys(g) + [("QselQ", qb), ("QselB", qb, g)],
                                      (tri[:, None, :].broadcast_to([128, 4, 128]) if kt == qs else None), ["tri"]),
                            "pvs": [("vstat", sb_, Vall[:, kt, g * 65:(g + 1) * 65], [("Vall", kt // 4)], idx == 0, last)],
                            "post": post}))

            LA = _DBG.get('LA', 2)
            deferred = []
            pend = []

            def retire():
                u, pi = pend.pop(0)
                emit_pv(u, pi)
                if u["post"] is not None:
                    u["post"]()
                for d_ in list(deferred):
                    d_[0] -= 1
                    if d_[0] <= 0:
                        deferred.remove(d_)
                        d_[1]()
            for kind, item in stream:
                if kind == "call":
                    item()
                    continue
                pend.append((item, emit_score(item)))
                if len(pend) > LA:
                    retire()
            while pend:
                retire()
            while deferred:
                deferred.sort(key=lambda x: x[0])
                deferred.pop(0)[1]()
            sc.flush()

    sc.finish()
    return nc, sc


def _shared_layout(inputs):
    f = np.float32
    a_w_in = np.asarray(inputs["a_w_in"], f)[0]
    w = a_w_in.reshape(8, 128, 4, 16, 128).transpose(3, 1, 0, 2, 4).reshape(16, 128, 4096)
    a_gT = np.asarray(inputs["a_norm"], f)[0].reshape(8, 128).T
    a_cw = np.asarray(inputs["a_conv_w"], f)[0].reshape(3, 16, 128).transpose(2, 1, 0).reshape(128, 48)
    w_out = np.asarray(inputs["a_w_out"], f)[0].reshape(16, 128, 1024).transpose(1, 0, 2).reshape(128, 16 * 1024)
    wkv = np.asarray(inputs["w_kv"], f).reshape(1024, 6, 256)[:, [0, 1, 2, 4, 3, 5], :].reshape(1024, 1536)
    wkv_r = wkv.reshape(8, 128, 1536).transpose(1, 0, 2).reshape(128, 8 * 1536)
    bw = np.asarray(inputs["b_w_in"], f)[0]
    bw = np.concatenate([bw[:, 0:1024], bw[:, 1072:2096], bw[:, 1024:1072]], axis=1)
    wbin_r = bw.reshape(8, 128, 2096).transpose(1, 0, 2).reshape(128, 8 * 2096)
    woutb = np.asarray(inputs["b_w_out"], f)[0].reshape(8, 128, 1024).transpose(1, 0, 2).reshape(128, 8 * 1024)

    def w1l(w1):
        return np.asarray(w1, f).reshape(16, 128, 256).transpose(1, 0, 2).reshape(128, 16 * 256)

    def w2l(w2):
        w = np.zeros((128, 2, 128), f)
        w[:, :, 0:64] = np.asarray(w2, f).reshape(2, 128, 64).transpose(1, 0, 2)
        return w.reshape(128, 256)
    tri = (np.arange(128)[:, None] <= np.arange(128)[None, :]).astype(f)
    upp = (np.arange(128)[:, None] > np.arange(128)[None, :]).astype(f)
    sh = {
        "w_in_r": w, "a_gT": a_gT, "a_cw": a_cw, "w_out_r": w_out, "ident": np.eye(128, dtype=f),
        "wkv_r": wkv_r, "wbin_r": wbin_r,
        "gkvT": np.asarray(inputs["kv_norm"], f).reshape(8, 128).T,
        "gbT": np.asarray(inputs["b_norm"], f)[0].reshape(8, 128).T,
        "w1k": w1l(inputs["cmp_w1_k"]), "w1v": w1l(inputs["cmp_w1_v"]),
        "w2k": w2l(inputs["cmp_w2_k"]), "w2v": w2l(inputs["cmp_w2_v"]),
        "poskT": np.asarray(inputs["cmp_pos_k"], f).reshape(16, 128).T, "posvT": np.asarray(inputs["cmp_pos_v"], f).reshape(16, 128).T,
        "tri": tri, "upp": upp, "woutb": woutb,
        "gfin": np.broadcast_to(np.asarray(inputs["final_norm"], f)[None, :], (128, 1024)),
    }
    return {k: np.ascontiguousarray(v, dtype=f) for k, v in sh.items()}


def _parity_consts(par):
    f = np.float32
    pos = (np.arange(S) + 2048 * par) % S
    inv = (np.float32(500000.0) ** (-np.arange(0, 16, 2, dtype=f) / np.float32(16))).astype(f)
    ang = pos.astype(f)[:, None] * inv[None, :]
    cosT = np.cos(ang).astype(f).reshape(32, 128, 8).transpose(1, 0, 2).reshape(128, 256)
    sinT = np.sin(ang).astype(f).reshape(32, 128, 8).transpose(1, 0, 2).reshape(128, 256)
    E = (pos[None, :] // 64 == np.arange(64)[:, None]).astype(f)
    kbias = np.zeros((1, S), f)
    if par == 0:
        kbias[0, 3584:] = NEG
    tq = np.arange(2048) + 2048 * par
    m = np.arange(256)
    a = (16 * m + 2048 * par) % S
    valid_m = (a + 31) < S
    n = a // 16
    cm = (valid_m[:, None] & ((16 * n + 31)[:, None] <= tq[None, :])).astype(f)
    cmask = cm.reshape(2, 128, 2048).transpose(1, 0, 2).reshape(128, 4096)
    A = np.zeros((256, 65), f)
    for mi in range(256):
        if not valid_m[mi] or n[mi] > 254:
            continue
        j, r = divmod(int(n[mi]), 4)
        A[mi, j] += 2.0 if r < 3 else 1.0
        if r == 3 and j + 1 < 64:
            A[mi, j + 1] += 1.0
    A[:, 64] = 1.0
    Aaug = A.reshape(2, 128, 65).transpose(1, 0, 2).reshape(128, 130)
    cur = tq // 64
    jj = np.arange(64)[None, :]
    validb = jj <= cur[:, None]
    forced = (jj == 0) | (validb & (jj > cur[:, None] - 2))
    fmul = (validb & ~forced).astype(f)
    fadd = np.where(forced, f(1e4), np.where(validb, f(0.0), f(-1e9))).astype(f)
    fmul = fmul.reshape(16, 128, 64).transpose(1, 0, 2).reshape(128, 1024)
    fadd = fadd.reshape(16, 128, 64).transpose(1, 0, 2).reshape(128, 1024)
    c = {"cosT": cosT, "sinT": sinT, "Emat": E, "kbias": kbias, "cmask": cmask, "Aaug": Aaug, "fmul": fmul, "fadd": fadd}
    return {k: np.ascontiguousarray(v, dtype=f) for k, v in c.items()}


def make_in_maps(inputs, cores=range(NCORES)):
    sh = _shared_layout(inputs)
    pc = [_parity_consts(0), _parity_consts(1)]
    x = np.asarray(inputs["x"], np.float32)
    maps = []
    for c in cores:
        b, par = c // 2, c % 2
        m = dict(sh)
        m.update(pc[par])
        m["xb"] = np.ascontiguousarray(np.roll(x[b], -2048 * par, axis=0))
        xh = np.zeros((4, D), np.float32)
        if par == 0:
            xh[2:4] = x[b, 2046:2048]
        else:
            xh[0:2] = x[b, 2046:2048]
        m["xh"] = xh
        maps.append(m)
    return maps


_CACHE = {}


def kernel(**inputs):
    if "nc" not in _CACHE:
        _CACHE["nc"] = build_program()[0]
    nc = _CACHE["nc"]
    in_maps = make_in_maps(inputs)
    res = run_bass_kernel_spmd(nc, in_maps, core_ids=list(range(NCORES)))
    out = np.empty((4, S, D), np.float32)
    for c in range(NCORES):
        b, par = c // 2, c % 2
        out[b, 2048 * par:2048 * (par + 1)] = res.results[c]["out"]
    return out
```

```python
import numpy as np
import concourse.bass as bass
import concourse.mybir as mybir
from concourse.bass_utils import run_bass_kernel_spmd
from contextlib import ExitStack

F32 = mybir.dt.float32
BF16 = mybir.dt.bfloat16
AF = mybir.ActivationFunctionType
ALU = mybir.AluOpType

D = 1024
S = 4096
CONV_D = 2048
EPS = 1e-6
NCORES = 8
_DBG = {"pe_sync": 0}


class Sched:
    def __init__(self, nc, n_dma_sems=10, same_engine_sync=False):
        self.nc = nc
        self.eng = {"pe": nc.tensor, "act": nc.scalar, "dve": nc.vector,
                    "pool": nc.gpsimd, "sp": nc.sync}
        self.ops = []
        self.last_w = {}
        self.readers = {}
        self.same_engine_sync = same_engine_sync
        self.n_dma_sems = n_dma_sems
        self.last_on = {}
        self.dma_all = []
        self.n_emitted = 0
        self.esem = None

    def _add(self, kind, eng, fn, reads, writes):
        idx = len(self.ops)
        reads = list(reads)
        writes = list(writes)
        for k in list(reads):
            if isinstance(k, tuple) and k and k[0] == "bank":
                reads.remove(k)
                if k not in writes:
                    writes.append(k)
        deps = set()
        for k in reads + writes:
            if k in self.last_w:
                deps.add(self.last_w[k])
        for k in writes:
            r = self.readers.get(k)
            if r:
                deps.update(r["c"].values())
                deps.update(r["d"])
        for k in writes:
            self.last_w[k] = idx
            self.readers[k] = {"c": {}, "d": []}
        for k in reads:
            if k in writes:
                continue
            r = self.readers.setdefault(k, {"c": {}, "d": []})
            if kind == "dma":
                r["d"].append(idx)
            else:
                r["c"][eng] = idx
        deps.discard(idx)
        self.ops.append({"kind": kind, "eng": eng, "fn": fn, "deps": deps, "need_inc": False})
        if kind == "dma":
            self.dma_all.append(idx)
        else:
            self.last_on[eng] = idx
        return idx

    def op(self, eng, fn, reads=(), writes=()):
        return self._add("c", eng, fn, reads, writes)

    def dma(self, queue, fn, reads=(), writes=()):
        return self._add("dma", queue, fn, reads, writes)

    def barrier(self):
        deps = set(self.last_on.values()) | set(self.dma_all)
        for e in ("pe", "act", "dve", "pool", "sp"):
            idx = len(self.ops)
            self.ops.append({"kind": "c", "eng": e, "fn": None, "deps": set(deps), "need_inc": False, "bar": True})
        self.dma_all = []

    def flush(self):
        self.barrier()
        nc = self.nc
        ops = self.ops
        start = self.n_emitted
        pe_sync = _DBG.get("pe_sync")

        def implicit(od, o):
            return (od["eng"] == o["eng"] and o["kind"] == "c" and od["kind"] == "c"
                    and (not self.same_engine_sync or (o["eng"] == "pe" and not pe_sync)))
        for o in ops[start:]:
            for d in o["deps"]:
                od = ops[d]
                if d < start or od["kind"] == "dma" or implicit(od, o):
                    continue
                od["need_inc"] = True
        if self.esem is None:
            self.esem = {e: nc.alloc_semaphore(name="sem_" + e) for e in ("pe", "act", "dve", "pool")}
            self.ecount = {e: 0 for e in self.esem}
            self.dsem = {}
            self.dcount = {}
            for q in ("sp", "act", "pool"):
                self.dsem[q] = [nc.alloc_semaphore(name="dma_%s_%d" % (q, i)) for i in range(self.n_dma_sems)]
                self.dcount[q] = 0
            self.waited = {e: {} for e in self.eng}
        esem, ecount, dsem, dcount, waited = self.esem, self.ecount, self.dsem, self.dcount, self.waited
        K = self.n_dma_sems
        for o in ops[start:]:
            e = o["eng"]
            engine = self.eng[e]
            waits = {}
            for d in o["deps"]:
                od = ops[d]
                if d < start and not o.get("bar"):
                    continue
                if od["kind"] == "dma":
                    key = ("d", od["eng"], od["slot"])
                    waits[key] = max(waits.get(key, 0), od["val"])
                else:
                    if implicit(od, o) or "val" not in od:
                        continue
                    key = ("c", od["eng"])
                    waits[key] = max(waits.get(key, 0), od["val"])
            if o["kind"] == "dma":
                j = dcount[e]
                slot = j % K
                if j >= K:
                    key = ("d", e, slot)
                    waits[key] = max(waits.get(key, 0), 16 * (j // K))
            for key, val in waits.items():
                if waited[e].get(key, 0) >= val:
                    continue
                waited[e][key] = val
                sem = dsem[key[1]][key[2]] if key[0] == "d" else esem[key[1]]
                engine.wait_ge(sem, val)
            if o["fn"] is None:
                continue
            inst = o["fn"]()
            if o["kind"] == "dma":
                j = dcount[e]
                o["slot"] = j % K
                o["val"] = 16 * (j // K + 1)
                inst.then_inc(dsem[e][o["slot"]], 16)
                dcount[e] = j + 1
            elif o["need_inc"]:
                ecount[e] += 1
                o["val"] = ecount[e]
                inst.then_inc(esem[e], 1)
        self.n_emitted = len(ops)
        self.stats = {"ops": len(ops), "incs": dict(ecount), "dmas": dict(dcount)}

    def finish(self):
        self.flush()


NEG = -30000.0


def build_program(stages=("A", "B", "C"), dbg=False, same_engine_sync=True):
    nc = bass.Bass("TRN2", target_bir_lowering=False)
    sc = Sched(nc, same_engine_sync=same_engine_sync)
    scratch_kind = "ExternalOutput" if dbg else "Internal"

    def dram(name, shape, dtype, kind):
        return nc.dram_tensor(name, list(shape), dtype, kind=kind).ap()

    def din(name, shape, dtype=F32):
        return dram(name, shape, dtype, "ExternalInput")

    xb = din("xb", [S, D])
    xh = din("xh", [4, D])
    w_in_r = din("w_in_r", [16, 128, 4096])
    a_gT = din("a_gT", [128, 8])
    a_cw = din("a_cw", [128, 48])
    w_out_r = din("w_out_r", [128, 16 * 1024])
    ident_d = din("ident", [128, 128])
    x1d = dram("x1", [S, D], F32, scratch_kind)
    wkv_r = din("wkv_r", [128, 8 * 1536])
    wbin_r = din("wbin_r", [128, 8 * 2096])
    gkvT_d = din("gkvT", [128, 8])
    gbT_d = din("gbT", [128, 8])
    cos_d = din("cosT", [128, 32 * 8])
    sin_d = din("sinT", [128, 32 * 8])
    kT4 = dram("kT4", [16, 64, S], BF16, scratch_kind)
    vtok = dram("vtok", [S, 520], BF16, scratch_kind)
    qT = dram("qT", [32, 64, 2048], BF16, scratch_kind)
    szd = dram("szd", [2048, 1024], F32, scratch_kind)
    gated = dram("gated", [2048, 48], F32, scratch_kind)
    w1k_d = din("w1k", [128, 16 * 256])
    w1v_d = din("w1v", [128, 16 * 256])
    w2k_d = din("w2k", [128, 2 * 128])
    w2v_d = din("w2v", [128, 2 * 128])
    posk_d = din("poskT", [128, 16])
    posv_d = din("posvT", [128, 16])
    E_d = din("Emat", [64, S])
    kbias_d = din("kbias", [1, S])
    cmask_d = din("cmask", [128, 2 * 2048])
    Aaug_d = din("Aaug", [128, 2 * 65])
    fmul_d = din("fmul", [128, 16 * 64])
    fadd_d = din("fadd", [128, 16 * 64])
    tri_d = din("tri", [128, 128])
    upp_d = din("upp", [128, 128])
    woutb_d = din("woutb", [128, 8 * 1024])
    gfin_d = din("gfin", [128, 1024])
    outd = dram("out", [2048, D], F32, "ExternalOutput")

    banks = [nc.alloc_psum_tensor("bank%d" % i, [128, 512], F32) for i in range(8)]
    bankctr = [0]

    def nextbank():
        b = bankctr[0] % 8
        bankctr[0] += 1
        return b

    def bk(i):
        return ("bank", i)

    def bfview(i):
        return banks[i].bitcast(BF16)

    ident = nc.alloc_sbuf_tensor("ident_sb", [128, 128], BF16)
    sc.dma("pool", lambda: nc.gpsimd.dma_start(out=ident[:], in_=ident_d[:, :]), writes=["ident"])

    def rms_front(stk, tag, xin, xhat, ss, sq, rstd, col, rows=128):
        sc.op("act", lambda: nc.scalar.activation(
            out=xhat[0:rows, :], in_=xin[0:rows, :], func=AF.Square, accum_out=ss[0:rows, col:col + 1]),
            reads=[(tag, "xin")], writes=[(tag, "xhat"), (tag, "ss", col)])
        sc.op("act", lambda: nc.scalar.activation(
            out=sq[0:rows, col:col + 1], in_=ss[0:rows, col:col + 1], func=AF.Sqrt, scale=1.0 / D, bias=EPS),
            reads=[(tag, "ss", col)], writes=[(tag, "sq", col)])
        sc.op("dve", lambda: nc.vector.reciprocal(out=rstd[0:rows, col:col + 1], in_=sq[0:rows, col:col + 1]),
              reads=[(tag, "sq", col)], writes=[(tag, "rstd", col)])
        sc.op("dve", lambda: nc.vector.tensor_scalar(
            out=xhat[0:rows, :], in0=xin[0:rows, :], scalar1=rstd[0:rows, col:col + 1], scalar2=None, op0=ALU.mult),
            reads=[(tag, "xin"), (tag, "rstd", col)], writes=[(tag, "xhat")])

    if "A" in stages:
        with ExitStack() as stk:
            def sb(name, shape, dtype):
                return stk.enter_context(nc.sbuf_tensor(name, list(shape), dtype))
            gT = sb("gT", [128, 8], F32)
            cw = sb("cw", [128, 48], F32)
            woutT = sb("woutT", [128, 16, 1024], BF16)
            hT = sb("hT", [128, 8, 2048], BF16)
            hTh = sb("hTh", [128, 8, 4], BF16)
            yT = sb("yT", [128, 16, 2048], BF16)
            wblk = [sb("wblk%d" % i, [128, 4096], BF16) for i in range(2)]
            xin = [sb("xin%d" % i, [128, 1024], F32) for i in range(2)]
            xhat = [sb("xhat%d" % i, [128, 1024], BF16) for i in range(2)]
            ss = sb("ss", [128, 17], F32)
            sq = sb("sq", [128, 17], F32)
            rstd = sb("rstd", [128, 17], F32)
            vhalo = sb("vhalo", [128, 16, 2], F32)
            hcs = sb("hcs", [128, 4], F32)
            csb = [sb("csb%d" % i, [128, 512], F32) for i in range(2)]
            vbuf = [sb("vbuf%d" % i, [128, 514], F32) for i in range(2)]
            tbuf = [sb("tbuf%d" % i, [128, 512], F32) for i in range(2)]
            szb = [sb("szb%d" % i, [128, 512], F32) for i in range(2)]
            x1t = [sb("x1t%d" % i, [128, 1024], F32) for i in range(2)]

            sc.dma("sp", lambda: nc.sync.dma_start(out=gT[:], in_=a_gT[:, :]), writes=["gT"])
            sc.dma("sp", lambda: nc.sync.dma_start(out=cw[:], in_=a_cw[:, :]), writes=["cw"])
            for q in range(8):
                sc.dma("pool", lambda q=q: nc.gpsimd.dma_start(
                    out=woutT[:, 2 * q:2 * q + 2, :],
                    in_=w_out_r[:, 2048 * q:2048 * (q + 1)].rearrange("p (a n) -> p a n", a=2)),
                    writes=[("woutT", q)])

            sc.op("pool", lambda: nc.gpsimd.memset(xin[1][:], 0.0), writes=[("A1", "xin")])
            sc.dma("sp", lambda: nc.sync.dma_start(out=xin[1][0:4, :], in_=xh[:, :]), reads=[("A1", "xin")], writes=[("A1", "xin")])
            rms_front(stk, "A1", xin[1], xhat[1], ss, sq, rstd, 16)
            bi = nextbank()

            def trh(bi=bi):
                pst = bfview(bi)
                inst = None
                for kc in range(8):
                    inst = nc.tensor.transpose(out=pst[:, kc * 128:(kc + 1) * 128],
                                               in_=xhat[1][:, kc * 128:(kc + 1) * 128], identity=ident[:])
                return inst
            sc.op("pe", trh, reads=[("A1", "xhat"), "ident"], writes=[bk(bi)])
            sc.op("dve", lambda bi=bi: nc.vector.tensor_tensor(
                out=hTh[:, :, :], in0=bfview(bi)[:, :].rearrange("p (k t) -> p k t", k=8)[:, :, 0:4],
                in1=gT[:, :, None].broadcast_to([128, 8, 4]), op=ALU.mult),
                reads=[bk(bi), "gT"], writes=["hTh"])

            nblk = 0
            nchunk = 0
            for hf in range(2):
                tok0 = hf * 2048
                for tt in range(16):
                    s = tt % 2
                    tg = "A%d" % s
                    r0 = tok0 + tt * 128
                    sc.dma("sp", lambda s=s, r0=r0: nc.sync.dma_start(out=xin[s][:], in_=xb[r0:r0 + 128, :]),
                           writes=[(tg, "xin")])
                    rms_front(stk, tg, xin[s], xhat[s], ss, sq, rstd, tt)
                    bi = nextbank()

                    def tr(s=s, bi=bi):
                        pst = bfview(bi)
                        inst = None
                        for kc in range(8):
                            inst = nc.tensor.transpose(out=pst[:, kc * 128:(kc + 1) * 128],
                                                       in_=xhat[s][:, kc * 128:(kc + 1) * 128], identity=ident[:])
                        return inst
                    sc.op("pe", tr, reads=[(tg, "xhat"), "ident"], writes=[bk(bi)])
                    sc.op("dve", lambda tt=tt, bi=bi: nc.vector.tensor_tensor(
                        out=hT[:, :, tt * 128:(tt + 1) * 128],
                        in0=bfview(bi)[:, :].rearrange("p (k t) -> p k t", k=8),
                        in1=gT[:, :, None].broadcast_to([128, 8, 128]), op=ALU.mult),
                        reads=[bk(bi), "gT"], writes=[("hT", tt)])

                for cb in range(16):
                    wb = nblk % 2
                    nblk += 1
                    for hh in range(2):
                        sc.dma("pool", lambda cb=cb, wb=wb, hh=hh: nc.gpsimd.dma_start(
                            out=wblk[wb][:, 2048 * hh:2048 * (hh + 1)], in_=w_in_r[cb, :, 2048 * hh:2048 * (hh + 1)]),
                            writes=[("wblk", wb, hh)])
                    wkeys = [("wblk", wb, 0), ("wblk", wb, 1)]
                    for tc in range(4):
                        sl = nchunk % 2
                        nchunk += 1
                        c0 = tc * 512
                        bb, bc, bu, bz = 4 * sl, 4 * sl + 1, 4 * sl + 2, 4 * sl + 3
                        if tc == 0:
                            for j, bi in ((1, bb), (2, bz)):
                                def mmh(j=j, bi=bi, wb=wb):
                                    inst = None
                                    for kc in range(8):
                                        off = kc * 512 + j * 128
                                        inst = nc.tensor.matmul(banks[bi][:, 0:4], lhsT=wblk[wb][:, off:off + 128],
                                                                rhs=hTh[:, kc, :], start=(kc == 0), stop=(kc == 7))
                                    return inst
                                sc.op("pe", mmh, reads=wkeys + ["hTh"], writes=[bk(bi)])
                            sc.op("act", lambda bb=bb: nc.scalar.copy(out=hcs[:], in_=banks[bb][:, 0:4]),
                                  reads=[bk(bb)], writes=["hcs"])
                            sc.op("dve", lambda bz=bz, cb=cb, hf=hf: nc.vector.tensor_tensor(
                                out=vhalo[:, cb, :], in0=hcs[:, 2 * hf:2 * hf + 2], in1=banks[bz][:, 2 * hf:2 * hf + 2],
                                op=ALU.mult), reads=["hcs", bk(bz)], writes=[("vhalo", cb)])
                        for j in (1, 2, 0, 3):
                            bi = 4 * sl + j

                            def mm(j=j, bi=bi, wb=wb, c0=c0):
                                inst = None
                                for kc in range(8):
                                    off = kc * 512 + j * 128
                                    inst = nc.tensor.matmul(banks[bi][:, :], lhsT=wblk[wb][:, off:off + 128],
                                                            rhs=hT[:, kc, c0:c0 + 512],
                                                            start=(kc == 0), stop=(kc == 7))
                                return inst
                            sc.op("pe", mm, reads=wkeys + [("hT", 4 * tc + i) for i in range(4)], writes=[bk(bi)])
                        sc.op("act", lambda sl=sl, bc=bc: nc.scalar.copy(out=csb[sl][:], in_=banks[bc][:, :]),
                              reads=[bk(bc)], writes=[("csb", sl)])
                        sc.op("pool", lambda sl=sl, cb=cb: nc.gpsimd.tensor_copy(out=vbuf[sl][:, 0:2], in_=vhalo[:, cb, :]),
                              reads=[("vhalo", cb)], writes=[("vbufh", sl)])
                        sc.op("dve", lambda sl=sl, bu=bu: nc.vector.tensor_tensor(
                            out=vbuf[sl][:, 2:514], in0=csb[sl][:], in1=banks[bu][:, :], op=ALU.mult),
                            reads=[("csb", sl), bk(bu)], writes=[("vbuf", sl)])
                        sc.op("pool", lambda sl=sl, cb=cb: nc.gpsimd.tensor_copy(out=vhalo[:, cb, :], in_=vbuf[sl][:, 512:514]),
                              reads=[("vbuf", sl)], writes=[("vhalo", cb)])
                        sc.op("dve", lambda sl=sl, cb=cb: nc.vector.tensor_scalar(
                            out=tbuf[sl][:], in0=vbuf[sl][:, 0:512], scalar1=cw[:, 3 * cb:3 * cb + 1], scalar2=None,
                            op0=ALU.mult),
                            reads=[("vbuf", sl), ("vbufh", sl), "cw"], writes=[("tbuf", sl)])
                        sc.op("dve", lambda sl=sl, cb=cb: nc.vector.scalar_tensor_tensor(
                            out=tbuf[sl][:], in0=vbuf[sl][:, 1:513], scalar=cw[:, 3 * cb + 1:3 * cb + 2], in1=tbuf[sl][:],
                            op0=ALU.mult, op1=ALU.add),
                            reads=[("vbuf", sl), ("vbufh", sl), ("tbuf", sl)], writes=[("tbuf", sl)])
                        sc.op("dve", lambda sl=sl, cb=cb: nc.vector.scalar_tensor_tensor(
                            out=tbuf[sl][:], in0=vbuf[sl][:, 2:514], scalar=cw[:, 3 * cb + 2:3 * cb + 3], in1=tbuf[sl][:],
                            op0=ALU.mult, op1=ALU.add),
                            reads=[("vbuf", sl), ("tbuf", sl)], writes=[("tbuf", sl)])
                        sc.op("act", lambda sl=sl, bz=bz: nc.scalar.activation(out=szb[sl][:], in_=banks[bz][:, :], func=AF.Silu),
                              reads=[bk(bz)], writes=[("szb", sl)])
                        sc.op("dve", lambda sl=sl, bb=bb: nc.vector.tensor_tensor(
                            out=tbuf[sl][:], in0=tbuf[sl][:], in1=banks[bb][:, :], op=ALU.mult),
                            reads=[("tbuf", sl), bk(bb)], writes=[("tbuf", sl)])
                        sc.op("dve", lambda sl=sl, cb=cb, c0=c0: nc.vector.tensor_tensor(
                            out=yT[:, cb, c0:c0 + 512], in0=tbuf[sl][:], in1=szb[sl][:], op=ALU.mult),
                            reads=[("tbuf", sl), ("szb", sl)], writes=[("yT", cb, tc)])

                for tt in range(16):
                    s = tt % 2
                    tg = "A%d" % s
                    r0 = tok0 + tt * 128
                    sc.dma("sp", lambda s=s, r0=r0: nc.sync.dma_start(out=xin[s][:], in_=xb[r0:r0 + 128, :]),
                           writes=[(tg, "xin")])
                    for nh in range(2):
                        bi = nextbank()

                        def mm2(bi=bi, tt=tt, nh=nh):
                            inst = None
                            for cb in range(16):
                                inst = nc.tensor.matmul(banks[bi][:, :], lhsT=yT[:, cb, tt * 128:(tt + 1) * 128],
                                                        rhs=woutT[:, cb, nh * 512:(nh + 1) * 512],
                                                        start=(cb == 0), stop=(cb == 15))
                            return inst
                        sc.op("pe", mm2, reads=[("yT", cb, tt // 4) for cb in range(16)] + [("woutT", q) for q in range(8)],
                              writes=[bk(bi)])
                        sc.op("dve", lambda s=s, bi=bi, nh=nh: nc.vector.tensor_tensor(
                            out=x1t[s][:, nh * 512:(nh + 1) * 512], in0=banks[bi][:, :],
                            in1=xin[s][:, nh * 512:(nh + 1) * 512], op=ALU.add),
                            reads=[bk(bi), (tg, "xin")], writes=[("x1t", s, nh)])
                    sc.dma("sp", lambda s=s, r0=r0: nc.sync.dma_start(out=x1d[r0:r0 + 128, :], in_=x1t[s][:]),
                           reads=[("x1t", s, 0), ("x1t", s, 1)], writes=[("x1d", r0 // 128)])
            sc.flush()

    if "B" in stages:
        with ExitStack() as stk:
            def sb(name, shape, dtype):
                return stk.enter_context(nc.sbuf_tensor(name, list(shape), dtype))
            wkv = sb("wkv", [128, 8, 1536], BF16)
            wbin = sb("wbin", [128, 8, 2096], BF16)
            gkvT = sb("gkvT_sb", [128, 8], F32)
            gbT = sb("gbT_sb", [128, 8], F32)
            cosT = sb("cosT_sb", [128, 32, 8], F32)
            sinT = sb("sinT_sb", [128, 32, 8], F32)
            xin = [sb("bxin%d" % i, [128, 1024], F32) for i in range(2)]
            xhat = [sb("bxhat%d" % i, [128, 1024], BF16) for i in range(2)]
            ss = sb("bss", [128, 32], F32)
            sq = sb("bsq", [128, 32], F32)
            rstd = sb("brstd", [128, 32], F32)
            hkT = [sb("hkT%d" % i, [128, 8, 128], BF16) for i in range(2)]
            hqT = [sb("hqT%d" % i, [128, 8, 128], BF16) for i in range(2)]
            kcv = [sb("kcv%d" % i, [128, 512], BF16) for i in range(2)]
            kk = [sb("kk%d" % i, [128, 8, 64], BF16) for i in range(2)]
            rt = [sb("rt%d" % i, [128, 8, 8], F32) for i in range(4)]
            kTst = [sb("kTst%d" % i, [128, 8, 512], BF16) for i in range(2)]
            vst = [sb("vst%d" % i, [128, 4, 8, 65], BF16) for i in range(2)]
            qr = [sb("qr%d" % i, [128, 16, 64], BF16) for i in range(2)]
            qc = [sb("qc%d" % i, [128, 16, 64], BF16) for i in range(2)]
            qTst = [sb("qTst%d" % i, [128, 16, 512], BF16) for i in range(2)]
            szst = [sb("szst%d" % i, [128, 1024], F32) for i in range(2)]
            gst = [sb("gst%d" % i, [128, 48], F32) for i in range(2)]

            for q in range(4):
                sc.dma("pool", lambda q=q: nc.gpsimd.dma_start(
                    out=wkv[:, 2 * q:2 * q + 2, :],
                    in_=wkv_r[:, 3072 * q:3072 * (q + 1)].rearrange("p (a n) -> p a n", a=2)),
                    writes=[("wkv", q)])
            for q in range(8):
                sc.dma("pool", lambda q=q: nc.gpsimd.dma_start(
                    out=wbin[:, q, :].rearrange("p (a n) -> p a n", a=2),
                    in_=wbin_r[:, 2096 * q:2096 * (q + 1)].rearrange("p (a n) -> p a n", a=2)), writes=[("wbin", q)])
            wkv_keys = [("wkv", q) for q in range(4)]
            wbin_keys = [("wbin", q) for q in range(8)]
            sc.dma("sp", lambda: nc.sync.dma_start(out=gkvT[:], in_=gkvT_d[:, :]), writes=["gkvT"])
            sc.dma("sp", lambda: nc.sync.dma_start(out=gbT[:], in_=gbT_d[:, :]), writes=["gbT"])
            sc.dma("sp", lambda: nc.sync.dma_start(out=cosT[:], in_=cos_d[:, :].rearrange("p (t f) -> p t f", f=8)),
                   writes=["cosT"])
            sc.dma("sp", lambda: nc.sync.dma_start(out=sinT[:], in_=sin_d[:, :].rearrange("p (t f) -> p t f", f=8)),
                   writes=["sinT"])
            for i in range(2):
                sc.op("pool", lambda i=i: nc.gpsimd.memset(vst[i][:], 1.0), writes=[("vst", i)])

            def rope(src_bank, col0, nh, dst, tt, tagk, wkey):
                psv = banks[src_bank][:, col0:col0 + nh * 64].rearrange("p (h d) -> p h d", d=64)
                cs = cosT[:, tt:tt + 1, :].broadcast_to([128, nh, 8])
                sn = sinT[:, tt:tt + 1, :].broadcast_to([128, nh, 8])
                rd = [bk(src_bank), "cosT", "sinT"]
                for t_i, (lo, tab) in enumerate(((0, cs), (8, sn), (8, cs), (0, sn))):
                    sc.op("dve", lambda t_i=t_i, lo=lo, tab=tab: nc.vector.tensor_tensor(
                        out=rt[t_i][:, 0:nh, :], in0=psv[:, :, lo:lo + 8], in1=tab, op=ALU.mult),
                        reads=rd, writes=[("rt", t_i)])
                if _DBG.get("r1"):
                    return []
                sc.op("dve", lambda: nc.vector.tensor_tensor(
                    out=dst[:, 0:nh, 0:8], in0=rt[0][:, 0:nh, :], in1=rt[1][:, 0:nh, :], op=ALU.subtract),
                    reads=[("rt", 0), ("rt", 1)], writes=[wkey + ("a",)])
                sc.op("dve", lambda: nc.vector.tensor_tensor(
                    out=dst[:, 0:nh, 8:16], in0=rt[2][:, 0:nh, :], in1=rt[3][:, 0:nh, :], op=ALU.add),
                    reads=[("rt", 2), ("rt", 3)], writes=[wkey + ("b",)])
                if _DBG.get("r2"):
                    return []
                sc.op("dve", lambda: nc.vector.tensor_copy(out=dst[:, 0:nh, 16:64], in_=psv[:, :, 16:64]),
                      reads=[bk(src_bank)], writes=[wkey + ("c",)])
                return [wkey + ("a",), wkey + ("b",), wkey + ("c",)]

            for ch in range(_DBG.get('b_nch', 8)):
                own = ch < 4 and not _DBG.get('b_noq')
                cs_ = ch % 2
                for i in range(4):
                    tt = 4 * ch + i
                    s = tt % 2
                    tg = "B%d" % s
                    r0 = tt * 128
                    sc.dma("sp", lambda s=s, r0=r0: nc.sync.dma_start(out=xin[s][:], in_=x1d[r0:r0 + 128, :]),
                           reads=[("x1d", tt)], writes=[(tg, "xin")])
                    rms_front(stk, tg, xin[s], xhat[s], ss, sq, rstd, tt)
                    bi = nextbank()

                    def tr(s=s, bi=bi):
                        pst = bfview(bi)
                        inst = None
                        for kc in range(8):
                            inst = nc.tensor.transpose(out=pst[:, kc * 128:(kc + 1) * 128],
                                                       in_=xhat[s][:, kc * 128:(kc + 1) * 128], identity=ident[:])
                        return inst
                    sc.op("pe", tr, reads=[(tg, "xhat"), "ident"], writes=[bk(bi)])
                    sc.op("dve", lambda s=s, bi=bi: nc.vector.tensor_tensor(
                        out=hkT[s][:, :, :], in0=bfview(bi)[:, :].rearrange("p (k t) -> p k t", k=8),
                        in1=gkvT[:, :, None].broadcast_to([128, 8, 128]), op=ALU.mult),
                        reads=[bk(bi), "gkvT"], writes=[("hkT", s)])
                    if own:
                        sc.op("dve", lambda s=s, bi=bi: nc.vector.tensor_tensor(
                            out=hqT[s][:, :, :], in0=bfview(bi)[:, :].rearrange("p (k t) -> p k t", k=8),
                            in1=gbT[:, :, None].broadcast_to([128, 8, 128]), op=ALU.mult),
                            reads=[bk(bi), "gbT"], writes=[("hqT", s)])

                    def proj(bi, act, wt, c0, ncols):
                        def f():
                            inst = None
                            for kc in range(8):
                                inst = nc.tensor.matmul(banks[bi][:, 0:ncols], lhsT=act[:, kc, :],
                                                        rhs=wt[:, kc, c0:c0 + ncols], start=(kc == 0), stop=(kc == 7))
                            return inst
                        return f
                    b0 = nextbank()
                    sc.op("pe", proj(b0, hkT[s], wkv, 0, 512), reads=[("hkT", s)] + wkv_keys, writes=[bk(b0)])
                    b1 = nextbank()
                    sc.op("pe", proj(b1, hkT[s], wkv, 512, 512), reads=[("hkT", s)] + wkv_keys, writes=[bk(b1)])
                    b2 = nextbank()
                    sc.op("pe", proj(b2, hkT[s], wkv, 1024, 512), reads=[("hkT", s)] + wkv_keys, writes=[bk(b2)])
                    if _DBG.get("b_stop1"):
                        continue
                    sc.op("act", lambda s=s, b0=b0: nc.scalar.copy(out=kcv[s][:], in_=banks[b0][:, :]),
                          reads=[bk(b0)], writes=[("kcv", s)])
                    if _DBG.get("b_stop2"):
                        continue
                    kkeys = rope(b1, 0, 8, kk[s], tt, "kk", ("kk", s))
                    if _DBG.get("b_stop3"):
                        continue
                    sc.op("act", lambda s=s, b2=b2, i=i, cs_=cs_: nc.scalar.copy(
                        out=vst[cs_][:, i, :, 0:64], in_=banks[b2][:, :].rearrange("p (a d) -> p a d", d=64)),
                        reads=[bk(b2)], writes=[("vst", cs_)])
                    if _DBG.get("b_stop4"):
                        continue
                    for half, src, skeys in ((0, kcv[s], [("kcv", s)]), (1, kk[s], kkeys)):
                        bt = nextbank()

                        def trk(bt=bt, src=src, half=half):
                            pst = bfview(bt)
                            sv = src[:, :] if half == 0 else src[:, :, :].rearrange("p a d -> p (a d)")
                            inst = None
                            for a in range(4):
                                inst = nc.tensor.transpose(out=pst[:, a * 128:(a + 1) * 128],
                                                           in_=sv[:, a * 128:(a + 1) * 128], identity=ident[:])
                            return inst
                        sc.op("pe", trk, reads=skeys + ["ident"], writes=[bk(bt)])
                        sc.op("act" if half == 0 else "dve", (lambda bt=bt, half=half, i=i, cs_=cs_: (
                            nc.scalar.copy if half == 0 else nc.vector.tensor_copy)(
                            out=kTst[cs_][:, 4 * half:4 * half + 4, i * 128:(i + 1) * 128],
                            in_=bfview(bt)[:, 0:512].rearrange("p (a t) -> p a t", a=4))),
                            reads=[bk(bt)], writes=[("kTst", cs_, half, i)])
                    if own:
                        for qh in range(2):
                            bq = nextbank()
                            sc.op("pe", proj(bq, hqT[s], wbin, 512 * qh, 512), reads=[("hqT", s)] + wbin_keys, writes=[bk(bq)])
                            if not _DBG.get("q_noqc"):
                                sc.op("act", lambda s=s, bq=bq, qh=qh: nc.scalar.copy(
                                    out=qc[s][:, 8 * qh:8 * qh + 8, :], in_=banks[bq][:, :].rearrange("p (h d) -> p h d", d=64)),
                                    reads=[bk(bq)], writes=[("qc", s, qh)])
                            if not _DBG.get("q_norope"):
                                rope(bq, 0, 8, qr[s][:, 8 * qh:8 * qh + 8, :], tt, "q", ("qr", s, qh))
                        for zh in range(0 if not _DBG.get("q_noz") else 2, 2):
                            bz = nextbank()
                            sc.op("pe", proj(bz, hqT[s], wbin, 1024 + 512 * zh, 512), reads=[("hqT", s)] + wbin_keys, writes=[bk(bz)])
                            sc.op("act", lambda s=s, bz=bz, zh=zh: nc.scalar.activation(
                                out=szst[s][:, 512 * zh:512 * (zh + 1)], in_=banks[bz][:, :], func=AF.Silu),
                                reads=[bk(bz)], writes=[("szst", s, zh)])
                        if _DBG.get("q_stopg"):
                            continue
                        bg = nextbank()
                        sc.op("pe", proj(bg, hqT[s], wbin, 2048, 48), reads=[("hqT", s)] + wbin_keys, writes=[bk(bg)])
                        sc.op("act", lambda s=s, bg=bg: nc.scalar.activation(out=gst[s][:], in_=banks[bg][:, 0:48], func=AF.Sigmoid),
                              reads=[bk(bg)], writes=[("gst", s)])
                        sc.dma("sp", lambda s=s, r0=r0: nc.sync.dma_start(out=szd[r0:r0 + 128, :], in_=szst[s][:]),
                               reads=[("szst", s, 0), ("szst", s, 1)], writes=[("szd", tt)])
                        sc.dma("sp", lambda s=s, r0=r0: nc.sync.dma_start(out=gated[r0:r0 + 128, :], in_=gst[s][:]),
                               reads=[("gst", s)], writes=[("gated", tt)])
                        if _DBG.get("q_stoptr"):
                            continue
                        for which, src, skeys in ((0, qc[s], [("qc", s, 0), ("qc", s, 1)]),
                                                  (1, qr[s], [("qr", s, qh, x) for qh in range(2) for x in "abc"])):
                            bt = nextbank()

                            def trq(bt=bt, src=src):
                                pst = bfview(bt)
                                sv = src[:, :, :].rearrange("p h d -> p (h d)")
                                inst = None
                                for a in range(8):
                                    inst = nc.tensor.transpose(out=pst[:, a * 128:(a + 1) * 128],
                                                               in_=sv[:, a * 128:(a + 1) * 128], identity=ident[:])
                                return inst
                            sc.op("pe", trq, reads=skeys + ["ident"], writes=[bk(bt)])
                            a0 = 8 * which
                            sc.op("dve" if which == 0 else "act", (lambda bt=bt, which=which, a0=a0, i=i, cs_=cs_: (
                                nc.vector.tensor_copy if which == 0 else nc.scalar.copy)(
                                out=qTst[cs_][:, a0:a0 + 8, i * 128:(i + 1) * 128],
                                in_=bfview(bt)[:, :].rearrange("p (a t) -> p a t", a=8))),
                                reads=[bk(bt)], writes=[("qTst", cs_, which, i)])
                c0 = ch * 512
                if _DBG.get('b_nostore'):
                    continue
                sc.dma("sp", lambda cs_=cs_, c0=c0: nc.sync.dma_start(
                    out=kT4[:, :, c0:c0 + 512].rearrange("(a gl) d t -> (gl d) a t", gl=2), in_=kTst[cs_][:, :, :]),
                    reads=[("kTst", cs_, h_, i_) for h_ in range(2) for i_ in range(4)], writes=[("kT4", ch)])
                sc.dma("sp", lambda cs_=cs_, c0=c0: nc.sync.dma_start(
                    out=vtok[c0:c0 + 512, :].rearrange("(i p) f -> p i f", p=128),
                    in_=vst[cs_][:, :, :, :].rearrange("p i a d -> p i (a d)")),
                    reads=[("vst", cs_)], writes=[("vtok", ch)])
                if own:
                    sc.dma("sp", lambda cs_=cs_, c0=c0: nc.sync.dma_start(
                        out=qT[:, :, c0:c0 + 512].rearrange("(a hl) d t -> (hl d) a t", hl=2), in_=qTst[cs_][:, :, :]),
                        reads=[("qTst", cs_, w_, i_) for w_ in range(2) for i_ in range(4)],
                        writes=[("qT", ch)])
            sc.flush()

    if "C" in stages:
        with ExitStack() as stk:
            def sb(name, shape, dtype):
                return stk.enter_context(nc.sbuf_tensor(name, list(shape), dtype))
            kcmpT = sb("kcmpT", [128, 4, 256], BF16)
            vcmp = sb("vcmp", [128, 2, 4, 64], BF16)
            AI = sb("AI", [128, 2, 64], BF16)
            sc.dma("pool", lambda: nc.gpsimd.dma_start(out=AI[:], in_=Aaug_d[:, :].rearrange("p (a c) -> p a c", a=2)[:, :, 0:64]),
                   writes=["AI"])
            all_kT4 = [("kT4", ch) for ch in range(8)]

            with ExitStack() as stk2:
                def sb2(name, shape, dtype):
                    return stk2.enter_context(nc.sbuf_tensor(name, list(shape), dtype))
                kcT = sb2("kcT", [128, 4, 4128], BF16)
                w1 = sb2("w1", [128, 16, 256], BF16)
                w2 = sb2("w2", [128, 2, 128], BF16)
                posT = sb2("posT", [128, 16], BF16)
                hidT = sb2("hidT", [128, 2, 4, 256], BF16)
                pb = sb2("pb", [128, 2], F32)
                ub = sb2("ub", [128, 512], F32)
                tb = sb2("tb", [128, 512], F32)
                sg = sb2("sg", [128, 512], F32)
                for st in range(2):
                    for g in range(4):
                        sc.dma("sp", lambda st=st, g=g: nc.sync.dma_start(out=kcT[0:64, g, 0:4096], in_=kT4[4 * st + g, :, :]),
                               reads=all_kT4, writes=[("kcT", g)])
                        sc.dma("sp", lambda st=st, g=g: nc.sync.dma_start(out=kcT[0:64, g, 4096:4128], in_=kT4[4 * st + g, :, 0:32]),
                               reads=all_kT4, writes=[("kcTw", g)])
                        sc.dma("sp", lambda st=st, g=g: nc.sync.dma_start(out=kcT[64:128, g, 0:4095], in_=kT4[4 * st + g, :, 1:4096]),
                               reads=all_kT4, writes=[("kcT2", g)])
                        sc.dma("sp", lambda st=st, g=g: nc.sync.dma_start(out=kcT[64:128, g, 4095:4127], in_=kT4[4 * st + g, :, 0:32]),
                               reads=all_kT4, writes=[("kcT2w", g)])
                    w1d = w1k_d if st == 0 else w1v_d
                    w2d = w2k_d if st == 0 else w2v_d
                    posd = posk_d if st == 0 else posv_d
                    for q in range(4):
                        sc.dma("pool", lambda q=q, w1d=w1d: nc.gpsimd.dma_start(
                            out=w1[:, 4 * q:4 * q + 4, :], in_=w1d[:, 1024 * q:1024 * (q + 1)].rearrange("p (l n) -> p l n", n=256)),
                            writes=[("w1", q)])
                    sc.dma("pool", lambda w2d=w2d: nc.gpsimd.dma_start(out=w2[:], in_=w2d[:, :].rearrange("p (a n) -> p a n", a=2)),
                           writes=["w2"])
                    sc.dma("pool", lambda posd=posd: nc.gpsimd.dma_start(out=posT[:], in_=posd[:, :]), writes=["posT"])
                    w1keys = [("w1", q) for q in range(4)]
                    kckeys = [(nm, g) for g in range(4) for nm in ("kcT", "kcTw", "kcT2", "kcT2w")]
                    for hc in range(2):
                        bi = nextbank()

                        def mpb(bi=bi, hc=hc):
                            inst = None
                            for l in range(16):
                                inst = nc.tensor.matmul(banks[bi][:, 0:1], lhsT=w1[:, l, hc * 128:(hc + 1) * 128],
                                                        rhs=posT[:, l:l + 1], start=(l == 0), stop=(l == 15))
                            return inst
                        sc.op("pe", mpb, reads=w1keys + ["posT"], writes=[bk(bi)])
                        sc.op("act", lambda bi=bi, hc=hc: nc.scalar.copy(out=pb[:, hc:hc + 1], in_=banks[bi][:, 0:1]),
                              reads=[bk(bi)], writes=[("pb", hc)])
                        for gp in range(2):
                            bh = nextbank()

                            def mh(bh=bh, hc=hc, gp=gp):
                                inst = None
                                for l in range(16):
                                    inst = nc.tensor.matmul(banks[bh][:, :], lhsT=w1[:, l, hc * 128:(hc + 1) * 128],
                                                            rhs=kcT[:, 2 * gp:2 * gp + 2, 2 * l:2 * l + 4096:16],
                                                            start=(l == 0), stop=(l == 15))
                                return inst
                            sc.op("pe", mh, reads=w1keys + kckeys, writes=[bk(bh)])
                            sc.op("act", lambda bh=bh, hc=hc: nc.scalar.activation(
                                out=ub[:], in_=banks[bh][:, :], func=AF.Identity, bias=pb[:, hc:hc + 1]),
                                reads=[bk(bh), ("pb", hc)], writes=["ub"])
                            sc.op("dve", lambda: nc.vector.tensor_tensor(out=tb[:], in0=ub[:], in1=ub[:], op=ALU.mult),
                                  reads=["ub"], writes=["tb"])
                            sc.op("dve", lambda: nc.vector.tensor_scalar(out=tb[:], in0=tb[:], scalar1=0.044715, scalar2=1.0,
                                                                         op0=ALU.mult, op1=ALU.add),
                                  reads=["tb"], writes=["tb"])
                            sc.op("dve", lambda: nc.vector.tensor_tensor(out=tb[:], in0=tb[:], in1=ub[:], op=ALU.mult),
                                  reads=["tb", "ub"], writes=["tb"])
                            sc.op("act", lambda: nc.scalar.activation(out=sg[:], in_=tb[:], func=AF.Sigmoid, scale=1.5957691216057308),
                                  reads=["tb"], writes=["sg"])
                            sc.op("dve", lambda hc=hc, gp=gp: nc.vector.tensor_tensor(
                                out=hidT[:, hc, 2 * gp:2 * gp + 2, :], in0=ub[:, :].rearrange("p (a m) -> p a m", a=2),
                                in1=sg[:, :].rearrange("p (a m) -> p a m", a=2), op=ALU.mult),
                                reads=["ub", "sg"], writes=[("hidT", hc, gp)])
                    hkeys = [("hidT", hc, gp) for hc in range(2) for gp in range(2)]
                    if st == 0:
                        for gp in range(2):
                            bo = nextbank()

                            def mk(bo=bo, gp=gp):
                                inst = None
                                for hc in range(2):
                                    inst = nc.tensor.matmul(banks[bo][:, :], lhsT=w2[:, hc, :],
                                                            rhs=hidT[:, hc, 2 * gp:2 * gp + 2, :], start=(hc == 0), stop=(hc == 1))
                                return inst
                            sc.op("pe", mk, reads=hkeys + ["w2"], writes=[bk(bo)])
                            sc.op("act", lambda bo=bo, gp=gp: nc.scalar.copy(
                                out=kcmpT[:, 2 * gp:2 * gp + 2, :], in_=banks[bo][:, :].rearrange("p (a m) -> p a m", a=2)),
                                reads=[bk(bo)], writes=[("kcmpT", gp)])
                    else:
                        for nt in range(2):
                            bo = nextbank()

                            def mv(bo=bo, nt=nt):
                                inst = None
                                for g in range(4):
                                    for hc in range(2):
                                        inst = nc.tensor.matmul(banks[bo][:, g * 64:(g + 1) * 64],
                                                                lhsT=hidT[:, hc, g, nt * 128:(nt + 1) * 128],
                                                                rhs=w2[:, hc, 0:64], start=(g == 0 and hc == 0), stop=(hc == 1),
                                                                skip_group_check=True)
                                return inst
                            sc.op("pe", mv, reads=hkeys + ["w2"], writes=[bk(bo)])
                            sc.op("act", lambda bo=bo, nt=nt: nc.scalar.copy(
                                out=vcmp[:, nt, :, :], in_=banks[bo][:, 0:256].rearrange("p (g d) -> p g d", g=4)),
                                reads=[bk(bo)], writes=[("vcmp", nt)])
                sc.flush()

            Ksel = sb("Ksel", [128, 4, 4096], BF16)
            KwT = sb("KwT", [128, 4, 4096], BF16)
            Vall = sb("Vall", [128, 32, 520], BF16)
            cmask = sb("cmask_sb", [128, 2, 2048], BF16)
            fmul = sb("fmul_sb", [128, 16, 64], F32)
            fadd = sb("fadd_sb", [128, 16, 64], F32)
            tri = sb("tri_sb", [128, 128], BF16)
            upp = sb("upp_sb", [128, 128], BF16)
            woutb = sb("woutb_sb", [128, 8, 1024], BF16)
            gfin = sb("gfin_sb", [128, 1024], F32)
            Qsel = [sb("Qsel%d" % i, [128, 16, 128], BF16) for i in range(2)]
            Qwin = [sb("Qwin%d" % i, [128, 16, 128], BF16) for i in range(2)]
            Qc = [sb("Qc%d" % i, [128, 16, 128], BF16) for i in range(2)]
            gts = [sb("gts%d" % i, [128, 16, 3], F32) for i in range(2)]
            szs = sb("szs", [128, 1024], F32)
            x1s = sb("x1s", [128, 1024], F32)
            oacc2 = [sb("oacc%d" % i, [128, 16, 64], F32) for i in range(2)]
            otmp = [sb("otmp%d" % i, [128, 4, 64], F32) for i in range(2)]
            NPT = 4
            PTb = [sb("PTb%d" % i, [128, 512], BF16) for i in range(NPT)]
            rs = [sb("rs%d" % i, [128, 4], F32) for i in range(3)]
            rc = [sb("rc%d" % i, [128, 4], F32) for i in range(3)]
            wg = [sb("wg%d" % i, [128, 4], F32) for i in range(3)]
            impb = sb("impb", [128, 64], F32)
            scr = sb("scr", [128, 64], F32)
            scr2 = sb("scr2", [128, 64], F32)
            m8a = sb("m8a", [128, 8], F32)
            m8b = sb("m8b", [128, 8], F32)
            thr = sb("thr", [128, 1], F32)
            negb4 = [sb("negb%d" % i, [128, 128], BF16) for i in range(4)]
            ybf = sb("ybf", [128, 1024], BF16)
            yT = sb("cyT", [128, 8, 128], BF16)
            x2 = sb("x2", [128, 1024], F32)
            fss = sb("fss", [128, 16], F32)
            fsq = sb("fsq", [128, 16], F32)
            frs = sb("frs", [128, 16], F32)
            junk = sb("junk", [128, 1024], BF16)
            osb = sb("osb", [128, 1024], F32)
            identF = sb("identF", [128, 128], F32)
            accT = [sb("accT%d" % i, [128, 512], F32) for i in range(2)]
            sc.dma("sp", lambda: nc.sync.dma_start(out=identF[:], in_=ident_d[:, :]), writes=["identF"])
            for i in range(2):
                sc.op("pool", lambda i=i: nc.gpsimd.memset(accT[i][:], 0.0), writes=[("accT", i)])

            for g in range(4):
                sc.dma("sp", lambda g=g: nc.sync.dma_start(out=Ksel[0:64, g, :], in_=kT4[8 + g, :, :]),
                       reads=all_kT4, writes=[("Ksel", g)])
                sc.op("pool", lambda g=g: nc.gpsimd.memset(KwT[64:128, g, :], 0.0), writes=[("KwTz", g)])
                sc.dma("sp", lambda g=g: nc.sync.dma_start(out=KwT[0:64, g, :], in_=kT4[12 + g, :, :]),
                       reads=all_kT4, writes=[("KwT", g)])
                for hh in range(2):
                    sc.dma("pool", lambda g=g, hh=hh: nc.gpsimd.dma_start(
                        out=Ksel[64:128, g, 2048 * hh:2048 * (hh + 1)], in_=E_d[:, 2048 * hh:2048 * (hh + 1)]),
                        writes=[("KselE", g, hh)])
                    sc.dma("pool", lambda g=g, hh=hh: nc.gpsimd.dma_start(
                        out=KwT[64:65, g, 2048 * hh:2048 * (hh + 1)], in_=kbias_d[:, 2048 * hh:2048 * (hh + 1)]),
                        reads=[("KwTz", g)], writes=[("KwTb", g, hh)])
            for ch in range(8):
                sc.dma("sp", lambda ch=ch: nc.sync.dma_start(
                    out=Vall[:, 4 * ch:4 * ch + 4, :], in_=vtok[512 * ch:512 * (ch + 1), :].rearrange("(i p) f -> p i f", p=128)),
                    reads=[("vtok", ch)], writes=[("Vall", ch)])
            for hh in range(2):
                sc.dma("pool", lambda hh=hh: nc.gpsimd.dma_start(out=cmask[:, hh, :], in_=cmask_d[:, 2048 * hh:2048 * (hh + 1)]),
                       writes=[("cmask", hh)])
            sc.dma("sp", lambda: nc.sync.dma_start(out=fmul[:], in_=fmul_d[:, :].rearrange("p (t j) -> p t j", j=64)), writes=["fmul"])
            sc.dma("sp", lambda: nc.sync.dma_start(out=fadd[:], in_=fadd_d[:, :].rearrange("p (t j) -> p t j", j=64)), writes=["fadd"])
            sc.dma("pool", lambda: nc.gpsimd.dma_start(out=tri[:], in_=tri_d[:, :]), writes=["tri"])
            sc.dma("pool", lambda: nc.gpsimd.dma_start(out=upp[:], in_=upp_d[:, :]), writes=["upp"])
            for q in range(8):
                sc.dma("pool", lambda q=q: nc.gpsimd.dma_start(out=woutb[:, q, :], in_=woutb_d[:, 1024 * q:1024 * (q + 1)]),
                       writes=[("woutb", q)])
            sc.dma("sp", lambda: nc.sync.dma_start(out=gfin[:], in_=gfin_d[:, :]), writes=["gfin"])
            for i in range(2):
                sc.op("pool", lambda i=i: nc.gpsimd.memset(Qwin[i][64:128, :, :], 0.0), writes=[("Qwin1", i)])
                sc.op("pool", lambda i=i: nc.gpsimd.memset(Qwin[i][64:65, :, :], 1.0), reads=[("Qwin1", i)], writes=[("Qwin1", i)])
                sc.op("pool", lambda i=i: nc.gpsimd.memset(Qc[i][64:128, :, :], 0.0), writes=[("Qc0", i)])
            for i in range(4):
                sc.op("pool", lambda i=i: nc.gpsimd.memset(negb4[i][:, 0:64], 0.0), writes=[("negb0", i)])
            ksel_keys = lambda g: [("Ksel", g), ("KselE", g, 0), ("KselE", g, 1)]
            kw_keys = lambda g: [("KwT", g), ("KwTz", g), ("KwTb", g, 0), ("KwTb", g, 1)]
            sbank = [0]
            ptc = [0]

            def emit_score(u):
                bS = sbank[0] % 3
                sbank[0] += 1
                pi = ptc[0] % NPT
                ptc[0] += 1
                lhsT, rhs, rkeys, mask, mkeys = u["score"]
                sc.op("pe", lambda: nc.tensor.matmul(banks[bS][:, :], lhsT=lhsT, rhs=rhs, start=True, stop=True),
                      reads=rkeys, writes=[bk(bS)])
                sc.op("act", lambda: nc.scalar.activation(out=PTb[pi][:], in_=banks[bS][:, :], func=AF.Exp, scale=0.125),
                      reads=[bk(bS)], writes=[("PT", pi)])
                if mask is not None:
                    sc.op("dve", lambda: nc.vector.tensor_tensor(
                        out=PTb[pi][:, :].rearrange("p (h q) -> p h q", h=4),
                        in0=PTb[pi][:, :].rearrange("p (h q) -> p h q", h=4),
                        in1=mask, op=ALU.mult), reads=[("PT", pi)] + list(mkeys), writes=[("PT", pi)])
                return pi

            def emit_pv(u, pi):
                for pv_ in u["pvs"]:
                    if pv_[0] == "vstat":
                        _, accb, lhsT_, rkeys, first, last = pv_
                        sc.op("pe", lambda accb=accb, lhsT_=lhsT_, first=first, last=last: nc.tensor.matmul(
                            banks[accb][0:65, :], lhsT=lhsT_, rhs=PTb[pi][:, :], start=first, stop=last),
                            reads=[("PT", pi)] + rkeys, writes=[bk(accb)])
                        continue
                    (accb, col0, width, rhs_, rkeys, first, last) = pv_

                    def f(accb=accb, col0=col0, width=width, rhs_=rhs_, first=first, last=last):
                        inst = None
                        for h in range(4):
                            inst = nc.tensor.matmul(banks[accb][:, col0 + h * width:col0 + (h + 1) * width],
                                                    lhsT=PTb[pi][:, h * 128:(h + 1) * 128], rhs=rhs_,
                                                    start=(first and h == 0), stop=last, skip_group_check=True)
                        return inst
                    sc.op("pe", f, reads=[("PT", pi)] + rkeys, writes=[bk(accb)])

            def untranspose(accb, ai, then):
                sc.op("dve", lambda: nc.vector.tensor_copy(out=accT[ai][0:65, :], in_=banks[accb][0:65, :]),
                      reads=[bk(accb), ("accT", ai)], writes=[("accT", ai)])

                def tail():
                    def tr4():
                        inst = None
                        for h in range(4):
                            inst = nc.tensor.transpose(out=banks[accb][:, h * 128:(h + 1) * 128],
                                                       in_=accT[ai][:, h * 128:(h + 1) * 128], identity=identF[:])
                        return inst
                    sc.op("pe", tr4, reads=[("accT", ai), "identF"], writes=[bk(accb)])
                    then()
                deferred.append([_DBG.get("dunt", 4), tail])

            def branch_out(accb, g, gidx, gt, slot, stride=65, ob=0):
                oacc = oacc2[ob]
                av = banks[accb][:, 0:4 * stride].rearrange("p (h c) -> p h c", c=stride)
                r_, c_, w_ = rs[slot], rc[slot], wg[slot]
                sc.op("dve", lambda: nc.vector.tensor_scalar(out=r_[:], in0=av[:, :, 64], scalar1=1e-30, scalar2=None,
                                                             op0=ALU.max), reads=[bk(accb)], writes=[("rs", slot)])
                sc.op("dve", lambda: nc.vector.reciprocal(out=c_[:], in_=r_[:]), reads=[("rs", slot)], writes=[("rc", slot)])
                sc.op("dve", lambda: nc.vector.tensor_tensor(out=w_[:], in0=c_[:], in1=gt[:, 4 * g:4 * g + 4, gidx], op=ALU.mult),
                      reads=[("rc", slot), "gts"], writes=[("wg", slot)])
                ot = otmp[slot - 1]
                sc.op("dve", lambda: nc.vector.tensor_tensor(
                    out=ot[:], in0=av[:, :, 0:64], in1=w_[:, :, None].broadcast_to([128, 4, 64]), op=ALU.mult),
                    reads=[bk(accb), ("wg", slot)], writes=[("otmp", slot)])
                sc.op("pool", lambda: nc.gpsimd.tensor_tensor(
                    out=oacc[:, 4 * g:4 * g + 4, :], in0=oacc[:, 4 * g:4 * g + 4, :], in1=ot[:], op=ALU.add),
                    reads=[("otmp", slot), ("oacc", ob, g)], writes=[("oacc", ob, g)])

            def select_chain(qs, g, qb):
                gt = gts[qb]
                oacc = oacc2[qb]
                negb = negb4[g]
                iv = banks[3][:, 0:256].rearrange("p (h c) -> p h c", c=64)
                r_, c_, w_ = rs[0], rc[0], wg[0]
                sc.op("dve", lambda: nc.vector.reduce_sum(out=r_[:], in_=iv, axis=mybir.AxisListType.X),
                      reads=[bk(3)], writes=[("rs", 0)])
                sc.op("dve", lambda: nc.vector.tensor_scalar(out=r_[:], in0=r_[:], scalar1=0.5, scalar2=1e-30,
                                                             op0=ALU.mult, op1=ALU.max), reads=[("rs", 0)], writes=[("rs", 0)])
                sc.op("dve", lambda: nc.vector.reciprocal(out=c_[:], in_=r_[:]), reads=[("rs", 0)], writes=[("rc", 0)])
                sc.op("dve", lambda: nc.vector.tensor_scalar(out=impb[:], in0=iv[:, 0, 0:64], scalar1=c_[:, 0:1],
                                                             scalar2=None, op0=ALU.mult),
                      reads=[bk(3), ("rc", 0)], writes=["impb"])
                for h in range(1, 4):
                    sc.op("dve", lambda h=h: nc.vector.scalar_tensor_tensor(
                        out=impb[:], in0=iv[:, h, 0:64], scalar=c_[:, h:h + 1], in1=impb[:], op0=ALU.mult, op1=ALU.add),
                        reads=[bk(3), ("rc", 0), "impb"], writes=["impb"])
                sc.op("dve", lambda: nc.vector.tensor_tensor(out=w_[:], in0=c_[:], in1=gt[:, 4 * g:4 * g + 4, 0], op=ALU.mult),
                      reads=[("rc", 0), "gts"], writes=[("wg", 0)])
                sc.op("dve", lambda: nc.vector.tensor_tensor(
                    out=oacc[:, 4 * g:4 * g + 4, :], in0=banks[3][:, 256:512].rearrange("p (h c) -> p h c", c=64),
                    in1=w_[:, :, None].broadcast_to([128, 4, 64]), op=ALU.mult),
                    reads=[bk(3), ("wg", 0)], writes=[("oacc", qb, g)])
                sc.op("dve", lambda: nc.vector.tensor_tensor(out=scr[:], in0=impb[:], in1=fmul[:, qs, :], op=ALU.mult),
                      reads=["impb", "fmul"], writes=["scr"])
                sc.op("dve", lambda: nc.vector.tensor_tensor(out=scr[:], in0=scr[:], in1=fadd[:, qs, :], op=ALU.add),
                      reads=["scr", "fadd"], writes=["scr"])
                sc.op("dve", lambda: nc.vector.max(out=m8a[:], in_=scr[:]), reads=["scr"], writes=["m8a"])
                sc.op("dve", lambda: nc.vector.match_replace(out=scr2[:], in_to_replace=m8a[:], in_values=scr[:], imm_value=-2e9),
                      reads=["scr", "m8a"], writes=["scr2"])
                sc.op("dve", lambda: nc.vector.max(out=m8b[:], in_=scr2[:]), reads=["scr2"], writes=["m8b"])
                sc.op("dve", lambda: nc.vector.tensor_scalar(out=thr[:], in0=m8b[:, 7:8], scalar1=-1e8, scalar2=None, op0=ALU.max),
                      reads=["m8b"], writes=["thr"])
                sc.op("dve", lambda: nc.vector.tensor_scalar(out=negb[:, 64:128], in0=scr[:], scalar1=thr[:, 0:1], scalar2=NEG,
                                                             op0=ALU.is_lt, op1=ALU.mult),
                      reads=["scr", "thr"], writes=[("negb", g)])

                def tail():
                    sc.op("pe", lambda: nc.tensor.transpose(out=bfview(7)[:, 0:128], in_=negb[:, :], identity=ident[:]),
                          reads=[("negb", g), ("negb0", g), "ident"], writes=[bk(7)])
                    sc.op("dve", lambda: nc.vector.tensor_copy(
                        out=Qsel[qb][64:128, 4 * g:4 * g + 4, :], in_=bfview(7)[64:128, None, 0:128].broadcast_to([64, 4, 128])),
                        reads=[bk(7)], writes=[("QselB", qb, g)])
                deferred.append([_DBG.get("dsel", 5), tail])

            def load_q(qs):
                qb = qs % 2
                q0 = qs * 128
                sc.dma("sp", lambda: nc.sync.dma_start(
                    out=Qsel[qb][0:64, :, :], in_=qT[16:32, :, q0:q0 + 128].rearrange("a d t -> d a t")),
                    reads=[("qT", qs // 4)], writes=[("QselQ", qb)])
                sc.dma("sp", lambda: nc.sync.dma_start(
                    out=Qwin[qb][0:64, :, :], in_=qT[16:32, :, q0:q0 + 128].rearrange("a d t -> d a t")),
                    reads=[("qT", qs // 4)], writes=[("QwinQ", qb)])
                sc.dma("sp", lambda: nc.sync.dma_start(
                    out=Qc[qb][0:64, :, :], in_=qT[0:16, :, q0:q0 + 128].rearrange("a d t -> d a t")),
                    reads=[("qT", qs // 4)], writes=[("Qc", qb)])
                sc.dma("sp", lambda: nc.sync.dma_start(
                    out=gts[qb][:, :, :], in_=gated[q0:q0 + 128, :].rearrange("p (h c) -> p h c", c=3)),
                    reads=[("gated", qs)], writes=["gts"])

            def load_tail(qs):
                q0 = qs * 128
                sc.dma("sp", lambda: nc.sync.dma_start(out=szs[:], in_=szd[q0:q0 + 128, :]),
                       reads=[("szd", qs)], writes=["szs"])
                sc.dma("sp", lambda: nc.sync.dma_start(out=x1s[:], in_=x1d[q0:q0 + 128, :]),
                       reads=[("x1d", qs)], writes=["x1s"])

            mhalf = sb("mhalf", [128, 1], F32)
            sc.op("pool", lambda: nc.gpsimd.memset(mhalf[:], -0.5), writes=["mhalf"])

            def epilogue(qs):
                q0 = qs * 128
                okeys = [("oacc", qs % 2, g) for g in range(4)]
                oacc = oacc2[qs % 2]
                sc.op("dve", lambda: nc.vector.tensor_tensor(out=ybf[:], in0=oacc[:, :, :].rearrange("p h d -> p (h d)"),
                                                             in1=szs[:], op=ALU.mult),
                      reads=okeys + ["szs"], writes=["ybf"])

                def p1():
                    def try_():
                        inst = None
                        for kc in range(8):
                            inst = nc.tensor.transpose(out=bfview(7)[:, kc * 128:(kc + 1) * 128],
                                                       in_=ybf[:, kc * 128:(kc + 1) * 128], identity=ident[:])
                        return inst
                    sc.op("pe", try_, reads=["ybf", "ident"], writes=[bk(7)])

                def p2():
                    sc.op("dve", lambda: nc.vector.tensor_copy(out=yT[:, :, :], in_=bfview(7)[:, :].rearrange("p (k t) -> p k t", k=8)),
                          reads=[bk(7)], writes=["cyT"])

                def p3():
                    for nh in range(2):
                        def mo(nh=nh):
                            inst = None
                            for kc in range(8):
                                inst = nc.tensor.matmul(banks[4 + nh][:, :], lhsT=yT[:, kc, :],
                                                        rhs=woutb[:, kc, nh * 512:(nh + 1) * 512],
                                                        start=(kc == 0), stop=(kc == 7))
                            return inst
                        sc.op("pe", mo, reads=["cyT"] + [("woutb", q) for q in range(8)], writes=[bk(4 + nh)])

                def p4():
                    for nh in range(2):
                        sc.op("dve", lambda nh=nh: nc.vector.tensor_tensor(
                            out=x2[:, nh * 512:(nh + 1) * 512], in0=banks[4 + nh][:, :], in1=x1s[:, nh * 512:(nh + 1) * 512],
                            op=ALU.add), reads=[bk(4 + nh), "x1s"], writes=[("x2", nh)])
                    sc.op("dve", lambda: nc.vector.scalar_tensor_tensor(
                        out=osb[:], in0=x2[:], scalar=1.0, in1=x2[:], op0=ALU.mult, op1=ALU.mult, accum_out=fss[:, qs:qs + 1]),
                        reads=[("x2", 0), ("x2", 1)], writes=["osb", "fss"])
                    sc.op("pool", lambda: nc.gpsimd.tensor_scalar(out=fsq[:, qs:qs + 1], in0=fss[:, qs:qs + 1], scalar1=1.0 / D,
                                                                  scalar2=EPS, op0=ALU.mult, op1=ALU.add),
                          reads=["fss"], writes=["fsq"])
                    sc.op("pool", lambda: nc.gpsimd.tensor_tensor(out=frs[:, qs:qs + 1], in0=fsq[:, qs:qs + 1], in1=mhalf[:],
                                                                  op=ALU.pow), reads=["fsq", "mhalf"], writes=["frs"])

                def p5():
                    sc.op("dve", lambda: nc.vector.scalar_tensor_tensor(
                        out=osb[:], in0=x2[:], scalar=frs[:, qs:qs + 1], in1=gfin[:], op0=ALU.mult, op1=ALU.mult),
                        reads=[("x2", 0), ("x2", 1), "frs", "gfin", "osb"], writes=["osb"])
                    sc.dma("sp", lambda: nc.sync.dma_start(out=outd[q0:q0 + 128, :], in_=osb[:]),
                           reads=["osb"], writes=[("out", qs)])
                for dly, fn in zip(_DBG.get("depi", (3, 5, 7, 10, 13)), (p1, p2, p3, p4, p5)):
                    deferred.append([dly, fn])

            stream = []

            def cmp_units(qs, g):
                qb = qs % 2
                q0 = qs * 128
                for nt in range(2):
                    stream.append(("unit", {
                        "score": (kcmpT[:, g, nt * 128:(nt + 1) * 128], Qc[qb][:, 4 * g:4 * g + 4, :],
                                  [("kcmpT", g // 2), ("Qc", qb), ("Qc0", qb)],
                                  cmask[:, nt:nt + 1, q0:q0 + 128].broadcast_to([128, 4, 128]), [("cmask", nt)]),
                        "pvs": [(3, 0, 64, AI[:, nt, :], ["AI"], nt == 0, nt == 1),
                                (3, 256, 64, vcmp[:, nt, g, :], [("vcmp", nt)], False, nt == 1)],
                        "post": ((lambda qs=qs, g=g, qb=qb: select_chain(qs, g, qb)) if nt == 1 else None)}))

            load_q(0)
            for g in range(4):
                cmp_units(0, g)
            for qs in range(16):
                qb = qs % 2
                q0 = qs * 128
                for g in range(4):
                    for k in (1, 2, 3, 0, 4):
                        kt = (qs - 4 + k) % 32
                        m = None
                        if k == 0:
                            m = upp[:, None, :].broadcast_to([128, 4, 128])
                        elif k == 4:
                            m = tri[:, None, :].broadcast_to([128, 4, 128])
                        stream.append(("unit", {
                            "score": (KwT[:, g, kt * 128:(kt + 1) * 128], Qwin[qb][:, 4 * g:4 * g + 4, :],
                                      kw_keys(g) + [("QwinQ", qb), ("Qwin1", qb)], m, ["tri", "upp"]),
                            "pvs": [(6, 0, 65, Vall[:, kt, (4 + g) * 65:(5 + g) * 65], [("Vall", kt // 4)], k == 1, k == 4)],
                            "post": ((lambda g=g, qb=qb: branch_out(6, g, 2, gts[qb], 2, ob=qb)) if k == 4 else None)}))
                    if g == 0 and qs + 1 < 16:
                        stream.append(("call", lambda qs=qs: load_q(qs + 1)))
                    if g == 3:
                        stream.append(("call", lambda qs=qs: load_tail(qs)))
                for g in range(4):
                    if qs + 1 < 16:
                        cmp_units(qs + 1, g)
                    klist = list(range(16, 32)) + list(range(0, qs + 1))
                    for idx, kt in enumerate(klist):
                        last = idx == len(klist) - 1
                        post = None
                        sb_ = 4 + g % 2
                        if last:
                            if g < 3:
                                post = (lambda g=g, qb=qb, sb_=sb_: untranspose(
                                    sb_, g % 2, lambda: branch_out(sb_, g, 1, gts[qb], 1, stride=128, ob=qb)))
                            else:
                                post = (lambda g=g, qb=qb, qs=qs, sb_=sb_: untranspose(
                                    sb_, g % 2, lambda: (branch_out(sb_, g, 1, gts[qb], 1, stride=128, ob=qb), epilogue(qs))))
                        stream.append(("unit", {
                            "score": (Ksel[:, g, kt * 128:(kt + 1) * 128], Qsel[qb][:, 4 * g:4 * g + 4, :],
                                      ksel_keys(g) + [("QselQ", qb), ("QselB", qb, g)],
                                      (tri[:, None, :].broadcast_to([128, 4, 128]) if kt == qs else None), ["tri"]),
                            "pvs": [("vstat", sb_, Vall[:, kt, g * 65:(g + 1) * 65], [("Vall", kt // 4)], idx == 0, last)],
                            "post": post}))

            LA = _DBG.get('LA', 2)
            deferred = []
            pend = []

            def retire():
                u, pi = pend.pop(0)
                emit_pv(u, pi)
                if u["post"] is not None:
                    u["post"]()
                for d_ in list(deferred):
                    d_[0] -= 1
                    if d_[0] <= 0:
                        deferred.remove(d_)
                        d_[1]()
            for kind, item in stream:
                if kind == "call":
                    item()
                    continue
                pend.append((item, emit_score(item)))
                if len(pend) > LA:
                    retire()
            while pend:
                retire()
            while deferred:
                deferred.sort(key=lambda x: x[0])
                deferred.pop(0)[1]()
            sc.flush()

    sc.finish()
    return nc, sc


def _shared_layout(inputs):
    f = np.float32
    a_w_in = np.asarray(inputs["a_w_in"], f)[0]
    w = a_w_in.reshape(8, 128, 4, 16, 128).transpose(3, 1, 0, 2, 4).reshape(16, 128, 4096)
    a_gT = np.asarray(inputs["a_norm"], f)[0].reshape(8, 128).T
    a_cw = np.asarray(inputs["a_conv_w"], f)[0].reshape(3, 16, 128).transpose(2, 1, 0).reshape(128, 48)
    w_out = np.asarray(inputs["a_w_out"], f)[0].reshape(16, 128, 1024).transpose(1, 0, 2).reshape(128, 16 * 1024)
    wkv = np.asarray(inputs["w_kv"], f).reshape(1024, 6, 256)[:, [0, 1, 2, 4, 3, 5], :].reshape(1024, 1536)
    wkv_r = wkv.reshape(8, 128, 1536).transpose(1, 0, 2).reshape(128, 8 * 1536)
    bw = np.asarray(inputs["b_w_in"], f)[0]
    bw = np.concatenate([bw[:, 0:1024], bw[:, 1072:2096], bw[:, 1024:1072]], axis=1)
    wbin_r = bw.reshape(8, 128, 2096).transpose(1, 0, 2).reshape(128, 8 * 2096)
    woutb = np.asarray(inputs["b_w_out"], f)[0].reshape(8, 128, 1024).transpose(1, 0, 2).reshape(128, 8 * 1024)

    def w1l(w1):
        return np.asarray(w1, f).reshape(16, 128, 256).transpose(1, 0, 2).reshape(128, 16 * 256)

    def w2l(w2):
        w = np.zeros((128, 2, 128), f)
        w[:, :, 0:64] = np.asarray(w2, f).reshape(2, 128, 64).transpose(1, 0, 2)
        return w.reshape(128, 256)
    tri = (np.arange(128)[:, None] <= np.arange(128)[None, :]).astype(f)
    upp = (np.arange(128)[:, None] > np.arange(128)[None, :]).astype(f)
    sh = {
        "w_in_r": w, "a_gT": a_gT, "a_cw": a_cw, "w_out_r": w_out, "ident": np.eye(128, dtype=f),
        "wkv_r": wkv_r, "wbin_r": wbin_r,
        "gkvT": np.asarray(inputs["kv_norm"], f).reshape(8, 128).T,
        "gbT": np.asarray(inputs["b_norm"], f)[0].reshape(8, 128).T,
        "w1k": w1l(inputs["cmp_w1_k"]), "w1v": w1l(inputs["cmp_w1_v"]),
        "w2k": w2l(inputs["cmp_w2_k"]), "w2v": w2l(inputs["cmp_w2_v"]),
        "poskT": np.asarray(inputs["cmp_pos_k"], f).reshape(16, 128).T, "posvT": np.asarray(inputs["cmp_pos_v"], f).reshape(16, 128).T,
        "tri": tri, "upp": upp, "woutb": woutb,
        "gfin": np.broadcast_to(np.asarray(inputs["final_norm"], f)[None, :], (128, 1024)),
    }
    return {k: np.ascontiguousarray(v, dtype=f) for k, v in sh.items()}


def _parity_consts(par):
    f = np.float32
    pos = (np.arange(S) + 2048 * par) % S
    inv = (np.float32(500000.0) ** (-np.arange(0, 16, 2, dtype=f) / np.float32(16))).astype(f)
    ang = pos.astype(f)[:, None] * inv[None, :]
    cosT = np.cos(ang).astype(f).reshape(32, 128, 8).transpose(1, 0, 2).reshape(128, 256)
    sinT = np.sin(ang).astype(f).reshape(32, 128, 8).transpose(1, 0, 2).reshape(128, 256)
    E = (pos[None, :] // 64 == np.arange(64)[:, None]).astype(f)
    kbias = np.zeros((1, S), f)
    if par == 0:
        kbias[0, 3584:] = NEG
    tq = np.arange(2048) + 2048 * par
    m = np.arange(256)
    a = (16 * m + 2048 * par) % S
    valid_m = (a + 31) < S
    n = a // 16
    cm = (valid_m[:, None] & ((16 * n + 31)[:, None] <= tq[None, :])).astype(f)
    cmask = cm.reshape(2, 128, 2048).transpose(1, 0, 2).reshape(128, 4096)
    A = np.zeros((256, 65), f)
    for mi in range(256):
        if not valid_m[mi] or n[mi] > 254:
            continue
        j, r = divmod(int(n[mi]), 4)
        A[mi, j] += 2.0 if r < 3 else 1.0
        if r == 3 and j + 1 < 64:
            A[mi, j + 1] += 1.0
    A[:, 64] = 1.0
    Aaug = A.reshape(2, 128, 65).transpose(1, 0, 2).reshape(128, 130)
    cur = tq // 64
    jj = np.arange(64)[None, :]
    validb = jj <= cur[:, None]
    forced = (jj == 0) | (validb & (jj > cur[:, None] - 2))
    fmul = (validb & ~forced).astype(f)
    fadd = np.where(forced, f(1e4), np.where(validb, f(0.0), f(-1e9))).astype(f)
    fmul = fmul.reshape(16, 128, 64).transpose(1, 0, 2).reshape(128, 1024)
    fadd = fadd.reshape(16, 128, 64).transpose(1, 0, 2).reshape(128, 1024)
    c = {"cosT": cosT, "sinT": sinT, "Emat": E, "kbias": kbias, "cmask": cmask, "Aaug": Aaug, "fmul": fmul, "fadd": fadd}
    return {k: np.ascontiguousarray(v, dtype=f) for k, v in c.items()}


def make_in_maps(inputs, cores=range(NCORES)):
    sh = _shared_layout(inputs)
    pc = [_parity_consts(0), _parity_consts(1)]
    x = np.asarray(inputs["x"], np.float32)
    maps = []
    for c in cores:
        b, par = c // 2, c % 2
        m = dict(sh)
        m.update(pc[par])
        m["xb"] = np.ascontiguousarray(np.roll(x[b], -2048 * par, axis=0))
        xh = np.zeros((4, D), np.float32)
        if par == 0:
            xh[2:4] = x[b, 2046:2048]
        else:
            xh[0:2] = x[b, 2046:2048]
        m["xh"] = xh
        maps.append(m)
    return maps


_CACHE = {}


def kernel(**inputs):
    if "nc" not in _CACHE:
        _CACHE["nc"] = build_program()[0]
    nc = _CACHE["nc"]
    in_maps = make_in_maps(inputs)
    res = run_bass_kernel_spmd(nc, in_maps, core_ids=list(range(NCORES)))
    out = np.empty((4, S, D), np.float32)
    for c in range(NCORES):
        b, par = c // 2, c % 2
        out[b, 2048 * par:2048 * (par + 1)] = res.results[c]["out"]
    return out
```

```python
import numpy as np
import concourse.bass as bass
import concourse.mybir as mybir
from concourse.bass_utils import run_bass_kernel_spmd
from contextlib import ExitStack

F32 = mybir.dt.float32
BF16 = mybir.dt.bfloat16
AF = mybir.ActivationFunctionType
ALU = mybir.AluOpType

D = 1024
S = 4096
CONV_D = 2048
EPS = 1e-6
NCORES = 8
_DBG = {"pe_sync": 0}


class Sched:
    def __init__(self, nc, n_dma_sems=10, same_engine_sync=False):
        self.nc = nc
        self.eng = {"pe": nc.tensor, "act": nc.scalar, "dve": nc.vector,
                    "pool": nc.gpsimd, "sp": nc.sync}
        self.ops = []
        self.last_w = {}
        self.readers = {}
        self.same_engine_sync = same_engine_sync
        self.n_dma_sems = n_dma_sems
        self.last_on = {}
        self.dma_all = []
        self.n_emitted = 0
        self.esem = None

    def _add(self, kind, eng, fn, reads, writes):
        idx = len(self.ops)
        reads = list(reads)
        writes = list(writes)
        for k in list(reads):
            if isinstance(k, tuple) and k and k[0] == "bank":
                reads.remove(k)
                if k not in writes:
                    writes.append(k)
        deps = set()
        for k in reads + writes:
            if k in self.last_w:
                deps.add(self.last_w[k])
        for k in writes:
            r = self.readers.get(k)
            if r:
                deps.update(r["c"].values())
                deps.update(r["d"])
        for k in writes:
            self.last_w[k] = idx
            self.readers[k] = {"c": {}, "d": []}
        for k in reads:
            if k in writes:
                continue
            r = self.readers.setdefault(k, {"c": {}, "d": []})
            if kind == "dma":
                r["d"].append(idx)
            else:
                r["c"][eng] = idx
        deps.discard(idx)
        self.ops.append({"kind": kind, "eng": eng, "fn": fn, "deps": deps, "need_inc": False})
        if kind == "dma":
            self.dma_all.append(idx)
        else:
            self.last_on[eng] = idx
        return idx

    def op(self, eng, fn, reads=(), writes=()):
        return self._add("c", eng, fn, reads, writes)

    def dma(self, queue, fn, reads=(), writes=()):
        return self._add("dma", queue, fn, reads, writes)

    def barrier(self):
        deps = set(self.last_on.values()) | set(self.dma_all)
        for e in ("pe", "act", "dve", "pool", "sp"):
            idx = len(self.ops)
            self.ops.append({"kind": "c", "eng": e, "fn": None, "deps": set(deps), "need_inc": False, "bar": True})
        self.dma_all = []

    def flush(self):
        self.barrier()
        nc = self.nc
        ops = self.ops
        start = self.n_emitted
        pe_sync = _DBG.get("pe_sync")

        def implicit(od, o):
            return (od["eng"] == o["eng"] and o["kind"] == "c" and od["kind"] == "c"
                    and (not self.same_engine_sync or (o["eng"] == "pe" and not pe_sync)))
        for o in ops[start:]:
            for d in o["deps"]:
                od = ops[d]
                if d < start or od["kind"] == "dma" or implicit(od, o):
                    continue
                od["need_inc"] = True
        if self.esem is None:
            self.esem = {e: nc.alloc_semaphore(name="sem_" + e) for e in ("pe", "act", "dve", "pool")}
            self.ecount = {e: 0 for e in self.esem}
            self.dsem = {}
            self.dcount = {}
            for q in ("sp", "act", "pool"):
                self.dsem[q] = [nc.alloc_semaphore(name="dma_%s_%d" % (q, i)) for i in range(self.n_dma_sems)]
                self.dcount[q] = 0
            self.waited = {e: {} for e in self.eng}
        esem, ecount, dsem, dcount, waited = self.esem, self.ecount, self.dsem, self.dcount, self.waited
        K = self.n_dma_sems
        for o in ops[start:]:
            e = o["eng"]
            engine = self.eng[e]
            waits = {}
            for d in o["deps"]:
                od = ops[d]
                if d < start and not o.get("bar"):
                    continue
                if od["kind"] == "dma":
                    key = ("d", od["eng"], od["slot"])
                    waits[key] = max(waits.get(key, 0), od["val"])
                else:
                    if implicit(od, o) or "val" not in od:
                        continue
                    key = ("c", od["eng"])
                    waits[key] = max(waits.get(key, 0), od["val"])
            if o["kind"] == "dma":
                j = dcount[e]
                slot = j % K
                if j >= K:
                    key = ("d", e, slot)
                    waits[key] = max(waits.get(key, 0), 16 * (j // K))
            for key, val in waits.items():
                if waited[e].get(key, 0) >= val:
                    continue
                waited[e][key] = val
                sem = dsem[key[1]][key[2]] if key[0] == "d" else esem[key[1]]
                engine.wait_ge(sem, val)
            if o["fn"] is None:
                continue
            inst = o["fn"]()
            if o["kind"] == "dma":
                j = dcount[e]
                o["slot"] = j % K
                o["val"] = 16 * (j // K + 1)
                inst.then_inc(dsem[e][o["slot"]], 16)
                dcount[e] = j + 1
            elif o["need_inc"]:
                ecount[e] += 1
                o["val"] = ecount[e]
                inst.then_inc(esem[e], 1)
        self.n_emitted = len(ops)
        self.stats = {"ops": len(ops), "incs": dict(ecount), "dmas": dict(dcount)}

    def finish(self):
        self.flush()


NEG = -30000.0


def build_program(stages=("A", "B", "C"), dbg=False, same_engine_sync=True):
    nc = bass.Bass("TRN2", target_bir_lowering=False)
    sc = Sched(nc, same_engine_sync=same_engine_sync)
    scratch_kind = "ExternalOutput" if dbg else "Internal"

    def dram(name, shape, dtype, kind):
        return nc.dram_tensor(name, list(shape), dtype, kind=kind).ap()

    def din(name, shape, dtype=F32):
        return dram(name, shape, dtype, "ExternalInput")

    xb = din("xb", [S, D])
    xh = din("xh", [4, D])
    w_in_r = din("w_in_r", [16, 128, 4096])
    a_gT = din("a_gT", [128, 8])
    a_cw = din("a_cw", [128, 48])
    w_out_r = din("w_out_r", [128, 16 * 1024])
    ident_d = din("ident", [128, 128])
    x1d = dram("x1", [S, D], F32, scratch_kind)
    wkv_r = din("wkv_r", [128, 8 * 1536])
    wbin_r = din("wbin_r", [128, 8 * 2096])
    gkvT_d = din("gkvT", [128, 8])
    gbT_d = din("gbT", [128, 8])
    cos_d = din("cosT", [128, 32 * 8])
    sin_d = din("sinT", [128, 32 * 8])
    kT4 = dram("kT4", [16, 64, S], BF16, scratch_kind)
    vtok = dram("vtok", [S, 520], BF16, scratch_kind)
    qT = dram("qT", [32, 64, 2048], BF16, scratch_kind)
    szd = dram("szd", [2048, 1024], F32, scratch_kind)
    gated = dram("gated", [2048, 48], F32, scratch_kind)
    w1k_d = din("w1k", [128, 16 * 256])
    w1v_d = din("w1v", [128, 16 * 256])
    w2k_d = din("w2k", [128, 2 * 128])
    w2v_d = din("w2v", [128, 2 * 128])
    posk_d = din("poskT", [128, 16])
    posv_d = din("posvT", [128, 16])
    E_d = din("Emat", [64, S])
    kbias_d = din("kbias", [1, S])
    cmask_d = din("cmask", [128, 2 * 2048])
    Aaug_d = din("Aaug", [128, 2 * 65])
    fmul_d = din("fmul", [128, 16 * 64])
    fadd_d = din("fadd", [128, 16 * 64])
    tri_d = din("tri", [128, 128])
    upp_d = din("upp", [128, 128])
    woutb_d = din("woutb", [128, 8 * 1024])
    gfin_d = din("gfin", [128, 1024])
    outd = dram("out", [2048, D], F32, "ExternalOutput")

    banks = [nc.alloc_psum_tensor("bank%d" % i, [128, 512], F32) for i in range(8)]
    bankctr = [0]

    def nextbank():
        b = bankctr[0] % 8
        bankctr[0] += 1
        return b

    def bk(i):
        return ("bank", i)

    def bfview(i):
        return banks[i].bitcast(BF16)

    mhalf_g = nc.alloc_sbuf_tensor("mhalf_g", [128, 1], F32)
    sc.op("pool", lambda: nc.gpsimd.memset(mhalf_g[:], -0.5), writes=["mhalf_g"])
    ident = nc.alloc_sbuf_tensor("ident_sb", [128, 128], BF16)
    sc.dma("pool", lambda: nc.gpsimd.dma_start(out=ident[:], in_=ident_d[:, :]), writes=["ident"])

    def rms_front(stk, tag, xin, xhat, ss, sq, rstd, col, rows=128):
        sc.op("act", lambda: nc.scalar.activation(
            out=xhat[0:rows, :], in_=xin[0:rows, :], func=AF.Square, accum_out=ss[0:rows, col:col + 1]),
            reads=[(tag, "xin")], writes=[(tag, "xhat"), (tag, "ss", col)])
        sc.op("pool", lambda: nc.gpsimd.tensor_scalar(
            out=sq[0:rows, col:col + 1], in0=ss[0:rows, col:col + 1], scalar1=1.0 / D, scalar2=EPS,
            op0=ALU.mult, op1=ALU.add), reads=[(tag, "ss", col)], writes=[(tag, "sq", col)])
        sc.op("pool", lambda: nc.gpsimd.tensor_tensor(
            out=rstd[0:rows, col:col + 1], in0=sq[0:rows, col:col + 1], in1=mhalf_g[0:rows, :], op=ALU.pow),
            reads=[(tag, "sq", col), "mhalf_g"], writes=[(tag, "rstd", col)])
        sc.op("dve", lambda: nc.vector.tensor_scalar(
            out=xhat[0:rows, :], in0=xin[0:rows, :], scalar1=rstd[0:rows, col:col + 1], scalar2=None, op0=ALU.mult),
            reads=[(tag, "xin"), (tag, "rstd", col)], writes=[(tag, "xhat")])

    if "A" in stages:
        with ExitStack() as stk:
            def sb(name, shape, dtype):
                return stk.enter_context(nc.sbuf_tensor(name, list(shape), dtype))
            gT = sb("gT", [128, 8], F32)
            cw = sb("cw", [128, 48], F32)
            woutT = sb("woutT", [128, 16, 1024], BF16)
            hT = sb("hT", [128, 8, 2048], BF16)
            hTh = sb("hTh", [128, 8, 4], BF16)
            yT = sb("yT", [128, 16, 2048], BF16)
            wblk = [sb("wblk%d" % i, [128, 4096], BF16) for i in range(2)]
            xin = [sb("xin%d" % i, [128, 1024], F32) for i in range(2)]
            xhat = [sb("xhat%d" % i, [128, 1024], BF16) for i in range(2)]
            ss = sb("ss", [128, 17], F32)
            sq = sb("sq", [128, 17], F32)
            rstd = sb("rstd", [128, 17], F32)
            vhalo = sb("vhalo", [128, 16, 2], F32)
            hcs = sb("hcs", [128, 4], F32)
            csb = [sb("csb%d" % i, [128, 512], F32) for i in range(2)]
            vbuf = [sb("vbuf%d" % i, [128, 514], F32) for i in range(2)]
            tbuf = [sb("tbuf%d" % i, [128, 512], F32) for i in range(2)]
            szb = [sb("szb%d" % i, [128, 512], F32) for i in range(2)]
            x1t = [sb("x1t%d" % i, [128, 1024], F32) for i in range(2)]

            sc.dma("sp", lambda: nc.sync.dma_start(out=gT[:], in_=a_gT[:, :]), writes=["gT"])
            sc.dma("sp", lambda: nc.sync.dma_start(out=cw[:], in_=a_cw[:, :]), writes=["cw"])
            for q in range(8):
                sc.dma("pool", lambda q=q: nc.gpsimd.dma_start(
                    out=woutT[:, 2 * q:2 * q + 2, :],
                    in_=w_out_r[:, 2048 * q:2048 * (q + 1)].rearrange("p (a n) -> p a n", a=2)),
                    writes=[("woutT", q)])

            sc.op("pool", lambda: nc.gpsimd.memset(xin[1][:], 0.0), writes=[("A1", "xin")])
            sc.dma("sp", lambda: nc.sync.dma_start(out=xin[1][0:4, :], in_=xh[:, :]), reads=[("A1", "xin")], writes=[("A1", "xin")])
            rms_front(stk, "A1", xin[1], xhat[1], ss, sq, rstd, 16)
            bi = nextbank()

            def trh(bi=bi):
                pst = bfview(bi)
                inst = None
                for kc in range(8):
                    inst = nc.tensor.transpose(out=pst[:, kc * 128:(kc + 1) * 128],
                                               in_=xhat[1][:, kc * 128:(kc + 1) * 128], identity=ident[:])
                return inst
            sc.op("pe", trh, reads=[("A1", "xhat"), "ident"], writes=[bk(bi)])
            sc.op("dve", lambda bi=bi: nc.vector.tensor_tensor(
                out=hTh[:, :, :], in0=bfview(bi)[:, :].rearrange("p (k t) -> p k t", k=8)[:, :, 0:4],
                in1=gT[:, :, None].broadcast_to([128, 8, 4]), op=ALU.mult),
                reads=[bk(bi), "gT"], writes=["hTh"])

            nblk = 0
            nchunk = 0
            for hf in range(2):
                tok0 = hf * 2048
                for tt in range(16):
                    s = tt % 2
                    tg = "A%d" % s
                    r0 = tok0 + tt * 128
                    sc.dma("sp", lambda s=s, r0=r0: nc.sync.dma_start(out=xin[s][:], in_=xb[r0:r0 + 128, :]),
                           writes=[(tg, "xin")])
                    rms_front(stk, tg, xin[s], xhat[s], ss, sq, rstd, tt)
                    bi = nextbank()

                    def tr(s=s, bi=bi):
                        pst = bfview(bi)
                        inst = None
                        for kc in range(8):
                            inst = nc.tensor.transpose(out=pst[:, kc * 128:(kc + 1) * 128],
                                                       in_=xhat[s][:, kc * 128:(kc + 1) * 128], identity=ident[:])
                        return inst
                    sc.op("pe", tr, reads=[(tg, "xhat"), "ident"], writes=[bk(bi)])
                    sc.op("dve", lambda tt=tt, bi=bi: nc.vector.tensor_tensor(
                        out=hT[:, :, tt * 128:(tt + 1) * 128],
                        in0=bfview(bi)[:, :].rearrange("p (k t) -> p k t", k=8),
                        in1=gT[:, :, None].broadcast_to([128, 8, 128]), op=ALU.mult),
                        reads=[bk(bi), "gT"], writes=[("hT", tt)])

                for cb in range(16):
                    wb = nblk % 2
                    nblk += 1
                    for hh in range(2):
                        sc.dma("pool", lambda cb=cb, wb=wb, hh=hh: nc.gpsimd.dma_start(
                            out=wblk[wb][:, 2048 * hh:2048 * (hh + 1)], in_=w_in_r[cb, :, 2048 * hh:2048 * (hh + 1)]),
                            writes=[("wblk", wb, hh)])
                    wkeys = [("wblk", wb, 0), ("wblk", wb, 1)]
                    for tc in range(4):
                        sl = nchunk % 2
                        nchunk += 1
                        c0 = tc * 512
                        bb, bc, bu, bz = 4 * sl, 4 * sl + 1, 4 * sl + 2, 4 * sl + 3
                        if tc == 0:
                            for j, bi in ((1, bb), (2, bz)):
                                def mmh(j=j, bi=bi, wb=wb):
                                    inst = None
                                    for kc in range(8):
                                        off = kc * 512 + j * 128
                                        inst = nc.tensor.matmul(banks[bi][:, 0:4], lhsT=wblk[wb][:, off:off + 128],
                                                                rhs=hTh[:, kc, :], start=(kc == 0), stop=(kc == 7))
                                    return inst
                                sc.op("pe", mmh, reads=wkeys + ["hTh"], writes=[bk(bi)])
                            sc.op("act", lambda bb=bb: nc.scalar.copy(out=hcs[:], in_=banks[bb][:, 0:4]),
                                  reads=[bk(bb)], writes=["hcs"])
                            sc.op("dve", lambda bz=bz, cb=cb, hf=hf: nc.vector.tensor_tensor(
                                out=vhalo[:, cb, :], in0=hcs[:, 2 * hf:2 * hf + 2], in1=banks[bz][:, 2 * hf:2 * hf + 2],
                                op=ALU.mult), reads=["hcs", bk(bz)], writes=[("vhalo", cb)])
                        for j in (1, 2, 0, 3):
                            bi = 4 * sl + j

                            def mm(j=j, bi=bi, wb=wb, c0=c0):
                                inst = None
                                for kc in range(8):
                                    off = kc * 512 + j * 128
                                    inst = nc.tensor.matmul(banks[bi][:, :], lhsT=wblk[wb][:, off:off + 128],
                                                            rhs=hT[:, kc, c0:c0 + 512],
                                                            start=(kc == 0), stop=(kc == 7))
                                return inst
                            sc.op("pe", mm, reads=wkeys + [("hT", 4 * tc + i) for i in range(4)], writes=[bk(bi)])
                        sc.op("act", lambda sl=sl, bc=bc: nc.scalar.copy(out=csb[sl][:], in_=banks[bc][:, :]),
                              reads=[bk(bc)], writes=[("csb", sl)])
                        sc.op("pool", lambda sl=sl, cb=cb: nc.gpsimd.tensor_copy(out=vbuf[sl][:, 0:2], in_=vhalo[:, cb, :]),
                              reads=[("vhalo", cb)], writes=[("vbufh", sl)])
                        sc.op("dve", lambda sl=sl, bu=bu: nc.vector.tensor_tensor(
                            out=vbuf[sl][:, 2:514], in0=csb[sl][:], in1=banks[bu][:, :], op=ALU.mult),
                            reads=[("csb", sl), bk(bu)], writes=[("vbuf", sl)])
                        sc.op("pool", lambda sl=sl, cb=cb: nc.gpsimd.tensor_copy(out=vhalo[:, cb, :], in_=vbuf[sl][:, 512:514]),
                              reads=[("vbuf", sl)], writes=[("vhalo", cb)])
                        sc.op("dve", lambda sl=sl, cb=cb: nc.vector.tensor_scalar(
                            out=tbuf[sl][:], in0=vbuf[sl][:, 0:512], scalar1=cw[:, 3 * cb:3 * cb + 1], scalar2=None,
                            op0=ALU.mult),
                            reads=[("vbuf", sl), ("vbufh", sl), "cw"], writes=[("tbuf", sl)])
                        sc.op("dve", lambda sl=sl, cb=cb: nc.vector.scalar_tensor_tensor(
                            out=tbuf[sl][:], in0=vbuf[sl][:, 1:513], scalar=cw[:, 3 * cb + 1:3 * cb + 2], in1=tbuf[sl][:],
                            op0=ALU.mult, op1=ALU.add),
                            reads=[("vbuf", sl), ("vbufh", sl), ("tbuf", sl)], writes=[("tbuf", sl)])
                        sc.op("dve", lambda sl=sl, cb=cb: nc.vector.scalar_tensor_tensor(
                            out=tbuf[sl][:], in0=vbuf[sl][:, 2:514], scalar=cw[:, 3 * cb + 2:3 * cb + 3], in1=tbuf[sl][:],
                            op0=ALU.mult, op1=ALU.add),
                            reads=[("vbuf", sl), ("tbuf", sl)], writes=[("tbuf", sl)])
                        sc.op("act", lambda sl=sl, bz=bz: nc.scalar.activation(out=szb[sl][:], in_=banks[bz][:, :], func=AF.Silu),
                              reads=[bk(bz)], writes=[("szb", sl)])
                        sc.op("dve", lambda sl=sl, bb=bb: nc.vector.tensor_tensor(
                            out=tbuf[sl][:], in0=tbuf[sl][:], in1=banks[bb][:, :], op=ALU.mult),
                            reads=[("tbuf", sl), bk(bb)], writes=[("tbuf", sl)])
                        sc.op("dve", lambda sl=sl, cb=cb, c0=c0: nc.vector.tensor_tensor(
                            out=yT[:, cb, c0:c0 + 512], in0=tbuf[sl][:], in1=szb[sl][:], op=ALU.mult),
                            reads=[("tbuf", sl), ("szb", sl)], writes=[("yT", cb, tc)])

                for tt in range(16):
                    s = tt % 2
                    tg = "A%d" % s
                    r0 = tok0 + tt * 128
                    sc.dma("sp", lambda s=s, r0=r0: nc.sync.dma_start(out=xin[s][:], in_=xb[r0:r0 + 128, :]),
                           writes=[(tg, "xin")])
                    for nh in range(2):
                        bi = nextbank()

                        def mm2(bi=bi, tt=tt, nh=nh):
                            inst = None
                            for cb in range(16):
                                inst = nc.tensor.matmul(banks[bi][:, :], lhsT=yT[:, cb, tt * 128:(tt + 1) * 128],
                                                        rhs=woutT[:, cb, nh * 512:(nh + 1) * 512],
                                                        start=(cb == 0), stop=(cb == 15))
                            return inst
                        sc.op("pe", mm2, reads=[("yT", cb, tt // 4) for cb in range(16)] + [("woutT", q) for q in range(8)],
                              writes=[bk(bi)])
                        sc.op("dve", lambda s=s, bi=bi, nh=nh: nc.vector.tensor_tensor(
                            out=x1t[s][:, nh * 512:(nh + 1) * 512], in0=banks[bi][:, :],
                            in1=xin[s][:, nh * 512:(nh + 1) * 512], op=ALU.add),
                            reads=[bk(bi), (tg, "xin")], writes=[("x1t", s, nh)])
                    sc.dma("sp", lambda s=s, r0=r0: nc.sync.dma_start(out=x1d[r0:r0 + 128, :], in_=x1t[s][:]),
                           reads=[("x1t", s, 0), ("x1t", s, 1)], writes=[("x1d", r0 // 128)])
            sc.flush()

    if "B" in stages:
        with ExitStack() as stk:
            def sb(name, shape, dtype):
                return stk.enter_context(nc.sbuf_tensor(name, list(shape), dtype))
            wkv = sb("wkv", [128, 8, 1536], BF16)
            wbin = sb("wbin", [128, 8, 2096], BF16)
            gkvT = sb("gkvT_sb", [128, 8], F32)
            gbT = sb("gbT_sb", [128, 8], F32)
            cosT = sb("cosT_sb", [128, 32, 8], F32)
            sinT = sb("sinT_sb", [128, 32, 8], F32)
            xin = [sb("bxin%d" % i, [128, 1024], F32) for i in range(2)]
            xhat = [sb("bxhat%d" % i, [128, 1024], BF16) for i in range(2)]
            ss = sb("bss", [128, 32], F32)
            sq = sb("bsq", [128, 32], F32)
            rstd = sb("brstd", [128, 32], F32)
            hkT = [sb("hkT%d" % i, [128, 8, 128], BF16) for i in range(2)]
            hqT = [sb("hqT%d" % i, [128, 8, 128], BF16) for i in range(2)]
            kcv = [sb("kcv%d" % i, [128, 512], BF16) for i in range(2)]
            kk = [sb("kk%d" % i, [128, 8, 64], BF16) for i in range(2)]
            rt = [sb("rt%d" % i, [128, 8, 8], F32) for i in range(4)]
            kTst = [sb("kTst%d" % i, [128, 8, 512], BF16) for i in range(2)]
            vst = [sb("vst%d" % i, [128, 4, 8, 65], BF16) for i in range(2)]
            qr = [sb("qr%d" % i, [128, 16, 64], BF16) for i in range(2)]
            qc = [sb("qc%d" % i, [128, 16, 64], BF16) for i in range(2)]
            qTst = [sb("qTst%d" % i, [128, 16, 512], BF16) for i in range(2)]
            szst = [sb("szst%d" % i, [128, 1024], F32) for i in range(2)]
            gst = [sb("gst%d" % i, [128, 48], F32) for i in range(2)]

            for q in range(4):
                sc.dma("pool", lambda q=q: nc.gpsimd.dma_start(
                    out=wkv[:, 2 * q:2 * q + 2, :],
                    in_=wkv_r[:, 3072 * q:3072 * (q + 1)].rearrange("p (a n) -> p a n", a=2)),
                    writes=[("wkv", q)])
            for q in range(8):
                sc.dma("pool", lambda q=q: nc.gpsimd.dma_start(
                    out=wbin[:, q, :].rearrange("p (a n) -> p a n", a=2),
                    in_=wbin_r[:, 2096 * q:2096 * (q + 1)].rearrange("p (a n) -> p a n", a=2)), writes=[("wbin", q)])
            wkv_keys = [("wkv", q) for q in range(4)]
            wbin_keys = [("wbin", q) for q in range(8)]
            sc.dma("sp", lambda: nc.sync.dma_start(out=gkvT[:], in_=gkvT_d[:, :]), writes=["gkvT"])
            sc.dma("sp", lambda: nc.sync.dma_start(out=gbT[:], in_=gbT_d[:, :]), writes=["gbT"])
            sc.dma("sp", lambda: nc.sync.dma_start(out=cosT[:], in_=cos_d[:, :].rearrange("p (t f) -> p t f", f=8)),
                   writes=["cosT"])
            sc.dma("sp", lambda: nc.sync.dma_start(out=sinT[:], in_=sin_d[:, :].rearrange("p (t f) -> p t f", f=8)),
                   writes=["sinT"])
            for i in range(2):
                sc.op("pool", lambda i=i: nc.gpsimd.memset(vst[i][:], 1.0), writes=[("vst", i)])

            def rope(src_bank, col0, nh, dst, tt, tagk, wkey):
                psv = banks[src_bank][:, col0:col0 + nh * 64].rearrange("p (h d) -> p h d", d=64)
                cs = cosT[:, tt:tt + 1, :].broadcast_to([128, nh, 8])
                sn = sinT[:, tt:tt + 1, :].broadcast_to([128, nh, 8])
                rd = [bk(src_bank), "cosT", "sinT"]
                for t_i, (lo, tab) in enumerate(((0, cs), (8, sn), (8, cs), (0, sn))):
                    sc.op("dve", lambda t_i=t_i, lo=lo, tab=tab: nc.vector.tensor_tensor(
                        out=rt[t_i][:, 0:nh, :], in0=psv[:, :, lo:lo + 8], in1=tab, op=ALU.mult),
                        reads=rd, writes=[("rt", t_i)])
                if _DBG.get("r1"):
                    return []
                sc.op("dve", lambda: nc.vector.tensor_tensor(
                    out=dst[:, 0:nh, 0:8], in0=rt[0][:, 0:nh, :], in1=rt[1][:, 0:nh, :], op=ALU.subtract),
                    reads=[("rt", 0), ("rt", 1)], writes=[wkey + ("a",)])
                sc.op("dve", lambda: nc.vector.tensor_tensor(
                    out=dst[:, 0:nh, 8:16], in0=rt[2][:, 0:nh, :], in1=rt[3][:, 0:nh, :], op=ALU.add),
                    reads=[("rt", 2), ("rt", 3)], writes=[wkey + ("b",)])
                if _DBG.get("r2"):
                    return []
                sc.op("dve", lambda: nc.vector.tensor_copy(out=dst[:, 0:nh, 16:64], in_=psv[:, :, 16:64]),
                      reads=[bk(src_bank)], writes=[wkey + ("c",)])
                return [wkey + ("a",), wkey + ("b",), wkey + ("c",)]

            for ch in range(_DBG.get('b_nch', 8)):
                own = ch < 4 and not _DBG.get('b_noq')
                cs_ = ch % 2
                for i in range(4):
                    tt = 4 * ch + i
                    s = tt % 2
                    tg = "B%d" % s
                    r0 = tt * 128
                    sc.dma("sp", lambda s=s, r0=r0: nc.sync.dma_start(out=xin[s][:], in_=x1d[r0:r0 + 128, :]),
                           reads=[("x1d", tt)], writes=[(tg, "xin")])
                    rms_front(stk, tg, xin[s], xhat[s], ss, sq, rstd, tt)
                    bi = nextbank()

                    def tr(s=s, bi=bi):
                        pst = bfview(bi)
                        inst = None
                        for kc in range(8):
                            inst = nc.tensor.transpose(out=pst[:, kc * 128:(kc + 1) * 128],
                                                       in_=xhat[s][:, kc * 128:(kc + 1) * 128], identity=ident[:])
                        return inst
                    sc.op("pe", tr, reads=[(tg, "xhat"), "ident"], writes=[bk(bi)])
                    sc.op("dve", lambda s=s, bi=bi: nc.vector.tensor_tensor(
                        out=hkT[s][:, :, :], in0=bfview(bi)[:, :].rearrange("p (k t) -> p k t", k=8),
                        in1=gkvT[:, :, None].broadcast_to([128, 8, 128]), op=ALU.mult),
                        reads=[bk(bi), "gkvT"], writes=[("hkT", s)])
                    if own:
                        sc.op("dve", lambda s=s, bi=bi: nc.vector.tensor_tensor(
                            out=hqT[s][:, :, :], in0=bfview(bi)[:, :].rearrange("p (k t) -> p k t", k=8),
                            in1=gbT[:, :, None].broadcast_to([128, 8, 128]), op=ALU.mult),
                            reads=[bk(bi), "gbT"], writes=[("hqT", s)])

                    def proj(bi, act, wt, c0, ncols):
                        def f():
                            inst = None
                            for kc in range(8):
                                inst = nc.tensor.matmul(banks[bi][:, 0:ncols], lhsT=act[:, kc, :],
                                                        rhs=wt[:, kc, c0:c0 + ncols], start=(kc == 0), stop=(kc == 7))
                            return inst
                        return f
                    b0 = nextbank()
                    sc.op("pe", proj(b0, hkT[s], wkv, 0, 512), reads=[("hkT", s)] + wkv_keys, writes=[bk(b0)])
                    b1 = nextbank()
                    sc.op("pe", proj(b1, hkT[s], wkv, 512, 512), reads=[("hkT", s)] + wkv_keys, writes=[bk(b1)])
                    b2 = nextbank()
                    sc.op("pe", proj(b2, hkT[s], wkv, 1024, 512), reads=[("hkT", s)] + wkv_keys, writes=[bk(b2)])
                    if _DBG.get("b_stop1"):
                        continue
                    sc.op("act", lambda s=s, b0=b0: nc.scalar.copy(out=kcv[s][:], in_=banks[b0][:, :]),
                          reads=[bk(b0)], writes=[("kcv", s)])
                    if _DBG.get("b_stop2"):
                        continue
                    kkeys = rope(b1, 0, 8, kk[s], tt, "kk", ("kk", s))
                    if _DBG.get("b_stop3"):
                        continue
                    sc.op("act", lambda s=s, b2=b2, i=i, cs_=cs_: nc.scalar.copy(
                        out=vst[cs_][:, i, :, 0:64], in_=banks[b2][:, :].rearrange("p (a d) -> p a d", d=64)),
                        reads=[bk(b2)], writes=[("vst", cs_)])
                    if _DBG.get("b_stop4"):
                        continue
                    for half, src, skeys in ((0, kcv[s], [("kcv", s)]), (1, kk[s], kkeys)):
                        bt = nextbank()

                        def trk(bt=bt, src=src, half=half):
                            pst = bfview(bt)
                            sv = src[:, :] if half == 0 else src[:, :, :].rearrange("p a d -> p (a d)")
                            inst = None
                            for a in range(4):
                                inst = nc.tensor.transpose(out=pst[:, a * 128:(a + 1) * 128],
                                                           in_=sv[:, a * 128:(a + 1) * 128], identity=ident[:])
                            return inst
                        sc.op("pe", trk, reads=skeys + ["ident"], writes=[bk(bt)])
                        sc.op("act" if half == 0 else "dve", (lambda bt=bt, half=half, i=i, cs_=cs_: (
                            nc.scalar.copy if half == 0 else nc.vector.tensor_copy)(
                            out=kTst[cs_][:, 4 * half:4 * half + 4, i * 128:(i + 1) * 128],
                            in_=bfview(bt)[:, 0:512].rearrange("p (a t) -> p a t", a=4))),
                            reads=[bk(bt)], writes=[("kTst", cs_, half, i)])
                    if own:
                        for qh in range(2):
                            bq = nextbank()
                            sc.op("pe", proj(bq, hqT[s], wbin, 512 * qh, 512), reads=[("hqT", s)] + wbin_keys, writes=[bk(bq)])
                            if not _DBG.get("q_noqc"):
                                sc.op("act", lambda s=s, bq=bq, qh=qh: nc.scalar.copy(
                                    out=qc[s][:, 8 * qh:8 * qh + 8, :], in_=banks[bq][:, :].rearrange("p (h d) -> p h d", d=64)),
                                    reads=[bk(bq)], writes=[("qc", s, qh)])
                            if not _DBG.get("q_norope"):
                                rope(bq, 0, 8, qr[s][:, 8 * qh:8 * qh + 8, :], tt, "q", ("qr", s, qh))
                        for zh in range(0 if not _DBG.get("q_noz") else 2, 2):
                            bz = nextbank()
                            sc.op("pe", proj(bz, hqT[s], wbin, 1024 + 512 * zh, 512), reads=[("hqT", s)] + wbin_keys, writes=[bk(bz)])
                            sc.op("act", lambda s=s, bz=bz, zh=zh: nc.scalar.activation(
                                out=szst[s][:, 512 * zh:512 * (zh + 1)], in_=banks[bz][:, :], func=AF.Sigmoid),
                                reads=[bk(bz)], writes=[("szst", s, zh)])
                            sc.op("dve", lambda s=s, bz=bz, zh=zh: nc.vector.tensor_tensor(
                                out=szst[s][:, 512 * zh:512 * (zh + 1)], in0=banks[bz][:, :],
                                in1=szst[s][:, 512 * zh:512 * (zh + 1)], op=ALU.mult),
                                reads=[bk(bz), ("szst", s, zh)], writes=[("szst", s, zh)])
                        if _DBG.get("q_stopg"):
                            continue
                        bg = nextbank()
                        sc.op("pe", proj(bg, hqT[s], wbin, 2048, 48), reads=[("hqT", s)] + wbin_keys, writes=[bk(bg)])
                        sc.op("act", lambda s=s, bg=bg: nc.scalar.activation(out=gst[s][:], in_=banks[bg][:, 0:48], func=AF.Sigmoid),
                              reads=[bk(bg)], writes=[("gst", s)])
                        sc.dma("sp", lambda s=s, r0=r0: nc.sync.dma_start(out=szd[r0:r0 + 128, :], in_=szst[s][:]),
                               reads=[("szst", s, 0), ("szst", s, 1)], writes=[("szd", tt)])
                        sc.dma("sp", lambda s=s, r0=r0: nc.sync.dma_start(out=gated[r0:r0 + 128, :], in_=gst[s][:]),
                               reads=[("gst", s)], writes=[("gated", tt)])
                        if _DBG.get("q_stoptr"):
                            continue
                        for which, src, skeys in ((0, qc[s], [("qc", s, 0), ("qc", s, 1)]),
                                                  (1, qr[s], [("qr", s, qh, x) for qh in range(2) for x in "abc"])):
                            bt = nextbank()

                            def trq(bt=bt, src=src):
                                pst = bfview(bt)
                                sv = src[:, :, :].rearrange("p h d -> p (h d)")
                                inst = None
                                for a in range(8):
                                    inst = nc.tensor.transpose(out=pst[:, a * 128:(a + 1) * 128],
                                                               in_=sv[:, a * 128:(a + 1) * 128], identity=ident[:])
                                return inst
                            sc.op("pe", trq, reads=skeys + ["ident"], writes=[bk(bt)])
                            a0 = 8 * which
                            sc.op("dve" if which == 0 else "act", (lambda bt=bt, which=which, a0=a0, i=i, cs_=cs_: (
                                nc.vector.tensor_copy if which == 0 else nc.scalar.copy)(
                                out=qTst[cs_][:, a0:a0 + 8, i * 128:(i + 1) * 128],
                                in_=bfview(bt)[:, :].rearrange("p (a t) -> p a t", a=8))),
                                reads=[bk(bt)], writes=[("qTst", cs_, which, i)])
                c0 = ch * 512
                if _DBG.get('b_nostore'):
                    continue
                sc.dma("sp", lambda cs_=cs_, c0=c0: nc.sync.dma_start(
                    out=kT4[:, :, c0:c0 + 512].rearrange("(a gl) d t -> (gl d) a t", gl=2), in_=kTst[cs_][:, :, :]),
                    reads=[("kTst", cs_, h_, i_) for h_ in range(2) for i_ in range(4)], writes=[("kT4", ch)])
                sc.dma("sp", lambda cs_=cs_, c0=c0: nc.sync.dma_start(
                    out=vtok[c0:c0 + 512, :].rearrange("(i p) f -> p i f", p=128),
                    in_=vst[cs_][:, :, :, :].rearrange("p i a d -> p i (a d)")),
                    reads=[("vst", cs_)], writes=[("vtok", ch)])
                if own:
                    sc.dma("sp", lambda cs_=cs_, c0=c0: nc.sync.dma_start(
                        out=qT[:, :, c0:c0 + 512].rearrange("(a hl) d t -> (hl d) a t", hl=2), in_=qTst[cs_][:, :, :]),
                        reads=[("qTst", cs_, w_, i_) for w_ in range(2) for i_ in range(4)],
                        writes=[("qT", ch)])
            sc.flush()

    if "C" in stages:
        with ExitStack() as stk:
            def sb(name, shape, dtype):
                return stk.enter_context(nc.sbuf_tensor(name, list(shape), dtype))
            kcmpT = sb("kcmpT", [128, 4, 256], BF16)
            vcmp = sb("vcmp", [128, 2, 4, 64], BF16)
            AI = sb("AI", [128, 2, 64], BF16)
            sc.dma("pool", lambda: nc.gpsimd.dma_start(out=AI[:], in_=Aaug_d[:, :].rearrange("p (a c) -> p a c", a=2)[:, :, 0:64]),
                   writes=["AI"])
            all_kT4 = [("kT4", ch) for ch in range(8)]

            with ExitStack() as stk2:
                def sb2(name, shape, dtype):
                    return stk2.enter_context(nc.sbuf_tensor(name, list(shape), dtype))
                kcT = sb2("kcT", [128, 4, 4128], BF16)
                w1 = sb2("w1", [128, 16, 256], BF16)
                w2 = sb2("w2", [128, 2, 128], BF16)
                posT = sb2("posT", [128, 16], BF16)
                hidT = sb2("hidT", [128, 2, 4, 256], BF16)
                pb = sb2("pb", [128, 2], F32)
                ub = sb2("ub", [128, 512], F32)
                tb = sb2("tb", [128, 512], F32)
                sg = sb2("sg", [128, 512], F32)
                for st in range(2):
                    for g in range(4):
                        sc.dma("sp", lambda st=st, g=g: nc.sync.dma_start(out=kcT[0:64, g, 0:4096], in_=kT4[4 * st + g, :, :]),
                               reads=all_kT4, writes=[("kcT", g)])
                        sc.dma("sp", lambda st=st, g=g: nc.sync.dma_start(out=kcT[0:64, g, 4096:4128], in_=kT4[4 * st + g, :, 0:32]),
                               reads=all_kT4, writes=[("kcTw", g)])
                        sc.dma("sp", lambda st=st, g=g: nc.sync.dma_start(out=kcT[64:128, g, 0:4095], in_=kT4[4 * st + g, :, 1:4096]),
                               reads=all_kT4, writes=[("kcT2", g)])
                        sc.dma("sp", lambda st=st, g=g: nc.sync.dma_start(out=kcT[64:128, g, 4095:4127], in_=kT4[4 * st + g, :, 0:32]),
                               reads=all_kT4, writes=[("kcT2w", g)])
                    w1d = w1k_d if st == 0 else w1v_d
                    w2d = w2k_d if st == 0 else w2v_d
                    posd = posk_d if st == 0 else posv_d
                    for q in range(4):
                        sc.dma("pool", lambda q=q, w1d=w1d: nc.gpsimd.dma_start(
                            out=w1[:, 4 * q:4 * q + 4, :], in_=w1d[:, 1024 * q:1024 * (q + 1)].rearrange("p (l n) -> p l n", n=256)),
                            writes=[("w1", q)])
                    sc.dma("pool", lambda w2d=w2d: nc.gpsimd.dma_start(out=w2[:], in_=w2d[:, :].rearrange("p (a n) -> p a n", a=2)),
                           writes=["w2"])
                    sc.dma("pool", lambda posd=posd: nc.gpsimd.dma_start(out=posT[:], in_=posd[:, :]), writes=["posT"])
                    w1keys = [("w1", q) for q in range(4)]
                    kckeys = [(nm, g) for g in range(4) for nm in ("kcT", "kcTw", "kcT2", "kcT2w")]
                    for hc in range(2):
                        bi = nextbank()

                        def mpb(bi=bi, hc=hc):
                            inst = None
                            for l in range(16):
                                inst = nc.tensor.matmul(banks[bi][:, 0:1], lhsT=w1[:, l, hc * 128:(hc + 1) * 128],
                                                        rhs=posT[:, l:l + 1], start=(l == 0), stop=(l == 15))
                            return inst
                        sc.op("pe", mpb, reads=w1keys + ["posT"], writes=[bk(bi)])
                        sc.op("act", lambda bi=bi, hc=hc: nc.scalar.copy(out=pb[:, hc:hc + 1], in_=banks[bi][:, 0:1]),
                              reads=[bk(bi)], writes=[("pb", hc)])
                        for gp in range(2):
                            bh = nextbank()

                            def mh(bh=bh, hc=hc, gp=gp):
                                inst = None
                                for l in range(16):
                                    inst = nc.tensor.matmul(banks[bh][:, :], lhsT=w1[:, l, hc * 128:(hc + 1) * 128],
                                                            rhs=kcT[:, 2 * gp:2 * gp + 2, 2 * l:2 * l + 4096:16],
                                                            start=(l == 0), stop=(l == 15))
                                return inst
                            sc.op("pe", mh, reads=w1keys + kckeys, writes=[bk(bh)])
                            sc.op("act", lambda bh=bh, hc=hc: nc.scalar.activation(
                                out=ub[:], in_=banks[bh][:, :], func=AF.Identity, bias=pb[:, hc:hc + 1]),
                                reads=[bk(bh), ("pb", hc)], writes=["ub"])
                            sc.op("dve", lambda: nc.vector.tensor_tensor(out=tb[:], in0=ub[:], in1=ub[:], op=ALU.mult),
                                  reads=["ub"], writes=["tb"])
                            sc.op("dve", lambda: nc.vector.tensor_scalar(out=tb[:], in0=tb[:], scalar1=0.044715, scalar2=1.0,
                                                                         op0=ALU.mult, op1=ALU.add),
                                  reads=["tb"], writes=["tb"])
                            sc.op("dve", lambda: nc.vector.tensor_tensor(out=tb[:], in0=tb[:], in1=ub[:], op=ALU.mult),
                                  reads=["tb", "ub"], writes=["tb"])
                            sc.op("act", lambda: nc.scalar.activation(out=sg[:], in_=tb[:], func=AF.Sigmoid, scale=1.5957691216057308),
                                  reads=["tb"], writes=["sg"])
                            sc.op("dve", lambda hc=hc, gp=gp: nc.vector.tensor_tensor(
                                out=hidT[:, hc, 2 * gp:2 * gp + 2, :], in0=ub[:, :].rearrange("p (a m) -> p a m", a=2),
                                in1=sg[:, :].rearrange("p (a m) -> p a m", a=2), op=ALU.mult),
                                reads=["ub", "sg"], writes=[("hidT", hc, gp)])
                    hkeys = [("hidT", hc, gp) for hc in range(2) for gp in range(2)]
                    if st == 0:
                        for gp in range(2):
                            bo = nextbank()

                            def mk(bo=bo, gp=gp):
                                inst = None
                                for hc in range(2):
                                    inst = nc.tensor.matmul(banks[bo][:, :], lhsT=w2[:, hc, :],
                                                            rhs=hidT[:, hc, 2 * gp:2 * gp + 2, :], start=(hc == 0), stop=(hc == 1))
                                return inst
                            sc.op("pe", mk, reads=hkeys + ["w2"], writes=[bk(bo)])
                            sc.op("act", lambda bo=bo, gp=gp: nc.scalar.copy(
                                out=kcmpT[:, 2 * gp:2 * gp + 2, :], in_=banks[bo][:, :].rearrange("p (a m) -> p a m", a=2)),
                                reads=[bk(bo)], writes=[("kcmpT", gp)])
                    else:
                        for nt in range(2):
                            bo = nextbank()

                            def mv(bo=bo, nt=nt):
                                inst = None
                                for g in range(4):
                                    for hc in range(2):
                                        inst = nc.tensor.matmul(banks[bo][:, g * 64:(g + 1) * 64],
                                                                lhsT=hidT[:, hc, g, nt * 128:(nt + 1) * 128],
                                                                rhs=w2[:, hc, 0:64], start=(g == 0 and hc == 0), stop=(hc == 1),
                                                                skip_group_check=True)
                                return inst
                            sc.op("pe", mv, reads=hkeys + ["w2"], writes=[bk(bo)])
                            sc.op("act", lambda bo=bo, nt=nt: nc.scalar.copy(
                                out=vcmp[:, nt, :, :], in_=banks[bo][:, 0:256].rearrange("p (g d) -> p g d", g=4)),
                                reads=[bk(bo)], writes=[("vcmp", nt)])
                sc.flush()

            Ksel = sb("Ksel", [128, 4, 4096], BF16)
            KwT = sb("KwT", [128, 4, 4096], BF16)
            Vall = sb("Vall", [128, 32, 520], BF16)
            cmask = sb("cmask_sb", [128, 2, 2048], BF16)
            fmul = sb("fmul_sb", [128, 16, 64], F32)
            fadd = sb("fadd_sb", [128, 16, 64], F32)
            tri = sb("tri_sb", [128, 128], BF16)
            upp = sb("upp_sb", [128, 128], BF16)
            woutb = sb("woutb_sb", [128, 8, 1024], BF16)
            gfin = sb("gfin_sb", [128, 1024], F32)
            Qsel = [sb("Qsel%d" % i, [128, 16, 128], BF16) for i in range(2)]
            Qwin = [sb("Qwin%d" % i, [128, 16, 128], BF16) for i in range(2)]
            Qc = [sb("Qc%d" % i, [128, 16, 128], BF16) for i in range(2)]
            gts = [sb("gts%d" % i, [128, 16, 3], F32) for i in range(2)]
            szs = sb("szs", [128, 1024], F32)
            x1s = sb("x1s", [128, 1024], F32)
            oacc2 = [sb("oacc%d" % i, [128, 16, 64], F32) for i in range(2)]
            otmp = [sb("otmp%d" % i, [128, 4, 64], F32) for i in range(2)]
            NPT = 4
            PTb = [sb("PTb%d" % i, [128, 512], BF16) for i in range(NPT)]
            rs = [sb("rs%d" % i, [128, 4], F32) for i in range(3)]
            rc = [sb("rc%d" % i, [128, 4], F32) for i in range(3)]
            wg = [sb("wg%d" % i, [128, 4], F32) for i in range(3)]
            impb = sb("impb", [128, 64], F32)
            scr = sb("scr", [128, 64], F32)
            scr2 = sb("scr2", [128, 64], F32)
            m8a = sb("m8a", [128, 8], F32)
            m8b = sb("m8b", [128, 8], F32)
            thr = sb("thr", [128, 1], F32)
            negb4 = [sb("negb%d" % i, [128, 128], BF16) for i in range(4)]
            ybf = sb("ybf", [128, 1024], BF16)
            yT = sb("cyT", [128, 8, 128], BF16)
            x2 = sb("x2", [128, 1024], F32)
            fss = sb("fss", [128, 16], F32)
            fsq = sb("fsq", [128, 16], F32)
            frs = sb("frs", [128, 16], F32)
            junk = sb("junk", [128, 1024], BF16)
            osb = sb("osb", [128, 1024], F32)
            identF = sb("identF", [128, 128], F32)
            accT = [sb("accT%d" % i, [128, 512], F32) for i in range(2)]
            sc.dma("sp", lambda: nc.sync.dma_start(out=identF[:], in_=ident_d[:, :]), writes=["identF"])
            for i in range(2):
                sc.op("pool", lambda i=i: nc.gpsimd.memset(accT[i][:], 0.0), writes=[("accT", i)])

            for g in range(4):
                sc.dma("sp", lambda g=g: nc.sync.dma_start(out=Ksel[0:64, g, :], in_=kT4[8 + g, :, :]),
                       reads=all_kT4, writes=[("Ksel", g)])
                sc.op("pool", lambda g=g: nc.gpsimd.memset(KwT[64:128, g, :], 0.0), writes=[("KwTz", g)])
                sc.dma("sp", lambda g=g: nc.sync.dma_start(out=KwT[0:64, g, :], in_=kT4[12 + g, :, :]),
                       reads=all_kT4, writes=[("KwT", g)])
                for hh in range(2):
                    sc.dma("pool", lambda g=g, hh=hh: nc.gpsimd.dma_start(
                        out=Ksel[64:128, g, 2048 * hh:2048 * (hh + 1)], in_=E_d[:, 2048 * hh:2048 * (hh + 1)]),
                        writes=[("KselE", g, hh)])
                    sc.dma("pool", lambda g=g, hh=hh: nc.gpsimd.dma_start(
                        out=KwT[64:65, g, 2048 * hh:2048 * (hh + 1)], in_=kbias_d[:, 2048 * hh:2048 * (hh + 1)]),
                        reads=[("KwTz", g)], writes=[("KwTb", g, hh)])
            for ch in range(8):
                sc.dma("sp", lambda ch=ch: nc.sync.dma_start(
                    out=Vall[:, 4 * ch:4 * ch + 4, :], in_=vtok[512 * ch:512 * (ch + 1), :].rearrange("(i p) f -> p i f", p=128)),
                    reads=[("vtok", ch)], writes=[("Vall", ch)])
            for hh in range(2):
                sc.dma("pool", lambda hh=hh: nc.gpsimd.dma_start(out=cmask[:, hh, :], in_=cmask_d[:, 2048 * hh:2048 * (hh + 1)]),
                       writes=[("cmask", hh)])
            sc.dma("sp", lambda: nc.sync.dma_start(out=fmul[:], in_=fmul_d[:, :].rearrange("p (t j) -> p t j", j=64)), writes=["fmul"])
            sc.dma("sp", lambda: nc.sync.dma_start(out=fadd[:], in_=fadd_d[:, :].rearrange("p (t j) -> p t j", j=64)), writes=["fadd"])
            sc.dma("pool", lambda: nc.gpsimd.dma_start(out=tri[:], in_=tri_d[:, :]), writes=["tri"])
            sc.dma("pool", lambda: nc.gpsimd.dma_start(out=upp[:], in_=upp_d[:, :]), writes=["upp"])
            for q in range(8):
                sc.dma("pool", lambda q=q: nc.gpsimd.dma_start(out=woutb[:, q, :], in_=woutb_d[:, 1024 * q:1024 * (q + 1)]),
                       writes=[("woutb", q)])
            sc.dma("sp", lambda: nc.sync.dma_start(out=gfin[:], in_=gfin_d[:, :]), writes=["gfin"])
            for i in range(2):
                sc.op("pool", lambda i=i: nc.gpsimd.memset(Qwin[i][64:128, :, :], 0.0), writes=[("Qwin1", i)])
                sc.op("pool", lambda i=i: nc.gpsimd.memset(Qwin[i][64:65, :, :], 1.0), reads=[("Qwin1", i)], writes=[("Qwin1", i)])
                sc.op("pool", lambda i=i: nc.gpsimd.memset(Qc[i][64:128, :, :], 0.0), writes=[("Qc0", i)])
            for i in range(4):
                sc.op("pool", lambda i=i: nc.gpsimd.memset(negb4[i][:, 0:64], 0.0), writes=[("negb0", i)])
            ksel_keys = lambda g: [("Ksel", g), ("KselE", g, 0), ("KselE", g, 1)]
            kw_keys = lambda g: [("KwT", g), ("KwTz", g), ("KwTb", g, 0), ("KwTb", g, 1)]
            sbank = [0]
            ptc = [0]

            def emit_score(u):
                bS = sbank[0] % 3
                sbank[0] += 1
                pi = ptc[0] % NPT
                ptc[0] += 1
                lhsT, rhs, rkeys, mask, mkeys = u["score"]
                sc.op("pe", lambda: nc.tensor.matmul(banks[bS][:, :], lhsT=lhsT, rhs=rhs, start=True, stop=True),
                      reads=rkeys, writes=[bk(bS)])
                sc.op("act", lambda: nc.scalar.activation(out=PTb[pi][:], in_=banks[bS][:, :], func=AF.Exp, scale=0.125),
                      reads=[bk(bS)], writes=[("PT", pi)])
                if mask is not None:
                    sc.op("dve", lambda: nc.vector.tensor_tensor(
                        out=PTb[pi][:, :].rearrange("p (h q) -> p h q", h=4),
                        in0=PTb[pi][:, :].rearrange("p (h q) -> p h q", h=4),
                        in1=mask, op=ALU.mult), reads=[("PT", pi)] + list(mkeys), writes=[("PT", pi)])
                return pi

            def emit_pv(u, pi):
                for pv_ in u["pvs"]:
                    if pv_[0] == "vstat":
                        _, accb, lhsT_, rkeys, first, last = pv_
                        sc.op("pe", lambda accb=accb, lhsT_=lhsT_, first=first, last=last: nc.tensor.matmul(
                            banks[accb][0:65, :], lhsT=lhsT_, rhs=PTb[pi][:, :], start=first, stop=last),
                            reads=[("PT", pi)] + rkeys, writes=[bk(accb)])
                        continue
                    (accb, col0, width, rhs_, rkeys, first, last) = pv_

                    def f(accb=accb, col0=col0, width=width, rhs_=rhs_, first=first, last=last):
                        inst = None
                        for h in range(4):
                            inst = nc.tensor.matmul(banks[accb][:, col0 + h * width:col0 + (h + 1) * width],
                                                    lhsT=PTb[pi][:, h * 128:(h + 1) * 128], rhs=rhs_,
                                                    start=(first and h == 0), stop=last, skip_group_check=True)
                        return inst
                    sc.op("pe", f, reads=[("PT", pi)] + rkeys, writes=[bk(accb)])

            def untranspose(accb, ai, then):
                sc.op("dve", lambda: nc.vector.tensor_copy(out=accT[ai][0:65, :], in_=banks[accb][0:65, :]),
                      reads=[bk(accb), ("accT", ai)], writes=[("accT", ai)])

                def tail():
                    def tr4():
                        inst = None
                        for h in range(4):
                            inst = nc.tensor.transpose(out=banks[accb][:, h * 128:(h + 1) * 128],
                                                       in_=accT[ai][:, h * 128:(h + 1) * 128], identity=identF[:])
                        return inst
                    sc.op("pe", tr4, reads=[("accT", ai), "identF"], writes=[bk(accb)])
                    then()
                deferred.append([_DBG.get("dunt", 4), tail])

            def branch_out(accb, g, gidx, gt, slot, stride=65, ob=0):
                oacc = oacc2[ob]
                av = banks[accb][:, 0:4 * stride].rearrange("p (h c) -> p h c", c=stride)
                r_, c_, w_ = rs[slot], rc[slot], wg[slot]
                sc.op("dve", lambda: nc.vector.tensor_scalar(out=r_[:], in0=av[:, :, 64], scalar1=1e-30, scalar2=None,
                                                             op0=ALU.max), reads=[bk(accb)], writes=[("rs", slot)])
                sc.op("dve", lambda: nc.vector.reciprocal(out=c_[:], in_=r_[:]), reads=[("rs", slot)], writes=[("rc", slot)])
                sc.op("dve", lambda: nc.vector.tensor_tensor(out=w_[:], in0=c_[:], in1=gt[:, 4 * g:4 * g + 4, gidx], op=ALU.mult),
                      reads=[("rc", slot), "gts"], writes=[("wg", slot)])
                ot = otmp[slot - 1]
                sc.op("dve", lambda: nc.vector.tensor_tensor(
                    out=ot[:], in0=av[:, :, 0:64], in1=w_[:, :, None].broadcast_to([128, 4, 64]), op=ALU.mult),
                    reads=[bk(accb), ("wg", slot)], writes=[("otmp", slot)])
                sc.op("pool", lambda: nc.gpsimd.tensor_tensor(
                    out=oacc[:, 4 * g:4 * g + 4, :], in0=oacc[:, 4 * g:4 * g + 4, :], in1=ot[:], op=ALU.add),
                    reads=[("otmp", slot), ("oacc", ob, g)], writes=[("oacc", ob, g)])

            def select_chain(qs, g, qb):
                gt = gts[qb]
                oacc = oacc2[qb]
                negb = negb4[g]
                iv = banks[3][:, 0:256].rearrange("p (h c) -> p h c", c=64)
                r_, c_, w_ = rs[0], rc[0], wg[0]
                sc.op("dve", lambda: nc.vector.reduce_sum(out=r_[:], in_=iv, axis=mybir.AxisListType.X),
                      reads=[bk(3)], writes=[("rs", 0)])
                sc.op("dve", lambda: nc.vector.tensor_scalar(out=r_[:], in0=r_[:], scalar1=0.5, scalar2=1e-30,
                                                             op0=ALU.mult, op1=ALU.max), reads=[("rs", 0)], writes=[("rs", 0)])
                sc.op("dve", lambda: nc.vector.reciprocal(out=c_[:], in_=r_[:]), reads=[("rs", 0)], writes=[("rc", 0)])
                sc.op("dve", lambda: nc.vector.tensor_scalar(out=impb[:], in0=iv[:, 0, 0:64], scalar1=c_[:, 0:1],
                                                             scalar2=None, op0=ALU.mult),
                      reads=[bk(3), ("rc", 0)], writes=["impb"])
                for h in range(1, 4):
                    sc.op("dve", lambda h=h: nc.vector.scalar_tensor_tensor(
                        out=impb[:], in0=iv[:, h, 0:64], scalar=c_[:, h:h + 1], in1=impb[:], op0=ALU.mult, op1=ALU.add),
                        reads=[bk(3), ("rc", 0), "impb"], writes=["impb"])
                sc.op("dve", lambda: nc.vector.tensor_tensor(out=w_[:], in0=c_[:], in1=gt[:, 4 * g:4 * g + 4, 0], op=ALU.mult),
                      reads=[("rc", 0), "gts"], writes=[("wg", 0)])
                sc.op("dve", lambda: nc.vector.tensor_tensor(
                    out=oacc[:, 4 * g:4 * g + 4, :], in0=banks[3][:, 256:512].rearrange("p (h c) -> p h c", c=64),
                    in1=w_[:, :, None].broadcast_to([128, 4, 64]), op=ALU.mult),
                    reads=[bk(3), ("wg", 0)], writes=[("oacc", qb, g)])
                sc.op("dve", lambda: nc.vector.tensor_tensor(out=scr[:], in0=impb[:], in1=fmul[:, qs, :], op=ALU.mult),
                      reads=["impb", "fmul"], writes=["scr"])
                sc.op("dve", lambda: nc.vector.tensor_tensor(out=scr[:], in0=scr[:], in1=fadd[:, qs, :], op=ALU.add),
                      reads=["scr", "fadd"], writes=["scr"])
                sc.op("dve", lambda: nc.vector.max(out=m8a[:], in_=scr[:]), reads=["scr"], writes=["m8a"])
                sc.op("dve", lambda: nc.vector.match_replace(out=scr2[:], in_to_replace=m8a[:], in_values=scr[:], imm_value=-2e9),
                      reads=["scr", "m8a"], writes=["scr2"])
                sc.op("dve", lambda: nc.vector.max(out=m8b[:], in_=scr2[:]), reads=["scr2"], writes=["m8b"])
                sc.op("dve", lambda: nc.vector.tensor_scalar(out=thr[:], in0=m8b[:, 7:8], scalar1=-1e8, scalar2=None, op0=ALU.max),
                      reads=["m8b"], writes=["thr"])
                sc.op("dve", lambda: nc.vector.tensor_scalar(out=negb[:, 64:128], in0=scr[:], scalar1=thr[:, 0:1], scalar2=NEG,
                                                             op0=ALU.is_lt, op1=ALU.mult),
                      reads=["scr", "thr"], writes=[("negb", g)])

                def tail():
                    sc.op("pe", lambda: nc.tensor.transpose(out=bfview(7)[:, 0:128], in_=negb[:, :], identity=ident[:]),
                          reads=[("negb", g), ("negb0", g), "ident"], writes=[bk(7)])
                    sc.op("dve", lambda: nc.vector.tensor_copy(
                        out=Qsel[qb][64:128, 4 * g:4 * g + 4, :], in_=bfview(7)[64:128, None, 0:128].broadcast_to([64, 4, 128])),
                        reads=[bk(7)], writes=[("QselB", qb, g)])
                deferred.append([_DBG.get("dsel", 14), tail])

            def load_q(qs):
                qb = qs % 2
                q0 = qs * 128
                sc.dma("sp", lambda: nc.sync.dma_start(
                    out=Qsel[qb][0:64, :, :], in_=qT[16:32, :, q0:q0 + 128].rearrange("a d t -> d a t")),
                    reads=[("qT", qs // 4)], writes=[("QselQ", qb)])
                sc.dma("sp", lambda: nc.sync.dma_start(
                    out=Qwin[qb][0:64, :, :], in_=qT[16:32, :, q0:q0 + 128].rearrange("a d t -> d a t")),
                    reads=[("qT", qs // 4)], writes=[("QwinQ", qb)])
                sc.dma("sp", lambda: nc.sync.dma_start(
                    out=Qc[qb][0:64, :, :], in_=qT[0:16, :, q0:q0 + 128].rearrange("a d t -> d a t")),
                    reads=[("qT", qs // 4)], writes=[("Qc", qb)])
                sc.dma("sp", lambda: nc.sync.dma_start(
                    out=gts[qb][:, :, :], in_=gated[q0:q0 + 128, :].rearrange("p (h c) -> p h c", c=3)),
                    reads=[("gated", qs)], writes=["gts"])

            def load_tail(qs):
                q0 = qs * 128
                sc.dma("sp", lambda: nc.sync.dma_start(out=szs[:], in_=szd[q0:q0 + 128, :]),
                       reads=[("szd", qs)], writes=["szs"])
                sc.dma("sp", lambda: nc.sync.dma_start(out=x1s[:], in_=x1d[q0:q0 + 128, :]),
                       reads=[("x1d", qs)], writes=["x1s"])

            mhalf = sb("mhalf", [128, 1], F32)
            sc.op("pool", lambda: nc.gpsimd.memset(mhalf[:], -0.5), writes=["mhalf"])

            def epilogue(qs):
                q0 = qs * 128
                okeys = [("oacc", qs % 2, g) for g in range(4)]
                oacc = oacc2[qs % 2]
                sc.op("dve", lambda: nc.vector.tensor_tensor(out=ybf[:], in0=oacc[:, :, :].rearrange("p h d -> p (h d)"),
                                                             in1=szs[:], op=ALU.mult),
                      reads=okeys + ["szs"], writes=["ybf"])

                def p1():
                    def try_():
                        inst = None
                        for kc in range(8):
                            inst = nc.tensor.transpose(out=bfview(7)[:, kc * 128:(kc + 1) * 128],
                                                       in_=ybf[:, kc * 128:(kc + 1) * 128], identity=ident[:])
                        return inst
                    sc.op("pe", try_, reads=["ybf", "ident"], writes=[bk(7)])

                def p2():
                    sc.op("dve", lambda: nc.vector.tensor_copy(out=yT[:, :, :], in_=bfview(7)[:, :].rearrange("p (k t) -> p k t", k=8)),
                          reads=[bk(7)], writes=["cyT"])

                def p3():
                    for nh in range(2):
                        def mo(nh=nh):
                            inst = None
                            for kc in range(8):
                                inst = nc.tensor.matmul(banks[4 + nh][:, :], lhsT=yT[:, kc, :],
                                                        rhs=woutb[:, kc, nh * 512:(nh + 1) * 512],
                                                        start=(kc == 0), stop=(kc == 7))
                            return inst
                        sc.op("pe", mo, reads=["cyT"] + [("woutb", q) for q in range(8)], writes=[bk(4 + nh)])

                def p4():
                    for nh in range(2):
                        sc.op("dve", lambda nh=nh: nc.vector.tensor_tensor(
                            out=x2[:, nh * 512:(nh + 1) * 512], in0=banks[4 + nh][:, :], in1=x1s[:, nh * 512:(nh + 1) * 512],
                            op=ALU.add), reads=[bk(4 + nh), "x1s"], writes=[("x2", nh)])
                    sc.op("dve", lambda: nc.vector.scalar_tensor_tensor(
                        out=osb[:], in0=x2[:], scalar=1.0, in1=x2[:], op0=ALU.mult, op1=ALU.mult, accum_out=fss[:, qs:qs + 1]),
                        reads=[("x2", 0), ("x2", 1)], writes=["osb", "fss"])
                    sc.op("pool", lambda: nc.gpsimd.tensor_scalar(out=fsq[:, qs:qs + 1], in0=fss[:, qs:qs + 1], scalar1=1.0 / D,
                                                                  scalar2=EPS, op0=ALU.mult, op1=ALU.add),
                          reads=["fss"], writes=["fsq"])
                    sc.op("pool", lambda: nc.gpsimd.tensor_tensor(out=frs[:, qs:qs + 1], in0=fsq[:, qs:qs + 1], in1=mhalf[:],
                                                                  op=ALU.pow), reads=["fsq", "mhalf"], writes=["frs"])

                def p5():
                    sc.op("dve", lambda: nc.vector.scalar_tensor_tensor(
                        out=osb[:], in0=x2[:], scalar=frs[:, qs:qs + 1], in1=gfin[:], op0=ALU.mult, op1=ALU.mult),
                        reads=[("x2", 0), ("x2", 1), "frs", "gfin", "osb"], writes=["osb"])
                    sc.dma("sp", lambda: nc.sync.dma_start(out=outd[q0:q0 + 128, :], in_=osb[:]),
                           reads=["osb"], writes=[("out", qs)])
                for dly, fn in zip(_DBG.get("depi", (3, 5, 7, 10, 13)), (p1, p2, p3, p4, p5)):
                    deferred.append([dly, fn])

            stream = []

            def cmp_units(qs, g):
                qb = qs % 2
                q0 = qs * 128
                for nt in range(2):
                    stream.append(("unit", {
                        "score": (kcmpT[:, g, nt * 128:(nt + 1) * 128], Qc[qb][:, 4 * g:4 * g + 4, :],
                                  [("kcmpT", g // 2), ("Qc", qb), ("Qc0", qb)],
                                  cmask[:, nt:nt + 1, q0:q0 + 128].broadcast_to([128, 4, 128]), [("cmask", nt)]),
                        "pvs": [(3, 0, 64, AI[:, nt, :], ["AI"], nt == 0, nt == 1),
                                (3, 256, 64, vcmp[:, nt, g, :], [("vcmp", nt)], False, nt == 1)],
                        "post": ((lambda qs=qs, g=g, qb=qb: select_chain(qs, g, qb)) if nt == 1 else None)}))

            load_q(0)
            for g in range(4):
                cmp_units(0, g)
            for qs in range(16):
                qb = qs % 2
                q0 = qs * 128
                for g in range(4):
                    for k in (1, 2, 3, 0, 4):
                        kt = (qs - 4 + k) % 32
                        m = None
                        if k == 0:
                            m = upp[:, None, :].broadcast_to([128, 4, 128])
                        elif k == 4:
                            m = tri[:, None, :].broadcast_to([128, 4, 128])
                        stream.append(("unit", {
                            "score": (KwT[:, g, kt * 128:(kt + 1) * 128], Qwin[qb][:, 4 * g:4 * g + 4, :],
                                      kw_keys(g) + [("QwinQ", qb), ("Qwin1", qb)], m, ["tri", "upp"]),
                            "pvs": [(6, 0, 65, Vall[:, kt, (4 + g) * 65:(5 + g) * 65], [("Vall", kt // 4)], k == 1, k == 4)],
                            "post": ((lambda g=g, qb=qb: branch_out(6, g, 2, gts[qb], 2, ob=qb)) if k == 4 else None)}))
                    if g == 0 and qs + 1 < 16:
                        stream.append(("call", lambda qs=qs: load_q(qs + 1)))
                    if g == 3:
                        stream.append(("call", lambda qs=qs: load_tail(qs)))
                for g in range(4):
                    if qs + 1 < 16:
                        cmp_units(qs + 1, g)
                    klist = list(range(16, 32)) + list(range(0, qs + 1))
                    for idx, kt in enumerate(klist):
                        last = idx == len(klist) - 1
                        post = None
                        sb_ = 4 + g % 2
                        if last:
                            if g < 3:
                                post = (lambda g=g, qb=qb, sb_=sb_: untranspose(
                                    sb_, g % 2, lambda: branch_out(sb_, g, 1, gts[qb], 1, stride=128, ob=qb)))
                            else:
                                post = (lambda g=g, qb=qb, qs=qs, sb_=sb_: untranspose(
                                    sb_, g % 2, lambda: (branch_out(sb_, g, 1, gts[qb], 1, stride=128, ob=qb), epilogue(qs))))
                        stream.append(("unit", {
                            "score": (Ksel[:, g, kt * 128:(kt + 1) * 128], Qsel[qb][:, 4 * g:4 * g + 4, :],
                                      ksel_keys(g) + [("QselQ", qb), ("QselB", qb, g)],
                                      (tri[:, None, :].broadcast_to([128, 4, 128]) if kt == qs else None), ["tri"]),
                            "pvs": [("vstat", sb_, Vall[:, kt, g * 65:(g + 1) * 65], [("Vall", kt // 4)], idx == 0, last)],
                            "post": post}))

            LA = _DBG.get('LA', 2)
            deferred = []
            pend = []

            def retire():
                u, pi = pend.pop(0)
                emit_pv(u, pi)
                if u["post"] is not None:
                    u["post"]()
                for d_ in list(deferred):
                    d_[0] -= 1
                    if d_[0] <= 0:
                        deferred.remove(d_)
                        d_[1]()
            for kind, item in stream:
                if kind == "call":
                    item()
                    continue
                pend.append((item, emit_score(item)))
                if len(pend) > LA:
                    retire()
            while pend:
                retire()
            while deferred:
                deferred.sort(key=lambda x: x[0])
                deferred.pop(0)[1]()
            sc.flush()

    sc.finish()
    return nc, sc


def _shared_layout(inputs):
    f = np.float32
    a_w_in = np.asarray(inputs["a_w_in"], f)[0]
    w = a_w_in.reshape(8, 128, 4, 16, 128).transpose(3, 1, 0, 2, 4).reshape(16, 128, 4096)
    a_gT = np.asarray(inputs["a_norm"], f)[0].reshape(8, 128).T
    a_cw = np.asarray(inputs["a_conv_w"], f)[0].reshape(3, 16, 128).transpose(2, 1, 0).reshape(128, 48)
    w_out = np.asarray(inputs["a_w_out"], f)[0].reshape(16, 128, 1024).transpose(1, 0, 2).reshape(128, 16 * 1024)
    wkv = np.asarray(inputs["w_kv"], f).reshape(1024, 6, 256)[:, [0, 1, 2, 4, 3, 5], :].reshape(1024, 1536)
    wkv_r = wkv.reshape(8, 128, 1536).transpose(1, 0, 2).reshape(128, 8 * 1536)
    bw = np.asarray(inputs["b_w_in"], f)[0]
    bw = np.concatenate([bw[:, 0:1024], bw[:, 1072:2096], bw[:, 1024:1072]], axis=1)
    wbin_r = bw.reshape(8, 128, 2096).transpose(1, 0, 2).reshape(128, 8 * 2096)
    woutb = np.asarray(inputs["b_w_out"], f)[0].reshape(8, 128, 1024).transpose(1, 0, 2).reshape(128, 8 * 1024)

    def w1l(w1):
        return np.asarray(w1, f).reshape(16, 128, 256).transpose(1, 0, 2).reshape(128, 16 * 256)

    def w2l(w2):
        w = np.zeros((128, 2, 128), f)
        w[:, :, 0:64] = np.asarray(w2, f).reshape(2, 128, 64).transpose(1, 0, 2)
        return w.reshape(128, 256)
    tri = (np.arange(128)[:, None] <= np.arange(128)[None, :]).astype(f)
    upp = (np.arange(128)[:, None] > np.arange(128)[None, :]).astype(f)
    sh = {
        "w_in_r": w, "a_gT": a_gT, "a_cw": a_cw, "w_out_r": w_out, "ident": np.eye(128, dtype=f),
        "wkv_r": wkv_r, "wbin_r": wbin_r,
        "gkvT": np.asarray(inputs["kv_norm"], f).reshape(8, 128).T,
        "gbT": np.asarray(inputs["b_norm"], f)[0].reshape(8, 128).T,
        "w1k": w1l(inputs["cmp_w1_k"]), "w1v": w1l(inputs["cmp_w1_v"]),
        "w2k": w2l(inputs["cmp_w2_k"]), "w2v": w2l(inputs["cmp_w2_v"]),
        "poskT": np.asarray(inputs["cmp_pos_k"], f).reshape(16, 128).T, "posvT": np.asarray(inputs["cmp_pos_v"], f).reshape(16, 128).T,
        "tri": tri, "upp": upp, "woutb": woutb,
        "gfin": np.broadcast_to(np.asarray(inputs["final_norm"], f)[None, :], (128, 1024)),
    }
    return {k: np.ascontiguousarray(v, dtype=f) for k, v in sh.items()}


def _parity_consts(par):
    f = np.float32
    pos = (np.arange(S) + 2048 * par) % S
    inv = (np.float32(500000.0) ** (-np.arange(0, 16, 2, dtype=f) / np.float32(16))).astype(f)
    ang = pos.astype(f)[:, None] * inv[None, :]
    cosT = np.cos(ang).astype(f).reshape(32, 128, 8).transpose(1, 0, 2).reshape(128, 256)
    sinT = np.sin(ang).astype(f).reshape(32, 128, 8).transpose(1, 0, 2).reshape(128, 256)
    E = (pos[None, :] // 64 == np.arange(64)[:, None]).astype(f)
    kbias = np.zeros((1, S), f)
    if par == 0:
        kbias[0, 3584:] = NEG
    tq = np.arange(2048) + 2048 * par
    m = np.arange(256)
    a = (16 * m + 2048 * par) % S
    valid_m = (a + 31) < S
    n = a // 16
    cm = (valid_m[:, None] & ((16 * n + 31)[:, None] <= tq[None, :])).astype(f)
    cmask = cm.reshape(2, 128, 2048).transpose(1, 0, 2).reshape(128, 4096)
    A = np.zeros((256, 65), f)
    for mi in range(256):
        if not valid_m[mi] or n[mi] > 254:
            continue
        j, r = divmod(int(n[mi]), 4)
        A[mi, j] += 2.0 if r < 3 else 1.0
        if r == 3 and j + 1 < 64:
            A[mi, j + 1] += 1.0
    A[:, 64] = 1.0
    Aaug = A.reshape(2, 128, 65).transpose(1, 0, 2).reshape(128, 130)
    cur = tq // 64
    jj = np.arange(64)[None, :]
    validb = jj <= cur[:, None]
    forced = (jj == 0) | (validb & (jj > cur[:, None] - 2))
    fmul = (validb & ~forced).astype(f)
    fadd = np.where(forced, f(1e4), np.where(validb, f(0.0), f(-1e9))).astype(f)
    fmul = fmul.reshape(16, 128, 64).transpose(1, 0, 2).reshape(128, 1024)
    fadd = fadd.reshape(16, 128, 64).transpose(1, 0, 2).reshape(128, 1024)
    c = {"cosT": cosT, "sinT": sinT, "Emat": E, "kbias": kbias, "cmask": cmask, "Aaug": Aaug, "fmul": fmul, "fadd": fadd}
    return {k: np.ascontiguousarray(v, dtype=f) for k, v in c.items()}


def make_in_maps(inputs, cores=range(NCORES)):
    sh = _shared_layout(inputs)
    pc = [_parity_consts(0), _parity_consts(1)]
    x = np.asarray(inputs["x"], np.float32)
    maps = []
    for c in cores:
        b, par = c // 2, c % 2
        m = dict(sh)
        m.update(pc[par])
        m["xb"] = np.ascontiguousarray(np.roll(x[b], -2048 * par, axis=0))
        xh = np.zeros((4, D), np.float32)
        if par == 0:
            xh[2:4] = x[b, 2046:2048]
        else:
            xh[0:2] = x[b, 2046:2048]
        m["xh"] = xh
        maps.append(m)
    return maps


_CACHE = {}


def kernel(**inputs):
    if "nc" not in _CACHE:
        _CACHE["nc"] = build_program()[0]
    nc = _CACHE["nc"]
    in_maps = make_in_maps(inputs)
    res = run_bass_kernel_spmd(nc, in_maps, core_ids=list(range(NCORES)))
    out = np.empty((4, S, D), np.float32)
    for c in range(NCORES):
        b, par = c // 2, c % 2
        out[b, 2048 * par:2048 * (par + 1)] = res.results[c]["out"]
    return out
```

```python
import numpy as np
import concourse.bass as bass
import concourse.mybir as mybir
from concourse.bass_utils import run_bass_kernel_spmd
from contextlib import ExitStack

F32 = mybir.dt.float32
BF16 = mybir.dt.bfloat16
AF = mybir.ActivationFunctionType
ALU = mybir.AluOpType

D = 1024
S = 4096
CONV_D = 2048
EPS = 1e-6
NCORES = 8
_DBG = {"pe_sync": 0}


class Sched:
    def __init__(self, nc, n_dma_sems=10, same_engine_sync=False):
        self.nc = nc
        self.eng = {"pe": nc.tensor, "act": nc.scalar, "dve": nc.vector,
                    "pool": nc.gpsimd, "sp": nc.sync}
        self.ops = []
        self.last_w = {}
        self.readers = {}
        self.same_engine_sync = same_engine_sync
        self.n_dma_sems = n_dma_sems
        self.last_on = {}
        self.dma_all = []
        self.n_emitted = 0
        self.esem = None

    def _add(self, kind, eng, fn, reads, writes):
        idx = len(self.ops)
        reads = list(reads)
        writes = list(writes)
        for k in list(reads):
            if isinstance(k, tuple) and k and k[0] == "bank":
                reads.remove(k)
                if k not in writes:
                    writes.append(k)
        deps = set()
        for k in reads + writes:
            if k in self.last_w:
                deps.add(self.last_w[k])
        for k in writes:
            r = self.readers.get(k)
            if r:
                deps.update(r["c"].values())
                deps.update(r["d"])
        for k in writes:
            self.last_w[k] = idx
            self.readers[k] = {"c": {}, "d": []}
        for k in reads:
            if k in writes:
                continue
            r = self.readers.setdefault(k, {"c": {}, "d": []})
            if kind == "dma":
                r["d"].append(idx)
            else:
                r["c"][eng] = idx
        deps.discard(idx)
        self.ops.append({"kind": kind, "eng": eng, "fn": fn, "deps": deps, "need_inc": False})
        if kind == "dma":
            self.dma_all.append(idx)
        else:
            self.last_on[eng] = idx
        return idx

    def op(self, eng, fn, reads=(), writes=()):
        return self._add("c", eng, fn, reads, writes)

    def dma(self, queue, fn, reads=(), writes=()):
        return self._add("dma", queue, fn, reads, writes)

    def barrier(self):
        deps = set(self.last_on.values()) | set(self.dma_all)
        for e in ("pe", "act", "dve", "pool", "sp"):
            idx = len(self.ops)
            self.ops.append({"kind": "c", "eng": e, "fn": None, "deps": set(deps), "need_inc": False, "bar": True})
        self.dma_all = []

    def flush(self):
        self.barrier()
        nc = self.nc
        ops = self.ops
        start = self.n_emitted
        pe_sync = _DBG.get("pe_sync")

        def implicit(od, o):
            return (od["eng"] == o["eng"] and o["kind"] == "c" and od["kind"] == "c"
                    and (not self.same_engine_sync or (o["eng"] == "pe" and not pe_sync)))
        for o in ops[start:]:
            for d in o["deps"]:
                od = ops[d]
                if d < start or od["kind"] == "dma" or implicit(od, o):
                    continue
                od["need_inc"] = True
        if self.esem is None:
            self.esem = {e: nc.alloc_semaphore(name="sem_" + e) for e in ("pe", "act", "dve", "pool")}
            self.ecount = {e: 0 for e in self.esem}
            self.dsem = {}
            self.dcount = {}
            for q in ("sp", "act", "pool"):
                self.dsem[q] = [nc.alloc_semaphore(name="dma_%s_%d" % (q, i)) for i in range(self.n_dma_sems)]
                self.dcount[q] = 0
            self.waited = {e: {} for e in self.eng}
        esem, ecount, dsem, dcount, waited = self.esem, self.ecount, self.dsem, self.dcount, self.waited
        K = self.n_dma_sems
        for o in ops[start:]:
            e = o["eng"]
            engine = self.eng[e]
            waits = {}
            for d in o["deps"]:
                od = ops[d]
                if d < start and not o.get("bar"):
                    continue
                if od["kind"] == "dma":
                    key = ("d", od["eng"], od["slot"])
                    waits[key] = max(waits.get(key, 0), od["val"])
                else:
                    if implicit(od, o) or "val" not in od:
                        continue
                    key = ("c", od["eng"])
                    waits[key] = max(waits.get(key, 0), od["val"])
            if o["kind"] == "dma":
                j = dcount[e]
                slot = j % K
                if j >= K:
                    key = ("d", e, slot)
                    waits[key] = max(waits.get(key, 0), 16 * (j // K))
            for key, val in waits.items():
                if waited[e].get(key, 0) >= val:
                    continue
                waited[e][key] = val
                sem = dsem[key[1]][key[2]] if key[0] == "d" else esem[key[1]]
                engine.wait_ge(sem, val)
            if o["fn"] is None:
                continue
            inst = o["fn"]()
            if o["kind"] == "dma":
                j = dcount[e]
                o["slot"] = j % K
                o["val"] = 16 * (j // K + 1)
                inst.then_inc(dsem[e][o["slot"]], 16)
                dcount[e] = j + 1
            elif o["need_inc"]:
                ecount[e] += 1
                o["val"] = ecount[e]
                inst.then_inc(esem[e], 1)
        self.n_emitted = len(ops)
        self.stats = {"ops": len(ops), "incs": dict(ecount), "dmas": dict(dcount)}

    def finish(self):
        self.flush()


NEG = -30000.0


def build_program(stages=("A", "B", "C"), dbg=False, same_engine_sync=True):
    nc = bass.Bass("TRN2", target_bir_lowering=False)
    sc = Sched(nc, same_engine_sync=same_engine_sync)
    scratch_kind = "ExternalOutput" if dbg else "Internal"

    def dram(name, shape, dtype, kind):
        return nc.dram_tensor(name, list(shape), dtype, kind=kind).ap()

    def din(name, shape, dtype=F32):
        return dram(name, shape, dtype, "ExternalInput")

    xb = din("xb", [S, D])
    xh = din("xh", [4, D])
    w_in_r = din("w_in_r", [16, 128, 4096])
    a_gT = din("a_gT", [128, 8])
    a_cw = din("a_cw", [128, 48])
    w_out_r = din("w_out_r", [128, 16 * 1024])
    ident_d = din("ident", [128, 128])
    x1d = dram("x1", [S, D], F32, scratch_kind)
    wkv_r = din("wkv_r", [128, 8 * 1536])
    wbin_r = din("wbin_r", [128, 8 * 2096])
    gkvT_d = din("gkvT", [128, 8])
    gbT_d = din("gbT", [128, 8])
    cos_d = din("cosT", [128, 32 * 8])
    sin_d = din("sinT", [128, 32 * 8])
    kT4 = dram("kT4", [16, 64, S], BF16, scratch_kind)
    vtok = dram("vtok", [S, 520], BF16, scratch_kind)
    qT = dram("qT", [32, 64, 2048], BF16, scratch_kind)
    szd = dram("szd", [2048, 1024], F32, scratch_kind)
    gated = dram("gated", [2048, 48], F32, scratch_kind)
    w1k_d = din("w1k", [128, 16 * 256])
    w1v_d = din("w1v", [128, 16 * 256])
    w2k_d = din("w2k", [128, 2 * 128])
    w2v_d = din("w2v", [128, 2 * 128])
    posk_d = din("poskT", [128, 16])
    posv_d = din("posvT", [128, 16])
    E_d = din("Emat", [64, S])
    kbias_d = din("kbias", [1, S])
    cmask_d = din("cmask", [128, 2 * 2048])
    Aaug_d = din("Aaug", [128, 2 * 65])
    fmul_d = din("fmul", [128, 16 * 64])
    fadd_d = din("fadd", [128, 16 * 64])
    tri_d = din("tri", [128, 128])
    upp_d = din("upp", [128, 128])
    woutb_d = din("woutb", [128, 8 * 1024])
    gfin_d = din("gfin", [128, 1024])
    outd = dram("out", [2048, D], F32, "ExternalOutput")

    banks = [nc.alloc_psum_tensor("bank%d" % i, [128, 512], F32) for i in range(8)]
    bankctr = [0]

    def nextbank():
        b = bankctr[0] % 8
        bankctr[0] += 1
        return b

    def bk(i):
        return ("bank", i)

    def bfview(i):
        return banks[i].bitcast(BF16)

    mhalf_g = nc.alloc_sbuf_tensor("mhalf_g", [128, 1], F32)
    sc.op("pool", lambda: nc.gpsimd.memset(mhalf_g[:], -0.5), writes=["mhalf_g"])
    ident = nc.alloc_sbuf_tensor("ident_sb", [128, 128], BF16)
    sc.dma("pool", lambda: nc.gpsimd.dma_start(out=ident[:], in_=ident_d[:, :]), writes=["ident"])

    def rms_front(stk, tag, xin, xhat, ss, sq, rstd, col, rows=128):
        sc.op("act", lambda: nc.scalar.activation(
            out=xhat[0:rows, :], in_=xin[0:rows, :], func=AF.Square, accum_out=ss[0:rows, col:col + 1]),
            reads=[(tag, "xin")], writes=[(tag, "xhat"), (tag, "ss", col)])
        sc.op("pool", lambda: nc.gpsimd.tensor_scalar(
            out=sq[0:rows, col:col + 1], in0=ss[0:rows, col:col + 1], scalar1=1.0 / D, scalar2=EPS,
            op0=ALU.mult, op1=ALU.add), reads=[(tag, "ss", col)], writes=[(tag, "sq", col)])
        sc.op("pool", lambda: nc.gpsimd.tensor_tensor(
            out=rstd[0:rows, col:col + 1], in0=sq[0:rows, col:col + 1], in1=mhalf_g[0:rows, :], op=ALU.pow),
            reads=[(tag, "sq", col), "mhalf_g"], writes=[(tag, "rstd", col)])
        sc.op("dve", lambda: nc.vector.tensor_scalar(
            out=xhat[0:rows, :], in0=xin[0:rows, :], scalar1=rstd[0:rows, col:col + 1], scalar2=None, op0=ALU.mult),
            reads=[(tag, "xin"), (tag, "rstd", col)], writes=[(tag, "xhat")])

    if "A" in stages:
        with ExitStack() as stk:
            def sb(name, shape, dtype):
                return stk.enter_context(nc.sbuf_tensor(name, list(shape), dtype))
            gT = sb("gT", [128, 8], F32)
            cw = sb("cw", [128, 48], F32)
            woutT = sb("woutT", [128, 16, 1024], BF16)
            hT = sb("hT", [128, 8, 2048], BF16)
            hTh = sb("hTh", [128, 8, 4], BF16)
            yT = sb("yT", [128, 16, 2048], BF16)
            wblk = [sb("wblk%d" % i, [128, 4096], BF16) for i in range(2)]
            xin = [sb("xin%d" % i, [128, 1024], F32) for i in range(2)]
            xhat = [sb("xhat%d" % i, [128, 1024], BF16) for i in range(2)]
            ss = sb("ss", [128, 17], F32)
            sq = sb("sq", [128, 17], F32)
            rstd = sb("rstd", [128, 17], F32)
            vhalo = sb("vhalo", [128, 16, 2], F32)
            hcs = sb("hcs", [128, 4], F32)
            csb = [sb("csb%d" % i, [128, 512], F32) for i in range(2)]
            vbuf = [sb("vbuf%d" % i, [128, 514], F32) for i in range(2)]
            tbuf = [sb("tbuf%d" % i, [128, 512], F32) for i in range(2)]
            szb = [sb("szb%d" % i, [128, 512], F32) for i in range(2)]
            x1t = [sb("x1t%d" % i, [128, 1024], F32) for i in range(2)]

            sc.dma("sp", lambda: nc.sync.dma_start(out=gT[:], in_=a_gT[:, :]), writes=["gT"])
            sc.dma("sp", lambda: nc.sync.dma_start(out=cw[:], in_=a_cw[:, :]), writes=["cw"])
            for q in range(8):
                sc.dma("pool", lambda q=q: nc.gpsimd.dma_start(
                    out=woutT[:, 2 * q:2 * q + 2, :],
                    in_=w_out_r[:, 2048 * q:2048 * (q + 1)].rearrange("p (a n) -> p a n", a=2)),
                    writes=[("woutT", q)])

            sc.op("pool", lambda: nc.gpsimd.memset(xin[1][:], 0.0), writes=[("A1", "xin")])
            sc.dma("sp", lambda: nc.sync.dma_start(out=xin[1][0:4, :], in_=xh[:, :]), reads=[("A1", "xin")], writes=[("A1", "xin")])
            rms_front(stk, "A1", xin[1], xhat[1], ss, sq, rstd, 16)
            bi = nextbank()

            def trh(bi=bi):
                pst = bfview(bi)
                inst = None
                for kc in range(8):
                    inst = nc.tensor.transpose(out=pst[:, kc * 128:(kc + 1) * 128],
                                               in_=xhat[1][:, kc * 128:(kc + 1) * 128], identity=ident[:])
                return inst
            sc.op("pe", trh, reads=[("A1", "xhat"), "ident"], writes=[bk(bi)])
            sc.op("dve", lambda bi=bi: nc.vector.tensor_tensor(
                out=hTh[:, :, :], in0=bfview(bi)[:, :].rearrange("p (k t) -> p k t", k=8)[:, :, 0:4],
                in1=gT[:, :, None].broadcast_to([128, 8, 4]), op=ALU.mult),
                reads=[bk(bi), "gT"], writes=["hTh"])

            nblk = 0
            nchunk = 0
            for hf in range(2):
                tok0 = hf * 2048
                for tt in range(16):
                    s = tt % 2
                    tg = "A%d" % s
                    r0 = tok0 + tt * 128
                    sc.dma("sp", lambda s=s, r0=r0: nc.sync.dma_start(out=xin[s][:], in_=xb[r0:r0 + 128, :]),
                           writes=[(tg, "xin")])
                    rms_front(stk, tg, xin[s], xhat[s], ss, sq, rstd, tt)
                    bi = nextbank()

                    def tr(s=s, bi=bi):
                        pst = bfview(bi)
                        inst = None
                        for kc in range(8):
                            inst = nc.tensor.transpose(out=pst[:, kc * 128:(kc + 1) * 128],
                                                       in_=xhat[s][:, kc * 128:(kc + 1) * 128], identity=ident[:])
                        return inst
                    sc.op("pe", tr, reads=[(tg, "xhat"), "ident"], writes=[bk(bi)])
                    sc.op("dve", lambda tt=tt, bi=bi: nc.vector.tensor_tensor(
                        out=hT[:, :, tt * 128:(tt + 1) * 128],
                        in0=bfview(bi)[:, :].rearrange("p (k t) -> p k t", k=8),
                        in1=gT[:, :, None].broadcast_to([128, 8, 128]), op=ALU.mult),
                        reads=[bk(bi), "gT"], writes=[("hT", tt)])

                for cb in range(16):
                    wb = nblk % 2
                    nblk += 1
                    for hh in range(2):
                        sc.dma("pool", lambda cb=cb, wb=wb, hh=hh: nc.gpsimd.dma_start(
                            out=wblk[wb][:, 2048 * hh:2048 * (hh + 1)], in_=w_in_r[cb, :, 2048 * hh:2048 * (hh + 1)]),
                            writes=[("wblk", wb, hh)])
                    wkeys = [("wblk", wb, 0), ("wblk", wb, 1)]
                    for tc in range(4):
                        sl = nchunk % 2
                        nchunk += 1
                        c0 = tc * 512
                        bb, bc, bu, bz = 4 * sl, 4 * sl + 1, 4 * sl + 2, 4 * sl + 3
                        if tc == 0:
                            for j, bi in ((1, bb), (2, bz)):
                                def mmh(j=j, bi=bi, wb=wb):
                                    inst = None
                                    for kc in range(8):
                                        off = kc * 512 + j * 128
                                        inst = nc.tensor.matmul(banks[bi][:, 0:4], lhsT=wblk[wb][:, off:off + 128],
                                                                rhs=hTh[:, kc, :], start=(kc == 0), stop=(kc == 7))
                                    return inst
                                sc.op("pe", mmh, reads=wkeys + ["hTh"], writes=[bk(bi)])
                            sc.op("act", lambda bb=bb: nc.scalar.copy(out=hcs[:], in_=banks[bb][:, 0:4]),
                                  reads=[bk(bb)], writes=["hcs"])
                            sc.op("dve", lambda bz=bz, cb=cb, hf=hf: nc.vector.tensor_tensor(
                                out=vhalo[:, cb, :], in0=hcs[:, 2 * hf:2 * hf + 2], in1=banks[bz][:, 2 * hf:2 * hf + 2],
                                op=ALU.mult), reads=["hcs", bk(bz)], writes=[("vhalo", cb)])
                        for j in (1, 2, 0, 3):
                            bi = 4 * sl + j

                            def mm(j=j, bi=bi, wb=wb, c0=c0):
                                inst = None
                                for kc in range(8):
                                    off = kc * 512 + j * 128
                                    inst = nc.tensor.matmul(banks[bi][:, :], lhsT=wblk[wb][:, off:off + 128],
                                                            rhs=hT[:, kc, c0:c0 + 512],
                                                            start=(kc == 0), stop=(kc == 7))
                                return inst
                            sc.op("pe", mm, reads=wkeys + [("hT", 4 * tc + i) for i in range(4)], writes=[bk(bi)])
                        sc.op("act", lambda sl=sl, bc=bc: nc.scalar.copy(out=csb[sl][:], in_=banks[bc][:, :]),
                              reads=[bk(bc)], writes=[("csb", sl)])
                        sc.op("pool", lambda sl=sl, cb=cb: nc.gpsimd.tensor_copy(out=vbuf[sl][:, 0:2], in_=vhalo[:, cb, :]),
                              reads=[("vhalo", cb)], writes=[("vbufh", sl)])
                        sc.op("dve", lambda sl=sl, bu=bu: nc.vector.tensor_tensor(
                            out=vbuf[sl][:, 2:514], in0=csb[sl][:], in1=banks[bu][:, :], op=ALU.mult),
                            reads=[("csb", sl), bk(bu)], writes=[("vbuf", sl)])
                        sc.op("pool", lambda sl=sl, cb=cb: nc.gpsimd.tensor_copy(out=vhalo[:, cb, :], in_=vbuf[sl][:, 512:514]),
                              reads=[("vbuf", sl)], writes=[("vhalo", cb)])
                        sc.op("dve", lambda sl=sl, cb=cb: nc.vector.tensor_scalar(
                            out=tbuf[sl][:], in0=vbuf[sl][:, 0:512], scalar1=cw[:, 3 * cb:3 * cb + 1], scalar2=None,
                            op0=ALU.mult),
                            reads=[("vbuf", sl), ("vbufh", sl), "cw"], writes=[("tbuf", sl)])
                        sc.op("dve", lambda sl=sl, cb=cb: nc.vector.scalar_tensor_tensor(
                            out=tbuf[sl][:], in0=vbuf[sl][:, 1:513], scalar=cw[:, 3 * cb + 1:3 * cb + 2], in1=tbuf[sl][:],
                            op0=ALU.mult, op1=ALU.add),
                            reads=[("vbuf", sl), ("vbufh", sl), ("tbuf", sl)], writes=[("tbuf", sl)])
                        sc.op("dve", lambda sl=sl, cb=cb: nc.vector.scalar_tensor_tensor(
                            out=tbuf[sl][:], in0=vbuf[sl][:, 2:514], scalar=cw[:, 3 * cb + 2:3 * cb + 3], in1=tbuf[sl][:],
                            op0=ALU.mult, op1=ALU.add),
                            reads=[("vbuf", sl), ("tbuf", sl)], writes=[("tbuf", sl)])
                        sc.op("act", lambda sl=sl, bz=bz: nc.scalar.activation(out=szb[sl][:], in_=banks[bz][:, :], func=AF.Silu),
                              reads=[bk(bz)], writes=[("szb", sl)])
                        sc.op("dve", lambda sl=sl, bb=bb: nc.vector.tensor_tensor(
                            out=tbuf[sl][:], in0=tbuf[sl][:], in1=banks[bb][:, :], op=ALU.mult),
                            reads=[("tbuf", sl), bk(bb)], writes=[("tbuf", sl)])
                        sc.op("dve", lambda sl=sl, cb=cb, c0=c0: nc.vector.tensor_tensor(
                            out=yT[:, cb, c0:c0 + 512], in0=tbuf[sl][:], in1=szb[sl][:], op=ALU.mult),
                            reads=[("tbuf", sl), ("szb", sl)], writes=[("yT", cb, tc)])

                for tt in range(16):
                    s = tt % 2
                    tg = "A%d" % s
                    r0 = tok0 + tt * 128
                    sc.dma("sp", lambda s=s, r0=r0: nc.sync.dma_start(out=xin[s][:], in_=xb[r0:r0 + 128, :]),
                           writes=[(tg, "xin")])
                    for nh in range(2):
                        bi = nextbank()

                        def mm2(bi=bi, tt=tt, nh=nh):
                            inst = None
                            for cb in range(16):
                                inst = nc.tensor.matmul(banks[bi][:, :], lhsT=yT[:, cb, tt * 128:(tt + 1) * 128],
                                                        rhs=woutT[:, cb, nh * 512:(nh + 1) * 512],
                                                        start=(cb == 0), stop=(cb == 15))
                            return inst
                        sc.op("pe", mm2, reads=[("yT", cb, tt // 4) for cb in range(16)] + [("woutT", q) for q in range(8)],
                              writes=[bk(bi)])
                        sc.op("dve", lambda s=s, bi=bi, nh=nh: nc.vector.tensor_tensor(
                            out=x1t[s][:, nh * 512:(nh + 1) * 512], in0=banks[bi][:, :],
                            in1=xin[s][:, nh * 512:(nh + 1) * 512], op=ALU.add),
                            reads=[bk(bi), (tg, "xin")], writes=[("x1t", s, nh)])
                    sc.dma("sp", lambda s=s, r0=r0: nc.sync.dma_start(out=x1d[r0:r0 + 128, :], in_=x1t[s][:]),
                           reads=[("x1t", s, 0), ("x1t", s, 1)], writes=[("x1d", r0 // 128)])
            sc.flush()

    if "B" in stages:
        with ExitStack() as stk:
            def sb(name, shape, dtype):
                return stk.enter_context(nc.sbuf_tensor(name, list(shape), dtype))
            wkv = sb("wkv", [128, 8, 1536], BF16)
            wbin = sb("wbin", [128, 8, 2096], BF16)
            gkvT = sb("gkvT_sb", [128, 8], F32)
            gbT = sb("gbT_sb", [128, 8], F32)
            cosT = sb("cosT_sb", [128, 32, 8], F32)
            sinT = sb("sinT_sb", [128, 32, 8], F32)
            xin = [sb("bxin%d" % i, [128, 1024], F32) for i in range(2)]
            xhat = [sb("bxhat%d" % i, [128, 1024], BF16) for i in range(2)]
            ss = sb("bss", [128, 32], F32)
            sq = sb("bsq", [128, 32], F32)
            rstd = sb("brstd", [128, 32], F32)
            hkT = [sb("hkT%d" % i, [128, 8, 128], BF16) for i in range(2)]
            hqT = [sb("hqT%d" % i, [128, 8, 128], BF16) for i in range(2)]
            kcv = [sb("kcv%d" % i, [128, 512], BF16) for i in range(2)]
            kk = [sb("kk%d" % i, [128, 8, 64], BF16) for i in range(2)]
            rt = [sb("rt%d" % i, [128, 8, 8], F32) for i in range(4)]
            kTst = [sb("kTst%d" % i, [128, 8, 512], BF16) for i in range(2)]
            vst = [sb("vst%d" % i, [128, 4, 8, 65], BF16) for i in range(2)]
            qr = [sb("qr%d" % i, [128, 16, 64], BF16) for i in range(2)]
            qc = [sb("qc%d" % i, [128, 16, 64], BF16) for i in range(2)]
            qTst = [sb("qTst%d" % i, [128, 16, 512], BF16) for i in range(2)]
            szst = [sb("szst%d" % i, [128, 1024], F32) for i in range(2)]
            gst = [sb("gst%d" % i, [128, 48], F32) for i in range(2)]

            for q in range(4):
                sc.dma("pool", lambda q=q: nc.gpsimd.dma_start(
                    out=wkv[:, 2 * q:2 * q + 2, :],
                    in_=wkv_r[:, 3072 * q:3072 * (q + 1)].rearrange("p (a n) -> p a n", a=2)),
                    writes=[("wkv", q)])
            for q in range(8):
                sc.dma("pool", lambda q=q: nc.gpsimd.dma_start(
                    out=wbin[:, q, :].rearrange("p (a n) -> p a n", a=2),
                    in_=wbin_r[:, 2096 * q:2096 * (q + 1)].rearrange("p (a n) -> p a n", a=2)), writes=[("wbin", q)])
            wkv_keys = [("wkv", q) for q in range(4)]
            wbin_keys = [("wbin", q) for q in range(8)]
            sc.dma("sp", lambda: nc.sync.dma_start(out=gkvT[:], in_=gkvT_d[:, :]), writes=["gkvT"])
            sc.dma("sp", lambda: nc.sync.dma_start(out=gbT[:], in_=gbT_d[:, :]), writes=["gbT"])
            sc.dma("sp", lambda: nc.sync.dma_start(out=cosT[:], in_=cos_d[:, :].rearrange("p (t f) -> p t f", f=8)),
                   writes=["cosT"])
            sc.dma("sp", lambda: nc.sync.dma_start(out=sinT[:], in_=sin_d[:, :].rearrange("p (t f) -> p t f", f=8)),
                   writes=["sinT"])
            for i in range(2):
                sc.op("pool", lambda i=i: nc.gpsimd.memset(vst[i][:], 1.0), writes=[("vst", i)])

            def rope(src_bank, col0, nh, dst, tt, tagk, wkey):
                psv = banks[src_bank][:, col0:col0 + nh * 64].rearrange("p (h d) -> p h d", d=64)
                cs = cosT[:, tt:tt + 1, :].broadcast_to([128, nh, 8])
                sn = sinT[:, tt:tt + 1, :].broadcast_to([128, nh, 8])
                rd = [bk(src_bank), "cosT", "sinT"]
                for t_i, (lo, tab) in enumerate(((0, cs), (8, sn), (8, cs), (0, sn))):
                    sc.op("dve", lambda t_i=t_i, lo=lo, tab=tab: nc.vector.tensor_tensor(
                        out=rt[t_i][:, 0:nh, :], in0=psv[:, :, lo:lo + 8], in1=tab, op=ALU.mult),
                        reads=rd, writes=[("rt", t_i)])
                if _DBG.get("r1"):
                    return []
                sc.op("dve", lambda: nc.vector.tensor_tensor(
                    out=dst[:, 0:nh, 0:8], in0=rt[0][:, 0:nh, :], in1=rt[1][:, 0:nh, :], op=ALU.subtract),
                    reads=[("rt", 0), ("rt", 1)], writes=[wkey + ("a",)])
                sc.op("dve", lambda: nc.vector.tensor_tensor(
                    out=dst[:, 0:nh, 8:16], in0=rt[2][:, 0:nh, :], in1=rt[3][:, 0:nh, :], op=ALU.add),
                    reads=[("rt", 2), ("rt", 3)], writes=[wkey + ("b",)])
                if _DBG.get("r2"):
                    return []
                sc.op("dve", lambda: nc.vector.tensor_copy(out=dst[:, 0:nh, 16:64], in_=psv[:, :, 16:64]),
                      reads=[bk(src_bank)], writes=[wkey + ("c",)])
                return [wkey + ("a",), wkey + ("b",), wkey + ("c",)]

            def proj(bi, act, wt, c0, ncols):
                def f():
                    inst = None
                    for kc in range(8):
                        inst = nc.tensor.matmul(banks[bi][:, 0:ncols], lhsT=act[:, kc, :],
                                                rhs=wt[:, kc, c0:c0 + ncols], start=(kc == 0), stop=(kc == 7))
                    return inst
                return f

            def front(ch, i):
                own = ch < 4
                tt = 4 * ch + i
                s = tt % 2
                tg = "B%d" % s
                r0 = tt * 128
                sc.dma("sp", lambda: nc.sync.dma_start(out=xin[s][:], in_=x1d[r0:r0 + 128, :]),
                       reads=[("x1d", tt)], writes=[(tg, "xin")])
                rms_front(stk, tg, xin[s], xhat[s], ss, sq, rstd, tt)
                bi = nextbank()

                def tr():
                    pst = bfview(bi)
                    inst = None
                    for kc in range(8):
                        inst = nc.tensor.transpose(out=pst[:, kc * 128:(kc + 1) * 128],
                                                   in_=xhat[s][:, kc * 128:(kc + 1) * 128], identity=ident[:])
                    return inst
                sc.op("pe", tr, reads=[(tg, "xhat"), "ident"], writes=[bk(bi)])
                sc.op("dve", lambda: nc.vector.tensor_tensor(
                    out=hkT[s][:, :, :], in0=bfview(bi)[:, :].rearrange("p (k t) -> p k t", k=8),
                    in1=gkvT[:, :, None].broadcast_to([128, 8, 128]), op=ALU.mult),
                    reads=[bk(bi), "gkvT"], writes=[("hkT", s)])
                if own:
                    sc.op("dve", lambda: nc.vector.tensor_tensor(
                        out=hqT[s][:, :, :], in0=bfview(bi)[:, :].rearrange("p (k t) -> p k t", k=8),
                        in1=gbT[:, :, None].broadcast_to([128, 8, 128]), op=ALU.mult),
                        reads=[bk(bi), "gbT"], writes=[("hqT", s)])

            def back(ch, i):
                own = ch < 4
                cs_ = ch % 2
                tt = 4 * ch + i
                s = tt % 2
                r0 = tt * 128
                b0 = nextbank()
                sc.op("pe", proj(b0, hkT[s], wkv, 0, 512), reads=[("hkT", s)] + wkv_keys, writes=[bk(b0)])
                sc.op("act", lambda: nc.scalar.copy(out=kcv[s][:], in_=banks[b0][:, :]),
                      reads=[bk(b0)], writes=[("kcv", s)])
                b1 = nextbank()
                sc.op("pe", proj(b1, hkT[s], wkv, 512, 512), reads=[("hkT", s)] + wkv_keys, writes=[bk(b1)])
                kkeys = rope(b1, 0, 8, kk[s], tt, "kk", ("kk", s))
                b2 = nextbank()
                sc.op("pe", proj(b2, hkT[s], wkv, 1024, 512), reads=[("hkT", s)] + wkv_keys, writes=[bk(b2)])
                sc.op("act", lambda: nc.scalar.copy(
                    out=vst[cs_][:, i, :, 0:64], in_=banks[b2][:, :].rearrange("p (a d) -> p a d", d=64)),
                    reads=[bk(b2)], writes=[("vst", cs_)])
                if own:
                    for qh in range(2):
                        bq = nextbank()
                        sc.op("pe", proj(bq, hqT[s], wbin, 512 * qh, 512), reads=[("hqT", s)] + wbin_keys, writes=[bk(bq)])
                        sc.op("act", lambda bq=bq, qh=qh: nc.scalar.copy(
                            out=qc[s][:, 8 * qh:8 * qh + 8, :], in_=banks[bq][:, :].rearrange("p (h d) -> p h d", d=64)),
                            reads=[bk(bq)], writes=[("qc", s, qh)])
                        rope(bq, 0, 8, qr[s][:, 8 * qh:8 * qh + 8, :], tt, "q", ("qr", s, qh))
                    for zh in range(2):
                        bz = nextbank()
                        sc.op("pe", proj(bz, hqT[s], wbin, 1024 + 512 * zh, 512), reads=[("hqT", s)] + wbin_keys, writes=[bk(bz)])
                        sc.op("act", lambda bz=bz, zh=zh: nc.scalar.activation(
                            out=szst[s][:, 512 * zh:512 * (zh + 1)], in_=banks[bz][:, :], func=AF.Sigmoid),
                            reads=[bk(bz)], writes=[("szst", s, zh)])
                        sc.op("dve", lambda bz=bz, zh=zh: nc.vector.tensor_tensor(
                            out=szst[s][:, 512 * zh:512 * (zh + 1)], in0=banks[bz][:, :],
                            in1=szst[s][:, 512 * zh:512 * (zh + 1)], op=ALU.mult),
                            reads=[bk(bz), ("szst", s, zh)], writes=[("szst", s, zh)])
                    bg = nextbank()
                    sc.op("pe", proj(bg, hqT[s], wbin, 2048, 48), reads=[("hqT", s)] + wbin_keys, writes=[bk(bg)])
                    sc.op("act", lambda: nc.scalar.activation(out=gst[s][:], in_=banks[bg][:, 0:48], func=AF.Sigmoid),
                          reads=[bk(bg)], writes=[("gst", s)])
                    sc.dma("sp", lambda: nc.sync.dma_start(out=szd[r0:r0 + 128, :], in_=szst[s][:]),
                           reads=[("szst", s, 0), ("szst", s, 1)], writes=[("szd", tt)])
                    sc.dma("sp", lambda: nc.sync.dma_start(out=gated[r0:r0 + 128, :], in_=gst[s][:]),
                           reads=[("gst", s)], writes=[("gated", tt)])
                for half, src, skeys in ((0, kcv[s], [("kcv", s)]), (1, kk[s], kkeys)):
                    bt = nextbank()

                    def trk(bt=bt, src=src, half=half):
                        pst = bfview(bt)
                        sv = src[:, :] if half == 0 else src[:, :, :].rearrange("p a d -> p (a d)")
                        inst = None
                        for a in range(4):
                            inst = nc.tensor.transpose(out=pst[:, a * 128:(a + 1) * 128],
                                                       in_=sv[:, a * 128:(a + 1) * 128], identity=ident[:])
                        return inst
                    sc.op("pe", trk, reads=skeys + ["ident"], writes=[bk(bt)])
                    sc.op("act" if half == 0 else "dve", (lambda bt=bt, half=half: (
                        nc.scalar.copy if half == 0 else nc.vector.tensor_copy)(
                        out=kTst[cs_][:, 4 * half:4 * half + 4, i * 128:(i + 1) * 128],
                        in_=bfview(bt)[:, 0:512].rearrange("p (a t) -> p a t", a=4))),
                        reads=[bk(bt)], writes=[("kTst", cs_, half, i)])
                if own:
                    for which, src, skeys in ((0, qc[s], [("qc", s, 0), ("qc", s, 1)]),
                                              (1, qr[s], [("qr", s, qh, x) for qh in range(2) for x in "abc"])):
                        bt = nextbank()

                        def trq(bt=bt, src=src):
                            pst = bfview(bt)
                            sv = src[:, :, :].rearrange("p h d -> p (h d)")
                            inst = None
                            for a in range(8):
                                inst = nc.tensor.transpose(out=pst[:, a * 128:(a + 1) * 128],
                                                           in_=sv[:, a * 128:(a + 1) * 128], identity=ident[:])
                            return inst
                        sc.op("pe", trq, reads=skeys + ["ident"], writes=[bk(bt)])
                        a0 = 8 * which
                        sc.op("dve" if which == 0 else "act", (lambda bt=bt, which=which, a0=a0: (
                            nc.vector.tensor_copy if which == 0 else nc.scalar.copy)(
                            out=qTst[cs_][:, a0:a0 + 8, i * 128:(i + 1) * 128],
                            in_=bfview(bt)[:, :].rearrange("p (a t) -> p a t", a=8))),
                            reads=[bk(bt)], writes=[("qTst", cs_, which, i)])

            def chunk_store(ch):
                own = ch < 4
                cs_ = ch % 2
                c0 = ch * 512
                sc.dma("sp", lambda: nc.sync.dma_start(
                    out=kT4[:, :, c0:c0 + 512].rearrange("(a gl) d t -> (gl d) a t", gl=2), in_=kTst[cs_][:, :, :]),
                    reads=[("kTst", cs_, h_, i_) for h_ in range(2) for i_ in range(4)], writes=[("kT4", ch)])
                sc.dma("sp", lambda: nc.sync.dma_start(
                    out=vtok[c0:c0 + 512, :].rearrange("(i p) f -> p i f", p=128),
                    in_=vst[cs_][:, :, :, :].rearrange("p i a d -> p i (a d)")),
                    reads=[("vst", cs_)], writes=[("vtok", ch)])
                if own:
                    sc.dma("sp", lambda: nc.sync.dma_start(
                        out=qT[:, :, c0:c0 + 512].rearrange("(a hl) d t -> (hl d) a t", hl=2), in_=qTst[cs_][:, :, :]),
                        reads=[("qTst", cs_, w_, i_) for w_ in range(2) for i_ in range(4)],
                        writes=[("qT", ch)])

            tiles = [(ch, i) for ch in range(8) for i in range(4)]
            front(*tiles[0])
            for n_, (ch, i) in enumerate(tiles):
                if n_ + 1 < len(tiles):
                    front(*tiles[n_ + 1])
                back(ch, i)
                if i == 3:
                    chunk_store(ch)
            sc.flush()

    if "C" in stages:
        with ExitStack() as stk:
            def sb(name, shape, dtype):
                return stk.enter_context(nc.sbuf_tensor(name, list(shape), dtype))
            kcmpT = sb("kcmpT", [128, 4, 256], BF16)
            vcmp = sb("vcmp", [128, 2, 4, 64], BF16)
            AI = sb("AI", [128, 2, 64], BF16)
            sc.dma("pool", lambda: nc.gpsimd.dma_start(out=AI[:], in_=Aaug_d[:, :].rearrange("p (a c) -> p a c", a=2)[:, :, 0:64]),
                   writes=["AI"])
            all_kT4 = [("kT4", ch) for ch in range(8)]

            with ExitStack() as stk2:
                def sb2(name, shape, dtype):
                    return stk2.enter_context(nc.sbuf_tensor(name, list(shape), dtype))
                kcT = sb2("kcT", [128, 4, 4128], BF16)
                w1 = sb2("w1", [128, 16, 256], BF16)
                w2 = sb2("w2", [128, 2, 128], BF16)
                posT = sb2("posT", [128, 16], BF16)
                hidT = sb2("hidT", [128, 2, 4, 256], BF16)
                pb = sb2("pb", [128, 2], F32)
                ub = sb2("ub", [128, 512], F32)
                tb = sb2("tb", [128, 512], F32)
                sg = sb2("sg", [128, 512], F32)
                for st in range(2):
                    for g in range(4):
                        sc.dma("sp", lambda st=st, g=g: nc.sync.dma_start(out=kcT[0:64, g, 0:4096], in_=kT4[4 * st + g, :, :]),
                               reads=all_kT4, writes=[("kcT", g)])
                        sc.dma("sp", lambda st=st, g=g: nc.sync.dma_start(out=kcT[0:64, g, 4096:4128], in_=kT4[4 * st + g, :, 0:32]),
                               reads=all_kT4, writes=[("kcTw", g)])
                        sc.dma("sp", lambda st=st, g=g: nc.sync.dma_start(out=kcT[64:128, g, 0:4095], in_=kT4[4 * st + g, :, 1:4096]),
                               reads=all_kT4, writes=[("kcT2", g)])
                        sc.dma("sp", lambda st=st, g=g: nc.sync.dma_start(out=kcT[64:128, g, 4095:4127], in_=kT4[4 * st + g, :, 0:32]),
                               reads=all_kT4, writes=[("kcT2w", g)])
                    w1d = w1k_d if st == 0 else w1v_d
                    w2d = w2k_d if st == 0 else w2v_d
                    posd = posk_d if st == 0 else posv_d
                    for q in range(4):
                        sc.dma("pool", lambda q=q, w1d=w1d: nc.gpsimd.dma_start(
                            out=w1[:, 4 * q:4 * q + 4, :], in_=w1d[:, 1024 * q:1024 * (q + 1)].rearrange("p (l n) -> p l n", n=256)),
                            writes=[("w1", q)])
                    sc.dma("pool", lambda w2d=w2d: nc.gpsimd.dma_start(out=w2[:], in_=w2d[:, :].rearrange("p (a n) -> p a n", a=2)),
                           writes=["w2"])
                    sc.dma("pool", lambda posd=posd: nc.gpsimd.dma_start(out=posT[:], in_=posd[:, :]), writes=["posT"])
                    w1keys = [("w1", q) for q in range(4)]
                    kckeys = [(nm, g) for g in range(4) for nm in ("kcT", "kcTw", "kcT2", "kcT2w")]
                    for hc in range(2):
                        bi = nextbank()

                        def mpb(bi=bi, hc=hc):
                            inst = None
                            for l in range(16):
                                inst = nc.tensor.matmul(banks[bi][:, 0:1], lhsT=w1[:, l, hc * 128:(hc + 1) * 128],
                                                        rhs=posT[:, l:l + 1], start=(l == 0), stop=(l == 15))
                            return inst
                        sc.op("pe", mpb, reads=w1keys + ["posT"], writes=[bk(bi)])
                        sc.op("act", lambda bi=bi, hc=hc: nc.scalar.copy(out=pb[:, hc:hc + 1], in_=banks[bi][:, 0:1]),
                              reads=[bk(bi)], writes=[("pb", hc)])
                        for gp in range(2):
                            bh = nextbank()

                            def mh(bh=bh, hc=hc, gp=gp):
                                inst = None
                                for l in range(16):
                                    inst = nc.tensor.matmul(banks[bh][:, :], lhsT=w1[:, l, hc * 128:(hc + 1) * 128],
                                                            rhs=kcT[:, 2 * gp:2 * gp + 2, 2 * l:2 * l + 4096:16],
                                                            start=(l == 0), stop=(l == 15))
                                return inst
                            sc.op("pe", mh, reads=w1keys + kckeys, writes=[bk(bh)])
                            sc.op("act", lambda bh=bh, hc=hc: nc.scalar.activation(
                                out=ub[:], in_=banks[bh][:, :], func=AF.Identity, bias=pb[:, hc:hc + 1]),
                                reads=[bk(bh), ("pb", hc)], writes=["ub"])
                            sc.op("dve", lambda: nc.vector.tensor_tensor(out=tb[:], in0=ub[:], in1=ub[:], op=ALU.mult),
                                  reads=["ub"], writes=["tb"])
                            sc.op("dve", lambda: nc.vector.tensor_scalar(out=tb[:], in0=tb[:], scalar1=0.044715, scalar2=1.0,
                                                                         op0=ALU.mult, op1=ALU.add),
                                  reads=["tb"], writes=["tb"])
                            sc.op("dve", lambda: nc.vector.tensor_tensor(out=tb[:], in0=tb[:], in1=ub[:], op=ALU.mult),
                                  reads=["tb", "ub"], writes=["tb"])
                            sc.op("act", lambda: nc.scalar.activation(out=sg[:], in_=tb[:], func=AF.Sigmoid, scale=1.5957691216057308),
                                  reads=["tb"], writes=["sg"])
                            sc.op("dve", lambda hc=hc, gp=gp: nc.vector.tensor_tensor(
                                out=hidT[:, hc, 2 * gp:2 * gp + 2, :], in0=ub[:, :].rearrange("p (a m) -> p a m", a=2),
                                in1=sg[:, :].rearrange("p (a m) -> p a m", a=2), op=ALU.mult),
                                reads=["ub", "sg"], writes=[("hidT", hc, gp)])
                    hkeys = [("hidT", hc, gp) for hc in range(2) for gp in range(2)]
                    if st == 0:
                        for gp in range(2):
                            bo = nextbank()

                            def mk(bo=bo, gp=gp):
                                inst = None
                                for hc in range(2):
                                    inst = nc.tensor.matmul(banks[bo][:, :], lhsT=w2[:, hc, :],
                                                            rhs=hidT[:, hc, 2 * gp:2 * gp + 2, :], start=(hc == 0), stop=(hc == 1))
                                return inst
                            sc.op("pe", mk, reads=hkeys + ["w2"], writes=[bk(bo)])
                            sc.op("act", lambda bo=bo, gp=gp: nc.scalar.copy(
                                out=kcmpT[:, 2 * gp:2 * gp + 2, :], in_=banks[bo][:, :].rearrange("p (a m) -> p a m", a=2)),
                                reads=[bk(bo)], writes=[("kcmpT", gp)])
                    else:
                        for nt in range(2):
                            bo = nextbank()

                            def mv(bo=bo, nt=nt):
                                inst = None
                                for g in range(4):
                                    for hc in range(2):
                                        inst = nc.tensor.matmul(banks[bo][:, g * 64:(g + 1) * 64],
                                                                lhsT=hidT[:, hc, g, nt * 128:(nt + 1) * 128],
                                                                rhs=w2[:, hc, 0:64], start=(g == 0 and hc == 0), stop=(hc == 1),
                                                                skip_group_check=True)
                                return inst
                            sc.op("pe", mv, reads=hkeys + ["w2"], writes=[bk(bo)])
                            sc.op("act", lambda bo=bo, nt=nt: nc.scalar.copy(
                                out=vcmp[:, nt, :, :], in_=banks[bo][:, 0:256].rearrange("p (g d) -> p g d", g=4)),
                                reads=[bk(bo)], writes=[("vcmp", nt)])
                sc.flush()

            Ksel = sb("Ksel", [128, 4, 4096], BF16)
            KwT = sb("KwT", [128, 4, 4096], BF16)
            Vall = sb("Vall", [128, 32, 520], BF16)
            cmask = sb("cmask_sb", [128, 2, 2048], BF16)
            fmul = sb("fmul_sb", [128, 16, 64], F32)
            fadd = sb("fadd_sb", [128, 16, 64], F32)
            tri = sb("tri_sb", [128, 128], BF16)
            upp = sb("upp_sb", [128, 128], BF16)
            woutb = sb("woutb_sb", [128, 8, 1024], BF16)
            gfin = sb("gfin_sb", [128, 1024], F32)
            Qsel = [sb("Qsel%d" % i, [128, 16, 128], BF16) for i in range(2)]
            Qwin = [sb("Qwin%d" % i, [128, 16, 128], BF16) for i in range(2)]
            Qc = [sb("Qc%d" % i, [128, 16, 128], BF16) for i in range(2)]
            gts = [sb("gts%d" % i, [128, 16, 3], F32) for i in range(2)]
            szs = sb("szs", [128, 1024], F32)
            x1s = sb("x1s", [128, 1024], F32)
            oacc2 = [sb("oacc%d" % i, [128, 16, 64], F32) for i in range(2)]
            otmp = [sb("otmp%d" % i, [128, 4, 64], F32) for i in range(2)]
            NPT = 4
            PTb = [sb("PTb%d" % i, [128, 512], BF16) for i in range(NPT)]
            rs = [sb("rs%d" % i, [128, 4], F32) for i in range(3)]
            rc = [sb("rc%d" % i, [128, 4], F32) for i in range(3)]
            wg = [sb("wg%d" % i, [128, 4], F32) for i in range(3)]
            impb = sb("impb", [128, 64], F32)
            scr = sb("scr", [128, 64], F32)
            scr2 = sb("scr2", [128, 64], F32)
            m8a = sb("m8a", [128, 8], F32)
            m8b = sb("m8b", [128, 8], F32)
            thr = sb("thr", [128, 1], F32)
            negb4 = [sb("negb%d" % i, [128, 128], BF16) for i in range(4)]
            ybf = sb("ybf", [128, 1024], BF16)
            yT = sb("cyT", [128, 8, 128], BF16)
            x2 = sb("x2", [128, 1024], F32)
            fss = sb("fss", [128, 16], F32)
            fsq = sb("fsq", [128, 16], F32)
            frs = sb("frs", [128, 16], F32)
            junk = sb("junk", [128, 1024], BF16)
            osb = sb("osb", [128, 1024], F32)
            identF = sb("identF", [128, 128], F32)
            accT = [sb("accT%d" % i, [128, 512], F32) for i in range(2)]
            sc.dma("sp", lambda: nc.sync.dma_start(out=identF[:], in_=ident_d[:, :]), writes=["identF"])
            for i in range(2):
                sc.op("pool", lambda i=i: nc.gpsimd.memset(accT[i][:], 0.0), writes=[("accT", i)])

            for g in range(4):
                sc.dma("sp", lambda g=g: nc.sync.dma_start(out=Ksel[0:64, g, :], in_=kT4[8 + g, :, :]),
                       reads=all_kT4, writes=[("Ksel", g)])
                sc.op("pool", lambda g=g: nc.gpsimd.memset(KwT[64:128, g, :], 0.0), writes=[("KwTz", g)])
                sc.dma("sp", lambda g=g: nc.sync.dma_start(out=KwT[0:64, g, :], in_=kT4[12 + g, :, :]),
                       reads=all_kT4, writes=[("KwT", g)])
                for hh in range(2):
                    sc.dma("pool", lambda g=g, hh=hh: nc.gpsimd.dma_start(
                        out=Ksel[64:128, g, 2048 * hh:2048 * (hh + 1)], in_=E_d[:, 2048 * hh:2048 * (hh + 1)]),
                        writes=[("KselE", g, hh)])
                    sc.dma("pool", lambda g=g, hh=hh: nc.gpsimd.dma_start(
                        out=KwT[64:65, g, 2048 * hh:2048 * (hh + 1)], in_=kbias_d[:, 2048 * hh:2048 * (hh + 1)]),
                        reads=[("KwTz", g)], writes=[("KwTb", g, hh)])
            for ch in range(8):
                sc.dma("sp", lambda ch=ch: nc.sync.dma_start(
                    out=Vall[:, 4 * ch:4 * ch + 4, :], in_=vtok[512 * ch:512 * (ch + 1), :].rearrange("(i p) f -> p i f", p=128)),
                    reads=[("vtok", ch)], writes=[("Vall", ch)])
            for hh in range(2):
                sc.dma("pool", lambda hh=hh: nc.gpsimd.dma_start(out=cmask[:, hh, :], in_=cmask_d[:, 2048 * hh:2048 * (hh + 1)]),
                       writes=[("cmask", hh)])
            sc.dma("sp", lambda: nc.sync.dma_start(out=fmul[:], in_=fmul_d[:, :].rearrange("p (t j) -> p t j", j=64)), writes=["fmul"])
            sc.dma("sp", lambda: nc.sync.dma_start(out=fadd[:], in_=fadd_d[:, :].rearrange("p (t j) -> p t j", j=64)), writes=["fadd"])
            sc.dma("pool", lambda: nc.gpsimd.dma_start(out=tri[:], in_=tri_d[:, :]), writes=["tri"])
            sc.dma("pool", lambda: nc.gpsimd.dma_start(out=upp[:], in_=upp_d[:, :]), writes=["upp"])
            for q in range(8):
                sc.dma("pool", lambda q=q: nc.gpsimd.dma_start(out=woutb[:, q, :], in_=woutb_d[:, 1024 * q:1024 * (q + 1)]),
                       writes=[("woutb", q)])
            sc.dma("sp", lambda: nc.sync.dma_start(out=gfin[:], in_=gfin_d[:, :]), writes=["gfin"])
            for i in range(2):
                sc.op("pool", lambda i=i: nc.gpsimd.memset(Qwin[i][64:128, :, :], 0.0), writes=[("Qwin1", i)])
                sc.op("pool", lambda i=i: nc.gpsimd.memset(Qwin[i][64:65, :, :], 1.0), reads=[("Qwin1", i)], writes=[("Qwin1", i)])
                sc.op("pool", lambda i=i: nc.gpsimd.memset(Qc[i][64:128, :, :], 0.0), writes=[("Qc0", i)])
            for i in range(4):
                sc.op("pool", lambda i=i: nc.gpsimd.memset(negb4[i][:, 0:64], 0.0), writes=[("negb0", i)])
            ksel_keys = lambda g: [("Ksel", g), ("KselE", g, 0), ("KselE", g, 1)]
            kw_keys = lambda g: [("KwT", g), ("KwTz", g), ("KwTb", g, 0), ("KwTb", g, 1)]
            sbank = [0]
            ptc = [0]

            def emit_score(u):
                bS = sbank[0] % 3
                sbank[0] += 1
                pi = ptc[0] % NPT
                ptc[0] += 1
                lhsT, rhs, rkeys, mask, mkeys = u["score"]
                sc.op("pe", lambda: nc.tensor.matmul(banks[bS][:, :], lhsT=lhsT, rhs=rhs, start=True, stop=True),
                      reads=rkeys, writes=[bk(bS)])
                sc.op("act", lambda: nc.scalar.activation(out=PTb[pi][:], in_=banks[bS][:, :], func=AF.Exp, scale=0.125),
                      reads=[bk(bS)], writes=[("PT", pi)])
                if mask is not None:
                    sc.op("dve", lambda: nc.vector.tensor_tensor(
                        out=PTb[pi][:, :].rearrange("p (h q) -> p h q", h=4),
                        in0=PTb[pi][:, :].rearrange("p (h q) -> p h q", h=4),
                        in1=mask, op=ALU.mult), reads=[("PT", pi)] + list(mkeys), writes=[("PT", pi)])
                return pi

            def emit_pv(u, pi):
                for pv_ in u["pvs"]:
                    if pv_[0] == "vstat":
                        _, accb, lhsT_, rkeys, first, last = pv_
                        sc.op("pe", lambda accb=accb, lhsT_=lhsT_, first=first, last=last: nc.tensor.matmul(
                            banks[accb][0:65, :], lhsT=lhsT_, rhs=PTb[pi][:, :], start=first, stop=last),
                            reads=[("PT", pi)] + rkeys, writes=[bk(accb)])
                        continue
                    (accb, col0, width, rhs_, rkeys, first, last) = pv_

                    def f(accb=accb, col0=col0, width=width, rhs_=rhs_, first=first, last=last):
                        inst = None
                        for h in range(4):
                            inst = nc.tensor.matmul(banks[accb][:, col0 + h * width:col0 + (h + 1) * width],
                                                    lhsT=PTb[pi][:, h * 128:(h + 1) * 128], rhs=rhs_,
                                                    start=(first and h == 0), stop=last, skip_group_check=True)
                        return inst
                    sc.op("pe", f, reads=[("PT", pi)] + rkeys, writes=[bk(accb)])

            def untranspose(accb, ai, then):
                sc.op("dve", lambda: nc.vector.tensor_copy(out=accT[ai][0:65, :], in_=banks[accb][0:65, :]),
                      reads=[bk(accb), ("accT", ai)], writes=[("accT", ai)])

                def tail():
                    def tr4():
                        inst = None
                        for h in range(4):
                            inst = nc.tensor.transpose(out=banks[accb][:, h * 128:(h + 1) * 128],
                                                       in_=accT[ai][:, h * 128:(h + 1) * 128], identity=identF[:])
                        return inst
                    sc.op("pe", tr4, reads=[("accT", ai), "identF"], writes=[bk(accb)])
                    then()
                deferred.append([_DBG.get("dunt", 4), tail])

            def branch_out(accb, g, gidx, gt, slot, stride=65, ob=0):
                oacc = oacc2[ob]
                av = banks[accb][:, 0:4 * stride].rearrange("p (h c) -> p h c", c=stride)
                r_, c_, w_ = rs[slot], rc[slot], wg[slot]
                sc.op("dve", lambda: nc.vector.tensor_scalar(out=r_[:], in0=av[:, :, 64], scalar1=1e-30, scalar2=None,
                                                             op0=ALU.max), reads=[bk(accb)], writes=[("rs", slot)])
                sc.op("dve", lambda: nc.vector.reciprocal(out=c_[:], in_=r_[:]), reads=[("rs", slot)], writes=[("rc", slot)])
                sc.op("dve", lambda: nc.vector.tensor_tensor(out=w_[:], in0=c_[:], in1=gt[:, 4 * g:4 * g + 4, gidx], op=ALU.mult),
                      reads=[("rc", slot), "gts"], writes=[("wg", slot)])
                ot = otmp[slot - 1]
                sc.op("dve", lambda: nc.vector.tensor_tensor(
                    out=ot[:], in0=av[:, :, 0:64], in1=w_[:, :, None].broadcast_to([128, 4, 64]), op=ALU.mult),
                    reads=[bk(accb), ("wg", slot)], writes=[("otmp", slot)])
                sc.op("pool", lambda: nc.gpsimd.tensor_tensor(
                    out=oacc[:, 4 * g:4 * g + 4, :], in0=oacc[:, 4 * g:4 * g + 4, :], in1=ot[:], op=ALU.add),
                    reads=[("otmp", slot), ("oacc", ob, g)], writes=[("oacc", ob, g)])

            def select_chain(qs, g, qb):
                gt = gts[qb]
                oacc = oacc2[qb]
                negb = negb4[g]
                iv = banks[3][:, 0:256].rearrange("p (h c) -> p h c", c=64)
                r_, c_, w_ = rs[0], rc[0], wg[0]
                sc.op("dve", lambda: nc.vector.reduce_sum(out=r_[:], in_=iv, axis=mybir.AxisListType.X),
                      reads=[bk(3)], writes=[("rs", 0)])
                sc.op("dve", lambda: nc.vector.tensor_scalar(out=r_[:], in0=r_[:], scalar1=0.5, scalar2=1e-30,
                                                             op0=ALU.mult, op1=ALU.max), reads=[("rs", 0)], writes=[("rs", 0)])
                sc.op("dve", lambda: nc.vector.reciprocal(out=c_[:], in_=r_[:]), reads=[("rs", 0)], writes=[("rc", 0)])
                sc.op("dve", lambda: nc.vector.tensor_scalar(out=impb[:], in0=iv[:, 0, 0:64], scalar1=c_[:, 0:1],
                                                             scalar2=None, op0=ALU.mult),
                      reads=[bk(3), ("rc", 0)], writes=["impb"])
                for h in range(1, 4):
                    sc.op("dve", lambda h=h: nc.vector.scalar_tensor_tensor(
                        out=impb[:], in0=iv[:, h, 0:64], scalar=c_[:, h:h + 1], in1=impb[:], op0=ALU.mult, op1=ALU.add),
                        reads=[bk(3), ("rc", 0), "impb"], writes=["impb"])
                sc.op("dve", lambda: nc.vector.tensor_tensor(out=w_[:], in0=c_[:], in1=gt[:, 4 * g:4 * g + 4, 0], op=ALU.mult),
                      reads=[("rc", 0), "gts"], writes=[("wg", 0)])
                sc.op("dve", lambda: nc.vector.tensor_tensor(
                    out=oacc[:, 4 * g:4 * g + 4, :], in0=banks[3][:, 256:512].rearrange("p (h c) -> p h c", c=64),
                    in1=w_[:, :, None].broadcast_to([128, 4, 64]), op=ALU.mult),
                    reads=[bk(3), ("wg", 0)], writes=[("oacc", qb, g)])
                sc.op("dve", lambda: nc.vector.tensor_tensor(out=scr[:], in0=impb[:], in1=fmul[:, qs, :], op=ALU.mult),
                      reads=["impb", "fmul"], writes=["scr"])
                sc.op("dve", lambda: nc.vector.tensor_tensor(out=scr[:], in0=scr[:], in1=fadd[:, qs, :], op=ALU.add),
                      reads=["scr", "fadd"], writes=["scr"])
                sc.op("dve", lambda: nc.vector.max(out=m8a[:], in_=scr[:]), reads=["scr"], writes=["m8a"])
                sc.op("dve", lambda: nc.vector.match_replace(out=scr2[:], in_to_replace=m8a[:], in_values=scr[:], imm_value=-2e9),
                      reads=["scr", "m8a"], writes=["scr2"])
                sc.op("dve", lambda: nc.vector.max(out=m8b[:], in_=scr2[:]), reads=["scr2"], writes=["m8b"])
                sc.op("dve", lambda: nc.vector.tensor_scalar(out=thr[:], in0=m8b[:, 7:8], scalar1=-1e8, scalar2=None, op0=ALU.max),
                      reads=["m8b"], writes=["thr"])
                sc.op("dve", lambda: nc.vector.tensor_scalar(out=negb[:, 64:128], in0=scr[:], scalar1=thr[:, 0:1], scalar2=NEG,
                                                             op0=ALU.is_lt, op1=ALU.mult),
                      reads=["scr", "thr"], writes=[("negb", g)])

                def tail():
                    sc.op("pe", lambda: nc.tensor.transpose(out=bfview(7)[:, 0:128], in_=negb[:, :], identity=ident[:]),
                          reads=[("negb", g), ("negb0", g), "ident"], writes=[bk(7)])
                    sc.op("dve", lambda: nc.vector.tensor_copy(
                        out=Qsel[qb][64:128, 4 * g:4 * g + 4, :], in_=bfview(7)[64:128, None, 0:128].broadcast_to([64, 4, 128])),
                        reads=[bk(7)], writes=[("QselB", qb, g)])
                deferred.append([_DBG.get("dsel", 14), tail])

            def load_q(qs):
                qb = qs % 2
                q0 = qs * 128
                sc.dma("sp", lambda: nc.sync.dma_start(
                    out=Qsel[qb][0:64, :, :], in_=qT[16:32, :, q0:q0 + 128].rearrange("a d t -> d a t")),
                    reads=[("qT", qs // 4)], writes=[("QselQ", qb)])
                sc.dma("sp", lambda: nc.sync.dma_start(
                    out=Qwin[qb][0:64, :, :], in_=qT[16:32, :, q0:q0 + 128].rearrange("a d t -> d a t")),
                    reads=[("qT", qs // 4)], writes=[("QwinQ", qb)])
                sc.dma("sp", lambda: nc.sync.dma_start(
                    out=Qc[qb][0:64, :, :], in_=qT[0:16, :, q0:q0 + 128].rearrange("a d t -> d a t")),
                    reads=[("qT", qs // 4)], writes=[("Qc", qb)])
                sc.dma("sp", lambda: nc.sync.dma_start(
                    out=gts[qb][:, :, :], in_=gated[q0:q0 + 128, :].rearrange("p (h c) -> p h c", c=3)),
                    reads=[("gated", qs)], writes=["gts"])

            def load_tail(qs):
                q0 = qs * 128
                sc.dma("sp", lambda: nc.sync.dma_start(out=szs[:], in_=szd[q0:q0 + 128, :]),
                       reads=[("szd", qs)], writes=["szs"])
                sc.dma("sp", lambda: nc.sync.dma_start(out=x1s[:], in_=x1d[q0:q0 + 128, :]),
                       reads=[("x1d", qs)], writes=["x1s"])

            mhalf = sb("mhalf", [128, 1], F32)
            sc.op("pool", lambda: nc.gpsimd.memset(mhalf[:], -0.5), writes=["mhalf"])

            def epilogue(qs):
                q0 = qs * 128
                okeys = [("oacc", qs % 2, g) for g in range(4)]
                oacc = oacc2[qs % 2]
                sc.op("dve", lambda: nc.vector.tensor_tensor(out=ybf[:], in0=oacc[:, :, :].rearrange("p h d -> p (h d)"),
                                                             in1=szs[:], op=ALU.mult),
                      reads=okeys + ["szs"], writes=["ybf"])

                def p1():
                    def try_():
                        inst = None
                        for kc in range(8):
                            inst = nc.tensor.transpose(out=bfview(7)[:, kc * 128:(kc + 1) * 128],
                                                       in_=ybf[:, kc * 128:(kc + 1) * 128], identity=ident[:])
                        return inst
                    sc.op("pe", try_, reads=["ybf", "ident"], writes=[bk(7)])

                def p2():
                    sc.op("dve", lambda: nc.vector.tensor_copy(out=yT[:, :, :], in_=bfview(7)[:, :].rearrange("p (k t) -> p k t", k=8)),
                          reads=[bk(7)], writes=["cyT"])

                def p3():
                    for nh in range(2):
                        def mo(nh=nh):
                            inst = None
                            for kc in range(8):
                                inst = nc.tensor.matmul(banks[4 + nh][:, :], lhsT=yT[:, kc, :],
                                                        rhs=woutb[:, kc, nh * 512:(nh + 1) * 512],
                                                        start=(kc == 0), stop=(kc == 7))
                            return inst
                        sc.op("pe", mo, reads=["cyT"] + [("woutb", q) for q in range(8)], writes=[bk(4 + nh)])

                def p4():
                    for nh in range(2):
                        sc.op("dve", lambda nh=nh: nc.vector.tensor_tensor(
                            out=x2[:, nh * 512:(nh + 1) * 512], in0=banks[4 + nh][:, :], in1=x1s[:, nh * 512:(nh + 1) * 512],
                            op=ALU.add), reads=[bk(4 + nh), "x1s"], writes=[("x2", nh)])
                    sc.op("dve", lambda: nc.vector.scalar_tensor_tensor(
                        out=osb[:], in0=x2[:], scalar=1.0, in1=x2[:], op0=ALU.mult, op1=ALU.mult, accum_out=fss[:, qs:qs + 1]),
                        reads=[("x2", 0), ("x2", 1)], writes=["osb", "fss"])
                    sc.op("pool", lambda: nc.gpsimd.tensor_scalar(out=fsq[:, qs:qs + 1], in0=fss[:, qs:qs + 1], scalar1=1.0 / D,
                                                                  scalar2=EPS, op0=ALU.mult, op1=ALU.add),
                          reads=["fss"], writes=["fsq"])
                    sc.op("pool", lambda: nc.gpsimd.tensor_tensor(out=frs[:, qs:qs + 1], in0=fsq[:, qs:qs + 1], in1=mhalf[:],
                                                                  op=ALU.pow), reads=["fsq", "mhalf"], writes=["frs"])

                def p5():
                    sc.op("dve", lambda: nc.vector.scalar_tensor_tensor(
                        out=osb[:], in0=x2[:], scalar=frs[:, qs:qs + 1], in1=gfin[:], op0=ALU.mult, op1=ALU.mult),
                        reads=[("x2", 0), ("x2", 1), "frs", "gfin", "osb"], writes=["osb"])
                    sc.dma("sp", lambda: nc.sync.dma_start(out=outd[q0:q0 + 128, :], in_=osb[:]),
                           reads=["osb"], writes=[("out", qs)])
                for dly, fn in zip(_DBG.get("depi", (3, 5, 7, 10, 13)), (p1, p2, p3, p4, p5)):
                    deferred.append([dly, fn])

            stream = []

            def cmp_units(qs, g):
                qb = qs % 2
                q0 = qs * 128
                for nt in range(2):
                    stream.append(("unit", {
                        "score": (kcmpT[:, g, nt * 128:(nt + 1) * 128], Qc[qb][:, 4 * g:4 * g + 4, :],
                                  [("kcmpT", g // 2), ("Qc", qb), ("Qc0", qb)],
                                  cmask[:, nt:nt + 1, q0:q0 + 128].broadcast_to([128, 4, 128]), [("cmask", nt)]),
                        "pvs": [(3, 0, 64, AI[:, nt, :], ["AI"], nt == 0, nt == 1),
                                (3, 256, 64, vcmp[:, nt, g, :], [("vcmp", nt)], False, nt == 1)],
                        "post": ((lambda qs=qs, g=g, qb=qb: select_chain(qs, g, qb)) if nt == 1 else None)}))

            load_q(0)
            for g in range(4):
                cmp_units(0, g)
            for qs in range(16):
                qb = qs % 2
                q0 = qs * 128
                for g in range(4):
                    for k in (1, 2, 3, 0, 4):
                        kt = (qs - 4 + k) % 32
                        m = None
                        if k == 0:
                            m = upp[:, None, :].broadcast_to([128, 4, 128])
                        elif k == 4:
                            m = tri[:, None, :].broadcast_to([128, 4, 128])
                        stream.append(("unit", {
                            "score": (KwT[:, g, kt * 128:(kt + 1) * 128], Qwin[qb][:, 4 * g:4 * g + 4, :],
                                      kw_keys(g) + [("QwinQ", qb), ("Qwin1", qb)], m, ["tri", "upp"]),
                            "pvs": [(6, 0, 65, Vall[:, kt, (4 + g) * 65:(5 + g) * 65], [("Vall", kt // 4)], k == 1, k == 4)],
                            "post": ((lambda g=g, qb=qb: branch_out(6, g, 2, gts[qb], 2, ob=qb)) if k == 4 else None)}))
                    if g == 0 and qs + 1 < 16:
                        stream.append(("call", lambda qs=qs: load_q(qs + 1)))
                    if g == 3:
                        stream.append(("call", lambda qs=qs: load_tail(qs)))
                for g in range(4):
                    if qs + 1 < 16:
                        cmp_units(qs + 1, g)
                    klist = list(range(16, 32)) + list(range(0, qs + 1))
                    for idx, kt in enumerate(klist):
                        last = idx == len(klist) - 1
                        post = None
                        sb_ = 4 + g % 2
                        if last:
                            if g < 3:
                                post = (lambda g=g, qb=qb, sb_=sb_: untranspose(
                                    sb_, g % 2, lambda: branch_out(sb_, g, 1, gts[qb], 1, stride=128, ob=qb)))
                            else:
                                post = (lambda g=g, qb=qb, qs=qs, sb_=sb_: untranspose(
                                    sb_, g % 2, lambda: (branch_out(sb_, g, 1, gts[qb], 1, stride=128, ob=qb), epilogue(qs))))
                        stream.append(("unit", {
                            "score": (Ksel[:, g, kt * 128:(kt + 1) * 128], Qsel[qb][:, 4 * g:4 * g + 4, :],
                                      ksel_keys(g) + [("QselQ", qb), ("QselB", qb, g)],
                                      (tri[:, None, :].broadcast_to([128, 4, 128]) if kt == qs else None), ["tri"]),
                            "pvs": [("vstat", sb_, Vall[:, kt, g * 65:(g + 1) * 65], [("Vall", kt // 4)], idx == 0, last)],
                            "post": post}))

            LA = _DBG.get('LA', 2)
            deferred = []
            pend = []

            def retire():
                u, pi = pend.pop(0)
                emit_pv(u, pi)
                if u["post"] is not None:
                    u["post"]()
                for d_ in list(deferred):
                    d_[0] -= 1
                    if d_[0] <= 0:
                        deferred.remove(d_)
                        d_[1]()
            for kind, item in stream:
                if kind == "call":
                    item()
                    continue
                pend.append((item, emit_score(item)))
                if len(pend) > LA:
                    retire()
            while pend:
                retire()
            while deferred:
                deferred.sort(key=lambda x: x[0])
                deferred.pop(0)[1]()
            sc.flush()

    sc.finish()
    return nc, sc


def _shared_layout(inputs):
    f = np.float32
    a_w_in = np.asarray(inputs["a_w_in"], f)[0]
    w = a_w_in.reshape(8, 128, 4, 16, 128).transpose(3, 1, 0, 2, 4).reshape(16, 128, 4096)
    a_gT = np.asarray(inputs["a_norm"], f)[0].reshape(8, 128).T
    a_cw = np.asarray(inputs["a_conv_w"], f)[0].reshape(3, 16, 128).transpose(2, 1, 0).reshape(128, 48)
    w_out = np.asarray(inputs["a_w_out"], f)[0].reshape(16, 128, 1024).transpose(1, 0, 2).reshape(128, 16 * 1024)
    wkv = np.asarray(inputs["w_kv"], f).reshape(1024, 6, 256)[:, [0, 1, 2, 4, 3, 5], :].reshape(1024, 1536)
    wkv_r = wkv.reshape(8, 128, 1536).transpose(1, 0, 2).reshape(128, 8 * 1536)
    bw = np.asarray(inputs["b_w_in"], f)[0]
    bw = np.concatenate([bw[:, 0:1024], bw[:, 1072:2096], bw[:, 1024:1072]], axis=1)
    wbin_r = bw.reshape(8, 128, 2096).transpose(1, 0, 2).reshape(128, 8 * 2096)
    woutb = np.asarray(inputs["b_w_out"], f)[0].reshape(8, 128, 1024).transpose(1, 0, 2).reshape(128, 8 * 1024)

    def w1l(w1):
        return np.asarray(w1, f).reshape(16, 128, 256).transpose(1, 0, 2).reshape(128, 16 * 256)

    def w2l(w2):
        w = np.zeros((128, 2, 128), f)
        w[:, :, 0:64] = np.asarray(w2, f).reshape(2, 128, 64).transpose(1, 0, 2)
        return w.reshape(128, 256)
    tri = (np.arange(128)[:, None] <= np.arange(128)[None, :]).astype(f)
    upp = (np.arange(128)[:, None] > np.arange(128)[None, :]).astype(f)
    sh = {
        "w_in_r": w, "a_gT": a_gT, "a_cw": a_cw, "w_out_r": w_out, "ident": np.eye(128, dtype=f),
        "wkv_r": wkv_r, "wbin_r": wbin_r,
        "gkvT": np.asarray(inputs["kv_norm"], f).reshape(8, 128).T,
        "gbT": np.asarray(inputs["b_norm"], f)[0].reshape(8, 128).T,
        "w1k": w1l(inputs["cmp_w1_k"]), "w1v": w1l(inputs["cmp_w1_v"]),
        "w2k": w2l(inputs["cmp_w2_k"]), "w2v": w2l(inputs["cmp_w2_v"]),
        "poskT": np.asarray(inputs["cmp_pos_k"], f).reshape(16, 128).T, "posvT": np.asarray(inputs["cmp_pos_v"], f).reshape(16, 128).T,
        "tri": tri, "upp": upp, "woutb": woutb,
        "gfin": np.broadcast_to(np.asarray(inputs["final_norm"], f)[None, :], (128, 1024)),
    }
    return {k: np.ascontiguousarray(v, dtype=f) for k, v in sh.items()}


def _parity_consts(par):
    f = np.float32
    pos = (np.arange(S) + 2048 * par) % S
    inv = (np.float32(500000.0) ** (-np.arange(0, 16, 2, dtype=f) / np.float32(16))).astype(f)
    ang = pos.astype(f)[:, None] * inv[None, :]
    cosT = np.cos(ang).astype(f).reshape(32, 128, 8).transpose(1, 0, 2).reshape(128, 256)
    sinT = np.sin(ang).astype(f).reshape(32, 128, 8).transpose(1, 0, 2).reshape(128, 256)
    E = (pos[None, :] // 64 == np.arange(64)[:, None]).astype(f)
    kbias = np.zeros((1, S), f)
    if par == 0:
        kbias[0, 3584:] = NEG
    tq = np.arange(2048) + 2048 * par
    m = np.arange(256)
    a = (16 * m + 2048 * par) % S
    valid_m = (a + 31) < S
    n = a // 16
    cm = (valid_m[:, None] & ((16 * n + 31)[:, None] <= tq[None, :])).astype(f)
    cmask = cm.reshape(2, 128, 2048).transpose(1, 0, 2).reshape(128, 4096)
    A = np.zeros((256, 65), f)
    for mi in range(256):
        if not valid_m[mi] or n[mi] > 254:
            continue
        j, r = divmod(int(n[mi]), 4)
        A[mi, j] += 2.0 if r < 3 else 1.0
        if r == 3 and j + 1 < 64:
            A[mi, j + 1] += 1.0
    A[:, 64] = 1.0
    Aaug = A.reshape(2, 128, 65).transpose(1, 0, 2).reshape(128, 130)
    cur = tq // 64
    jj = np.arange(64)[None, :]
    validb = jj <= cur[:, None]
    forced = (jj == 0) | (validb & (jj > cur[:, None] - 2))
    fmul = (validb & ~forced).astype(f)
    fadd = np.where(forced, f(1e4), np.where(validb, f(0.0), f(-1e9))).astype(f)
    fmul = fmul.reshape(16, 128, 64).transpose(1, 0, 2).reshape(128, 1024)
    fadd = fadd.reshape(16, 128, 64).transpose(1, 0, 2).reshape(128, 1024)
    c = {"cosT": cosT, "sinT": sinT, "Emat": E, "kbias": kbias, "cmask": cmask, "Aaug": Aaug, "fmul": fmul, "fadd": fadd}
    return {k: np.ascontiguousarray(v, dtype=f) for k, v in c.items()}


def make_in_maps(inputs, cores=range(NCORES)):
    sh = _shared_layout(inputs)
    pc = [_parity_consts(0), _parity_consts(1)]
    x = np.asarray(inputs["x"], np.float32)
    maps = []
    for c in cores:
        b, par = c // 2, c % 2
        m = dict(sh)
        m.update(pc[par])
        m["xb"] = np.ascontiguousarray(np.roll(x[b], -2048 * par, axis=0))
        xh = np.zeros((4, D), np.float32)
        if par == 0:
            xh[2:4] = x[b, 2046:2048]
        else:
            xh[0:2] = x[b, 2046:2048]
        m["xh"] = xh
        maps.append(m)
    return maps


_CACHE = {}


def kernel(**inputs):
    if "nc" not in _CACHE:
        _CACHE["nc"] = build_program()[0]
    nc = _CACHE["nc"]
    in_maps = make_in_maps(inputs)
    res = run_bass_kernel_spmd(nc, in_maps, core_ids=list(range(NCORES)))
    out = np.empty((4, S, D), np.float32)
    for c in range(NCORES):
        b, par = c // 2, c % 2
        out[b, 2048 * par:2048 * (par + 1)] = res.results[c]["out"]
    return out
```

```python
import numpy as np
import concourse.bass as bass
import concourse.mybir as mybir
from concourse.bass_utils import run_bass_kernel_spmd
from contextlib import ExitStack

F32 = mybir.dt.float32
BF16 = mybir.dt.bfloat16
AF = mybir.ActivationFunctionType
ALU = mybir.AluOpType

D = 1024
S = 4096
CONV_D = 2048
EPS = 1e-6
NCORES = 8
_DBG = {"pe_sync": 0}


class Sched:
    def __init__(self, nc, n_dma_sems=10, same_engine_sync=False):
        self.nc = nc
        self.eng = {"pe": nc.tensor, "act": nc.scalar, "dve": nc.vector,
                    "pool": nc.gpsimd, "sp": nc.sync}
        self.ops = []
        self.last_w = {}
        self.readers = {}
        self.same_engine_sync = same_engine_sync
        self.n_dma_sems = n_dma_sems
        self.last_on = {}
        self.dma_all = []
        self.n_emitted = 0
        self.esem = None

    def _add(self, kind, eng, fn, reads, writes):
        idx = len(self.ops)
        reads = list(reads)
        writes = list(writes)
        for k in list(reads):
            if isinstance(k, tuple) and k and k[0] == "bank":
                reads.remove(k)
                if k not in writes:
                    writes.append(k)
        deps = set()
        for k in reads + writes:
            if k in self.last_w:
                deps.add(self.last_w[k])
        for k in writes:
            r = self.readers.get(k)
            if r:
                deps.update(r["c"].values())
                deps.update(r["d"])
        for k in writes:
            self.last_w[k] = idx
            self.readers[k] = {"c": {}, "d": []}
        for k in reads:
            if k in writes:
                continue
            r = self.readers.setdefault(k, {"c": {}, "d": []})
            if kind == "dma":
                r["d"].append(idx)
            else:
                r["c"][eng] = idx
        deps.discard(idx)
        self.ops.append({"kind": kind, "eng": eng, "fn": fn, "deps": deps, "need_inc": False})
        if kind == "dma":
            self.dma_all.append(idx)
        else:
            self.last_on[eng] = idx
        return idx

    def op(self, eng, fn, reads=(), writes=()):
        return self._add("c", eng, fn, reads, writes)

    def dma(self, queue, fn, reads=(), writes=()):
        return self._add("dma", queue, fn, reads, writes)

    def barrier(self):
        deps = set(self.last_on.values()) | set(self.dma_all)
        for e in ("pe", "act", "dve", "pool", "sp"):
            idx = len(self.ops)
            self.ops.append({"kind": "c", "eng": e, "fn": None, "deps": set(deps), "need_inc": False, "bar": True})
        self.dma_all = []

    def flush(self):
        self.barrier()
        nc = self.nc
        ops = self.ops
        start = self.n_emitted
        pe_sync = _DBG.get("pe_sync")

        def implicit(od, o):
            return (od["eng"] == o["eng"] and o["kind"] == "c" and od["kind"] == "c"
                    and (not self.same_engine_sync or (o["eng"] == "pe" and not pe_sync)))
        for o in ops[start:]:
            for d in o["deps"]:
                od = ops[d]
                if d < start or od["kind"] == "dma" or implicit(od, o):
                    continue
                od["need_inc"] = True
        if self.esem is None:
            self.esem = {e: nc.alloc_semaphore(name="sem_" + e) for e in ("pe", "act", "dve", "pool")}
            self.ecount = {e: 0 for e in self.esem}
            self.dsem = {}
            self.dcount = {}
            for q in ("sp", "act", "pool"):
                self.dsem[q] = [nc.alloc_semaphore(name="dma_%s_%d" % (q, i)) for i in range(self.n_dma_sems)]
                self.dcount[q] = 0
            self.waited = {e: {} for e in self.eng}
        esem, ecount, dsem, dcount, waited = self.esem, self.ecount, self.dsem, self.dcount, self.waited
        K = self.n_dma_sems
        for o in ops[start:]:
            e = o["eng"]
            engine = self.eng[e]
            waits = {}
            for d in o["deps"]:
                od = ops[d]
                if d < start and not o.get("bar"):
                    continue
                if od["kind"] == "dma":
                    key = ("d", od["eng"], od["slot"])
                    waits[key] = max(waits.get(key, 0), od["val"])
                else:
                    if implicit(od, o) or "val" not in od:
                        continue
                    key = ("c", od["eng"])
                    waits[key] = max(waits.get(key, 0), od["val"])
            if o["kind"] == "dma":
                j = dcount[e]
                slot = j % K
                if j >= K:
                    key = ("d", e, slot)
                    waits[key] = max(waits.get(key, 0), 16 * (j // K))
            for key, val in waits.items():
                if waited[e].get(key, 0) >= val:
                    continue
                waited[e][key] = val
                sem = dsem[key[1]][key[2]] if key[0] == "d" else esem[key[1]]
                engine.wait_ge(sem, val)
            if o["fn"] is None:
                continue
            inst = o["fn"]()
            if o["kind"] == "dma":
                j = dcount[e]
                o["slot"] = j % K
                o["val"] = 16 * (j // K + 1)
                inst.then_inc(dsem[e][o["slot"]], 16)
                dcount[e] = j + 1
            elif o["need_inc"]:
                ecount[e] += 1
                o["val"] = ecount[e]
                inst.then_inc(esem[e], 1)
        self.n_emitted = len(ops)
        self.stats = {"ops": len(ops), "incs": dict(ecount), "dmas": dict(dcount)}

    def finish(self):
        self.flush()


NEG = -30000.0


def build_program(stages=("A", "B", "C"), dbg=False, same_engine_sync=True):
    nc = bass.Bass("TRN2", target_bir_lowering=False)
    sc = Sched(nc, same_engine_sync=same_engine_sync)
    scratch_kind = "ExternalOutput" if dbg else "Internal"

    def dram(name, shape, dtype, kind):
        return nc.dram_tensor(name, list(shape), dtype, kind=kind).ap()

    def din(name, shape, dtype=F32):
        return dram(name, shape, dtype, "ExternalInput")

    xb = din("xb", [S, D])
    xh = din("xh", [4, D])
    w_in_r = din("w_in_r", [16, 128, 4096])
    a_gT = din("a_gT", [128, 8])
    a_cw = din("a_cw", [128, 48])
    w_out_r = din("w_out_r", [128, 16 * 1024])
    ident_d = din("ident", [128, 128])
    x1d = dram("x1", [S, D], F32, scratch_kind)
    wkv_r = din("wkv_r", [128, 8 * 1536])
    wbin_r = din("wbin_r", [128, 8 * 2096])
    gkvT_d = din("gkvT", [128, 8])
    gbT_d = din("gbT", [128, 8])
    cos_d = din("cosT", [128, 32 * 8])
    sin_d = din("sinT", [128, 32 * 8])
    kT4 = dram("kT4", [16, 64, S], BF16, scratch_kind)
    vtok = dram("vtok", [S, 520], BF16, scratch_kind)
    qT = dram("qT", [32, 64, 2048], BF16, scratch_kind)
    szd = dram("szd", [2048, 1024], F32, scratch_kind)
    gated = dram("gated", [2048, 48], F32, scratch_kind)
    w1k_d = din("w1k", [128, 16 * 256])
    w1v_d = din("w1v", [128, 16 * 256])
    w2k_d = din("w2k", [128, 2 * 128])
    w2v_d = din("w2v", [128, 2 * 128])
    posk_d = din("poskT", [128, 16])
    posv_d = din("posvT", [128, 16])
    E_d = din("Emat", [64, S])
    kbias_d = din("kbias", [1, S])
    cmask_d = din("cmask", [128, 2 * 2048])
    Aaug_d = din("Aaug", [128, 2 * 65])
    fmul_d = din("fmul", [128, 16 * 64])
    fadd_d = din("fadd", [128, 16 * 64])
    tri_d = din("tri", [128, 128])
    upp_d = din("upp", [128, 128])
    woutb_d = din("woutb", [128, 8 * 1024])
    gfin_d = din("gfin", [128, 1024])
    outd = dram("out", [2048, D], F32, "ExternalOutput")

    banks = [nc.alloc_psum_tensor("bank%d" % i, [128, 512], F32) for i in range(8)]
    bankctr = [0]

    def nextbank():
        b = bankctr[0] % 8
        bankctr[0] += 1
        return b

    def bk(i):
        return ("bank", i)

    def bfview(i):
        return banks[i].bitcast(BF16)

    mhalf_g = nc.alloc_sbuf_tensor("mhalf_g", [128, 1], F32)
    sc.op("pool", lambda: nc.gpsimd.memset(mhalf_g[:], -0.5), writes=["mhalf_g"])
    ident = nc.alloc_sbuf_tensor("ident_sb", [128, 128], BF16)
    sc.dma("pool", lambda: nc.gpsimd.dma_start(out=ident[:], in_=ident_d[:, :]), writes=["ident"])

    def rms_front(stk, tag, xin, xhat, ss, sq, rstd, col, rows=128):
        sc.op("act", lambda: nc.scalar.activation(
            out=xhat[0:rows, :], in_=xin[0:rows, :], func=AF.Square, accum_out=ss[0:rows, col:col + 1]),
            reads=[(tag, "xin")], writes=[(tag, "xhat"), (tag, "ss", col)])
        sc.op("pool", lambda: nc.gpsimd.tensor_scalar(
            out=sq[0:rows, col:col + 1], in0=ss[0:rows, col:col + 1], scalar1=1.0 / D, scalar2=EPS,
            op0=ALU.mult, op1=ALU.add), reads=[(tag, "ss", col)], writes=[(tag, "sq", col)])
        sc.op("pool", lambda: nc.gpsimd.tensor_tensor(
            out=rstd[0:rows, col:col + 1], in0=sq[0:rows, col:col + 1], in1=mhalf_g[0:rows, :], op=ALU.pow),
            reads=[(tag, "sq", col), "mhalf_g"], writes=[(tag, "rstd", col)])
        sc.op("dve", lambda: nc.vector.tensor_scalar(
            out=xhat[0:rows, :], in0=xin[0:rows, :], scalar1=rstd[0:rows, col:col + 1], scalar2=None, op0=ALU.mult),
            reads=[(tag, "xin"), (tag, "rstd", col)], writes=[(tag, "xhat")])

    if "A" in stages:
        with ExitStack() as stk:
            def sb(name, shape, dtype):
                return stk.enter_context(nc.sbuf_tensor(name, list(shape), dtype))
            gT = sb("gT", [128, 8], F32)
            cw = sb("cw", [128, 48], F32)
            woutT = sb("woutT", [128, 16, 1024], BF16)
            hT = sb("hT", [128, 8, 2048], BF16)
            hTh = sb("hTh", [128, 8, 4], BF16)
            yT = sb("yT", [128, 16, 2048], BF16)
            wblk = [sb("wblk%d" % i, [128, 4096], BF16) for i in range(2)]
            xin = [sb("xin%d" % i, [128, 1024], F32) for i in range(2)]
            xhat = [sb("xhat%d" % i, [128, 1024], BF16) for i in range(2)]
            ss = sb("ss", [128, 17], F32)
            sq = sb("sq", [128, 17], F32)
            rstd = sb("rstd", [128, 17], F32)
            vhalo = sb("vhalo", [128, 16, 2], F32)
            hcs = sb("hcs", [128, 4], F32)
            csb = [sb("csb%d" % i, [128, 512], F32) for i in range(2)]
            vbuf = [sb("vbuf%d" % i, [128, 514], F32) for i in range(2)]
            tbuf = [sb("tbuf%d" % i, [128, 512], F32) for i in range(2)]
            szb = [sb("szb%d" % i, [128, 512], F32) for i in range(2)]
            x1t = [sb("x1t%d" % i, [128, 1024], F32) for i in range(2)]

            sc.dma("sp", lambda: nc.sync.dma_start(out=gT[:], in_=a_gT[:, :]), writes=["gT"])
            sc.dma("sp", lambda: nc.sync.dma_start(out=cw[:], in_=a_cw[:, :]), writes=["cw"])
            for q in range(8):
                sc.dma("pool", lambda q=q: nc.gpsimd.dma_start(
                    out=woutT[:, 2 * q:2 * q + 2, :],
                    in_=w_out_r[:, 2048 * q:2048 * (q + 1)].rearrange("p (a n) -> p a n", a=2)),
                    writes=[("woutT", q)])

            sc.op("pool", lambda: nc.gpsimd.memset(xin[1][:], 0.0), writes=[("A1", "xin")])
            sc.dma("sp", lambda: nc.sync.dma_start(out=xin[1][0:4, :], in_=xh[:, :]), reads=[("A1", "xin")], writes=[("A1", "xin")])
            rms_front(stk, "A1", xin[1], xhat[1], ss, sq, rstd, 16)
            bi = nextbank()

            def trh(bi=bi):
                pst = bfview(bi)
                inst = None
                for kc in range(8):
                    inst = nc.tensor.transpose(out=pst[:, kc * 128:(kc + 1) * 128],
                                               in_=xhat[1][:, kc * 128:(kc + 1) * 128], identity=ident[:])
                return inst
            sc.op("pe", trh, reads=[("A1", "xhat"), "ident"], writes=[bk(bi)])
            sc.op("dve", lambda bi=bi: nc.vector.tensor_tensor(
                out=hTh[:, :, :], in0=bfview(bi)[:, :].rearrange("p (k t) -> p k t", k=8)[:, :, 0:4],
                in1=gT[:, :, None].broadcast_to([128, 8, 4]), op=ALU.mult),
                reads=[bk(bi), "gT"], writes=["hTh"])

            nblk = 0
            nchunk = 0
            for hf in range(2):
                tok0 = hf * 2048
                for tt in range(16):
                    s = tt % 2
                    tg = "A%d" % s
                    r0 = tok0 + tt * 128
                    sc.dma("sp", lambda s=s, r0=r0: nc.sync.dma_start(out=xin[s][:], in_=xb[r0:r0 + 128, :]),
                           writes=[(tg, "xin")])
                    rms_front(stk, tg, xin[s], xhat[s], ss, sq, rstd, tt)
                    bi = nextbank()

                    def tr(s=s, bi=bi):
                        pst = bfview(bi)
                        inst = None
                        for kc in range(8):
                            inst = nc.tensor.transpose(out=pst[:, kc * 128:(kc + 1) * 128],
                                                       in_=xhat[s][:, kc * 128:(kc + 1) * 128], identity=ident[:])
                        return inst
                    sc.op("pe", tr, reads=[(tg, "xhat"), "ident"], writes=[bk(bi)])
                    sc.op("dve", lambda tt=tt, bi=bi: nc.vector.tensor_tensor(
                        out=hT[:, :, tt * 128:(tt + 1) * 128],
                        in0=bfview(bi)[:, :].rearrange("p (k t) -> p k t", k=8),
                        in1=gT[:, :, None].broadcast_to([128, 8, 128]), op=ALU.mult),
                        reads=[bk(bi), "gT"], writes=[("hT", tt)])

                for cb in range(16):
                    wb = nblk % 2
                    nblk += 1
                    for hh in range(2):
                        sc.dma("pool", lambda cb=cb, wb=wb, hh=hh: nc.gpsimd.dma_start(
                            out=wblk[wb][:, 2048 * hh:2048 * (hh + 1)], in_=w_in_r[cb, :, 2048 * hh:2048 * (hh + 1)]),
                            writes=[("wblk", wb, hh)])
                    wkeys = [("wblk", wb, 0), ("wblk", wb, 1)]
                    for tc in range(4):
                        sl = nchunk % 2
                        nchunk += 1
                        c0 = tc * 512
                        bb, bc, bu, bz = 4 * sl, 4 * sl + 1, 4 * sl + 2, 4 * sl + 3
                        if tc == 0:
                            for j, bi in ((1, bb), (2, bz)):
                                def mmh(j=j, bi=bi, wb=wb):
                                    inst = None
                                    for kc in range(8):
                                        off = kc * 512 + j * 128
                                        inst = nc.tensor.matmul(banks[bi][:, 0:4], lhsT=wblk[wb][:, off:off + 128],
                                                                rhs=hTh[:, kc, :], start=(kc == 0), stop=(kc == 7))
                                    return inst
                                sc.op("pe", mmh, reads=wkeys + ["hTh"], writes=[bk(bi)])
                            sc.op("act", lambda bb=bb: nc.scalar.copy(out=hcs[:], in_=banks[bb][:, 0:4]),
                                  reads=[bk(bb)], writes=["hcs"])
                            sc.op("dve", lambda bz=bz, cb=cb, hf=hf: nc.vector.tensor_tensor(
                                out=vhalo[:, cb, :], in0=hcs[:, 2 * hf:2 * hf + 2], in1=banks[bz][:, 2 * hf:2 * hf + 2],
                                op=ALU.mult), reads=["hcs", bk(bz)], writes=[("vhalo", cb)])
                        for j in (1, 2, 0, 3):
                            bi = 4 * sl + j

                            def mm(j=j, bi=bi, wb=wb, c0=c0):
                                inst = None
                                for kc in range(8):
                                    off = kc * 512 + j * 128
                                    inst = nc.tensor.matmul(banks[bi][:, :], lhsT=wblk[wb][:, off:off + 128],
                                                            rhs=hT[:, kc, c0:c0 + 512],
                                                            start=(kc == 0), stop=(kc == 7))
                                return inst
                            sc.op("pe", mm, reads=wkeys + [("hT", 4 * tc + i) for i in range(4)], writes=[bk(bi)])
                        sc.op("act", lambda sl=sl, bc=bc: nc.scalar.copy(out=csb[sl][:], in_=banks[bc][:, :]),
                              reads=[bk(bc)], writes=[("csb", sl)])
                        sc.op("pool", lambda sl=sl, cb=cb: nc.gpsimd.tensor_copy(out=vbuf[sl][:, 0:2], in_=vhalo[:, cb, :]),
                              reads=[("vhalo", cb)], writes=[("vbufh", sl)])
                        sc.op("dve", lambda sl=sl, bu=bu: nc.vector.tensor_tensor(
                            out=vbuf[sl][:, 2:514], in0=csb[sl][:], in1=banks[bu][:, :], op=ALU.mult),
                            reads=[("csb", sl), bk(bu)], writes=[("vbuf", sl)])
                        sc.op("pool", lambda sl=sl, cb=cb: nc.gpsimd.tensor_copy(out=vhalo[:, cb, :], in_=vbuf[sl][:, 512:514]),
                              reads=[("vbuf", sl)], writes=[("vhalo", cb)])
                        sc.op("dve", lambda sl=sl, cb=cb: nc.vector.tensor_scalar(
                            out=tbuf[sl][:], in0=vbuf[sl][:, 0:512], scalar1=cw[:, 3 * cb:3 * cb + 1], scalar2=None,
                            op0=ALU.mult),
                            reads=[("vbuf", sl), ("vbufh", sl), "cw"], writes=[("tbuf", sl)])
                        sc.op("dve", lambda sl=sl, cb=cb: nc.vector.scalar_tensor_tensor(
                            out=tbuf[sl][:], in0=vbuf[sl][:, 1:513], scalar=cw[:, 3 * cb + 1:3 * cb + 2], in1=tbuf[sl][:],
                            op0=ALU.mult, op1=ALU.add),
                            reads=[("vbuf", sl), ("vbufh", sl), ("tbuf", sl)], writes=[("tbuf", sl)])
                        sc.op("dve", lambda sl=sl, cb=cb: nc.vector.scalar_tensor_tensor(
                            out=tbuf[sl][:], in0=vbuf[sl][:, 2:514], scalar=cw[:, 3 * cb + 2:3 * cb + 3], in1=tbuf[sl][:],
                            op0=ALU.mult, op1=ALU.add),
                            reads=[("vbuf", sl), ("tbuf", sl)], writes=[("tbuf", sl)])
                        sc.op("act", lambda sl=sl, bz=bz: nc.scalar.activation(out=szb[sl][:], in_=banks[bz][:, :], func=AF.Silu),
                              reads=[bk(bz)], writes=[("szb", sl)])
                        sc.op("dve", lambda sl=sl, bb=bb: nc.vector.tensor_tensor(
                            out=tbuf[sl][:], in0=tbuf[sl][:], in1=banks[bb][:, :], op=ALU.mult),
                            reads=[("tbuf", sl), bk(bb)], writes=[("tbuf", sl)])
                        sc.op("dve", lambda sl=sl, cb=cb, c0=c0: nc.vector.tensor_tensor(
                            out=yT[:, cb, c0:c0 + 512], in0=tbuf[sl][:], in1=szb[sl][:], op=ALU.mult),
                            reads=[("tbuf", sl), ("szb", sl)], writes=[("yT", cb, tc)])

                for tt in range(16):
                    s = tt % 2
                    tg = "A%d" % s
                    r0 = tok0 + tt * 128
                    sc.dma("sp", lambda s=s, r0=r0: nc.sync.dma_start(out=xin[s][:], in_=xb[r0:r0 + 128, :]),
                           writes=[(tg, "xin")])
                    for nh in range(2):
                        bi = nextbank()

                        def mm2(bi=bi, tt=tt, nh=nh):
                            inst = None
                            for cb in range(16):
                                inst = nc.tensor.matmul(banks[bi][:, :], lhsT=yT[:, cb, tt * 128:(tt + 1) * 128],
                                                        rhs=woutT[:, cb, nh * 512:(nh + 1) * 512],
                                                        start=(cb == 0), stop=(cb == 15))
                            return inst
                        sc.op("pe", mm2, reads=[("yT", cb, tt // 4) for cb in range(16)] + [("woutT", q) for q in range(8)],
                              writes=[bk(bi)])
                        sc.op("dve", lambda s=s, bi=bi, nh=nh: nc.vector.tensor_tensor(
                            out=x1t[s][:, nh * 512:(nh + 1) * 512], in0=banks[bi][:, :],
                            in1=xin[s][:, nh * 512:(nh + 1) * 512], op=ALU.add),
                            reads=[bk(bi), (tg, "xin")], writes=[("x1t", s, nh)])
                    sc.dma("sp", lambda s=s, r0=r0: nc.sync.dma_start(out=x1d[r0:r0 + 128, :], in_=x1t[s][:]),
                           reads=[("x1t", s, 0), ("x1t", s, 1)], writes=[("x1d", r0 // 128)])
            sc.flush()

    if "B" in stages:
        with ExitStack() as stk:
            def sb(name, shape, dtype):
                return stk.enter_context(nc.sbuf_tensor(name, list(shape), dtype))
            wkv = sb("wkv", [128, 8, 1536], BF16)
            wbin = sb("wbin", [128, 8, 2096], BF16)
            gkvT = sb("gkvT_sb", [128, 8], F32)
            gbT = sb("gbT_sb", [128, 8], F32)
            cosT = sb("cosT_sb", [128, 32, 8], F32)
            sinT = sb("sinT_sb", [128, 32, 8], F32)
            xin = [sb("bxin%d" % i, [128, 1024], F32) for i in range(2)]
            xhat = [sb("bxhat%d" % i, [128, 1024], BF16) for i in range(2)]
            ss = sb("bss", [128, 32], F32)
            sq = sb("bsq", [128, 32], F32)
            rstd = sb("brstd", [128, 32], F32)
            hkT = [sb("hkT%d" % i, [128, 8, 128], BF16) for i in range(2)]
            hqT = [sb("hqT%d" % i, [128, 8, 128], BF16) for i in range(2)]
            kcv = [sb("kcv%d" % i, [128, 512], BF16) for i in range(2)]
            kk = [sb("kk%d" % i, [128, 8, 64], BF16) for i in range(2)]
            rt = [sb("rt%d" % i, [128, 8, 8], F32) for i in range(4)]
            kTst = [sb("kTst%d" % i, [128, 8, 512], BF16) for i in range(2)]
            vst = [sb("vst%d" % i, [128, 4, 8, 65], BF16) for i in range(2)]
            qr = [sb("qr%d" % i, [128, 16, 64], BF16) for i in range(2)]
            qc = [sb("qc%d" % i, [128, 16, 64], BF16) for i in range(2)]
            qTst = [sb("qTst%d" % i, [128, 16, 512], BF16) for i in range(2)]
            szst = [sb("szst%d" % i, [128, 1024], F32) for i in range(2)]
            gst = [sb("gst%d" % i, [128, 48], F32) for i in range(2)]

            for q in range(4):
                sc.dma("pool", lambda q=q: nc.gpsimd.dma_start(
                    out=wkv[:, 2 * q:2 * q + 2, :],
                    in_=wkv_r[:, 3072 * q:3072 * (q + 1)].rearrange("p (a n) -> p a n", a=2)),
                    writes=[("wkv", q)])
            for q in range(8):
                sc.dma("pool", lambda q=q: nc.gpsimd.dma_start(
                    out=wbin[:, q, :].rearrange("p (a n) -> p a n", a=2),
                    in_=wbin_r[:, 2096 * q:2096 * (q + 1)].rearrange("p (a n) -> p a n", a=2)), writes=[("wbin", q)])
            wkv_keys = [("wkv", q) for q in range(4)]
            wbin_keys = [("wbin", q) for q in range(8)]
            sc.dma("sp", lambda: nc.sync.dma_start(out=gkvT[:], in_=gkvT_d[:, :]), writes=["gkvT"])
            sc.dma("sp", lambda: nc.sync.dma_start(out=gbT[:], in_=gbT_d[:, :]), writes=["gbT"])
            sc.dma("sp", lambda: nc.sync.dma_start(out=cosT[:], in_=cos_d[:, :].rearrange("p (t f) -> p t f", f=8)),
                   writes=["cosT"])
            sc.dma("sp", lambda: nc.sync.dma_start(out=sinT[:], in_=sin_d[:, :].rearrange("p (t f) -> p t f", f=8)),
                   writes=["sinT"])
            for i in range(2):
                sc.op("pool", lambda i=i: nc.gpsimd.memset(vst[i][:], 1.0), writes=[("vst", i)])

            def rope(src_bank, col0, nh, dst, tt, tagk, wkey):
                psv = banks[src_bank][:, col0:col0 + nh * 64].rearrange("p (h d) -> p h d", d=64)
                cs = cosT[:, tt:tt + 1, :].broadcast_to([128, nh, 8])
                sn = sinT[:, tt:tt + 1, :].broadcast_to([128, nh, 8])
                rd = [bk(src_bank), "cosT", "sinT"]
                for t_i, (lo, tab) in enumerate(((0, cs), (8, sn), (8, cs), (0, sn))):
                    sc.op("dve", lambda t_i=t_i, lo=lo, tab=tab: nc.vector.tensor_tensor(
                        out=rt[t_i][:, 0:nh, :], in0=psv[:, :, lo:lo + 8], in1=tab, op=ALU.mult),
                        reads=rd, writes=[("rt", t_i)])
                if _DBG.get("r1"):
                    return []
                sc.op("dve", lambda: nc.vector.tensor_tensor(
                    out=dst[:, 0:nh, 0:8], in0=rt[0][:, 0:nh, :], in1=rt[1][:, 0:nh, :], op=ALU.subtract),
                    reads=[("rt", 0), ("rt", 1)], writes=[wkey + ("a",)])
                sc.op("dve", lambda: nc.vector.tensor_tensor(
                    out=dst[:, 0:nh, 8:16], in0=rt[2][:, 0:nh, :], in1=rt[3][:, 0:nh, :], op=ALU.add),
                    reads=[("rt", 2), ("rt", 3)], writes=[wkey + ("b",)])
                if _DBG.get("r2"):
                    return []
                sc.op("dve", lambda: nc.vector.tensor_copy(out=dst[:, 0:nh, 16:64], in_=psv[:, :, 16:64]),
                      reads=[bk(src_bank)], writes=[wkey + ("c",)])
                return [wkey + ("a",), wkey + ("b",), wkey + ("c",)]

            def proj(bi, act, wt, c0, ncols):
                def f():
                    inst = None
                    for kc in range(8):
                        inst = nc.tensor.matmul(banks[bi][:, 0:ncols], lhsT=act[:, kc, :],
                                                rhs=wt[:, kc, c0:c0 + ncols], start=(kc == 0), stop=(kc == 7))
                    return inst
                return f

            def front(ch, i):
                own = ch < 4
                tt = 4 * ch + i
                s = tt % 2
                tg = "B%d" % s
                r0 = tt * 128
                sc.dma("sp", lambda: nc.sync.dma_start(out=xin[s][:], in_=x1d[r0:r0 + 128, :]),
                       reads=[("x1d", tt)], writes=[(tg, "xin")])
                rms_front(stk, tg, xin[s], xhat[s], ss, sq, rstd, tt)

            def front2(ch, i):
                own = ch < 4
                tt = 4 * ch + i
                s = tt % 2
                tg = "B%d" % s
                bi = nextbank()

                def tr():
                    pst = bfview(bi)
                    inst = None
                    for kc in range(8):
                        inst = nc.tensor.transpose(out=pst[:, kc * 128:(kc + 1) * 128],
                                                   in_=xhat[s][:, kc * 128:(kc + 1) * 128], identity=ident[:])
                    return inst
                sc.op("pe", tr, reads=[(tg, "xhat"), "ident"], writes=[bk(bi)])
                sc.op("dve", lambda: nc.vector.tensor_tensor(
                    out=hkT[s][:, :, :], in0=bfview(bi)[:, :].rearrange("p (k t) -> p k t", k=8),
                    in1=gkvT[:, :, None].broadcast_to([128, 8, 128]), op=ALU.mult),
                    reads=[bk(bi), "gkvT"], writes=[("hkT", s)])
                if own:
                    sc.op("dve", lambda: nc.vector.tensor_tensor(
                        out=hqT[s][:, :, :], in0=bfview(bi)[:, :].rearrange("p (k t) -> p k t", k=8),
                        in1=gbT[:, :, None].broadcast_to([128, 8, 128]), op=ALU.mult),
                        reads=[bk(bi), "gbT"], writes=[("hqT", s)])

            def back(ch, i):
                own = ch < 4
                cs_ = ch % 2
                tt = 4 * ch + i
                s = tt % 2
                r0 = tt * 128
                b0 = nextbank()
                sc.op("pe", proj(b0, hkT[s], wkv, 0, 512), reads=[("hkT", s)] + wkv_keys, writes=[bk(b0)])
                sc.op("act", lambda: nc.scalar.copy(out=kcv[s][:], in_=banks[b0][:, :]),
                      reads=[bk(b0)], writes=[("kcv", s)])
                b1 = nextbank()
                sc.op("pe", proj(b1, hkT[s], wkv, 512, 512), reads=[("hkT", s)] + wkv_keys, writes=[bk(b1)])
                kkeys = rope(b1, 0, 8, kk[s], tt, "kk", ("kk", s))
                b2 = nextbank()
                sc.op("pe", proj(b2, hkT[s], wkv, 1024, 512), reads=[("hkT", s)] + wkv_keys, writes=[bk(b2)])
                sc.op("act", lambda: nc.scalar.copy(
                    out=vst[cs_][:, i, :, 0:64], in_=banks[b2][:, :].rearrange("p (a d) -> p a d", d=64)),
                    reads=[bk(b2)], writes=[("vst", cs_)])
                if own:
                    for qh in range(2):
                        bq = nextbank()
                        sc.op("pe", proj(bq, hqT[s], wbin, 512 * qh, 512), reads=[("hqT", s)] + wbin_keys, writes=[bk(bq)])
                        sc.op("act", lambda bq=bq, qh=qh: nc.scalar.copy(
                            out=qc[s][:, 8 * qh:8 * qh + 8, :], in_=banks[bq][:, :].rearrange("p (h d) -> p h d", d=64)),
                            reads=[bk(bq)], writes=[("qc", s, qh)])
                        rope(bq, 0, 8, qr[s][:, 8 * qh:8 * qh + 8, :], tt, "q", ("qr", s, qh))
                    for zh in range(2):
                        bz = nextbank()
                        sc.op("pe", proj(bz, hqT[s], wbin, 1024 + 512 * zh, 512), reads=[("hqT", s)] + wbin_keys, writes=[bk(bz)])
                        sc.op("act", lambda bz=bz, zh=zh: nc.scalar.activation(
                            out=szst[s][:, 512 * zh:512 * (zh + 1)], in_=banks[bz][:, :], func=AF.Sigmoid),
                            reads=[bk(bz)], writes=[("szst", s, zh)])
                        sc.op("dve", lambda bz=bz, zh=zh: nc.vector.tensor_tensor(
                            out=szst[s][:, 512 * zh:512 * (zh + 1)], in0=banks[bz][:, :],
                            in1=szst[s][:, 512 * zh:512 * (zh + 1)], op=ALU.mult),
                            reads=[bk(bz), ("szst", s, zh)], writes=[("szst", s, zh)])
                    bg = nextbank()
                    sc.op("pe", proj(bg, hqT[s], wbin, 2048, 48), reads=[("hqT", s)] + wbin_keys, writes=[bk(bg)])
                    sc.op("act", lambda: nc.scalar.activation(out=gst[s][:], in_=banks[bg][:, 0:48], func=AF.Sigmoid),
                          reads=[bk(bg)], writes=[("gst", s)])
                    sc.dma("sp", lambda: nc.sync.dma_start(out=szd[r0:r0 + 128, :], in_=szst[s][:]),
                           reads=[("szst", s, 0), ("szst", s, 1)], writes=[("szd", tt)])
                    sc.dma("sp", lambda: nc.sync.dma_start(out=gated[r0:r0 + 128, :], in_=gst[s][:]),
                           reads=[("gst", s)], writes=[("gated", tt)])
                for half, src, skeys in ((0, kcv[s], [("kcv", s)]), (1, kk[s], kkeys)):
                    bt = nextbank()

                    def trk(bt=bt, src=src, half=half):
                        pst = bfview(bt)
                        sv = src[:, :] if half == 0 else src[:, :, :].rearrange("p a d -> p (a d)")
                        inst = None
                        for a in range(4):
                            inst = nc.tensor.transpose(out=pst[:, a * 128:(a + 1) * 128],
                                                       in_=sv[:, a * 128:(a + 1) * 128], identity=ident[:])
                        return inst
                    sc.op("pe", trk, reads=skeys + ["ident"], writes=[bk(bt)])
                    sc.op("act" if half == 0 else "dve", (lambda bt=bt, half=half: (
                        nc.scalar.copy if half == 0 else nc.vector.tensor_copy)(
                        out=kTst[cs_][:, 4 * half:4 * half + 4, i * 128:(i + 1) * 128],
                        in_=bfview(bt)[:, 0:512].rearrange("p (a t) -> p a t", a=4))),
                        reads=[bk(bt)], writes=[("kTst", cs_, half, i)])
                if own:
                    for which, src, skeys in ((0, qc[s], [("qc", s, 0), ("qc", s, 1)]),
                                              (1, qr[s], [("qr", s, qh, x) for qh in range(2) for x in "abc"])):
                        bt = nextbank()

                        def trq(bt=bt, src=src):
                            pst = bfview(bt)
                            sv = src[:, :, :].rearrange("p h d -> p (h d)")
                            inst = None
                            for a in range(8):
                                inst = nc.tensor.transpose(out=pst[:, a * 128:(a + 1) * 128],
                                                           in_=sv[:, a * 128:(a + 1) * 128], identity=ident[:])
                            return inst
                        sc.op("pe", trq, reads=skeys + ["ident"], writes=[bk(bt)])
                        a0 = 8 * which
                        sc.op("dve" if which == 0 else "act", (lambda bt=bt, which=which, a0=a0: (
                            nc.vector.tensor_copy if which == 0 else nc.scalar.copy)(
                            out=qTst[cs_][:, a0:a0 + 8, i * 128:(i + 1) * 128],
                            in_=bfview(bt)[:, :].rearrange("p (a t) -> p a t", a=8))),
                            reads=[bk(bt)], writes=[("qTst", cs_, which, i)])

            def chunk_store(ch):
                own = ch < 4
                cs_ = ch % 2
                c0 = ch * 512
                sc.dma("sp", lambda: nc.sync.dma_start(
                    out=kT4[:, :, c0:c0 + 512].rearrange("(a gl) d t -> (gl d) a t", gl=2), in_=kTst[cs_][:, :, :]),
                    reads=[("kTst", cs_, h_, i_) for h_ in range(2) for i_ in range(4)], writes=[("kT4", ch)])
                sc.dma("sp", lambda: nc.sync.dma_start(
                    out=vtok[c0:c0 + 512, :].rearrange("(i p) f -> p i f", p=128),
                    in_=vst[cs_][:, :, :, :].rearrange("p i a d -> p i (a d)")),
                    reads=[("vst", cs_)], writes=[("vtok", ch)])
                if own:
                    sc.dma("sp", lambda: nc.sync.dma_start(
                        out=qT[:, :, c0:c0 + 512].rearrange("(a hl) d t -> (hl d) a t", hl=2), in_=qTst[cs_][:, :, :]),
                        reads=[("qTst", cs_, w_, i_) for w_ in range(2) for i_ in range(4)],
                        writes=[("qT", ch)])

            tiles = [(ch, i) for ch in range(8) for i in range(4)]
            front(*tiles[0])
            front2(*tiles[0])
            for n_, (ch, i) in enumerate(tiles):
                if n_ + 1 < len(tiles):
                    front(*tiles[n_ + 1])
                back(ch, i)
                if n_ + 1 < len(tiles):
                    front2(*tiles[n_ + 1])
                if i == 3:
                    chunk_store(ch)
            sc.flush()

    if "C" in stages:
        with ExitStack() as stk:
            def sb(name, shape, dtype):
                return stk.enter_context(nc.sbuf_tensor(name, list(shape), dtype))
            kcmpT = sb("kcmpT", [128, 4, 256], BF16)
            vcmp = sb("vcmp", [128, 2, 4, 64], BF16)
            AI = sb("AI", [128, 2, 64], BF16)
            sc.dma("pool", lambda: nc.gpsimd.dma_start(out=AI[:], in_=Aaug_d[:, :].rearrange("p (a c) -> p a c", a=2)[:, :, 0:64]),
                   writes=["AI"])
            all_kT4 = [("kT4", ch) for ch in range(8)]

            with ExitStack() as stk2:
                def sb2(name, shape, dtype):
                    return stk2.enter_context(nc.sbuf_tensor(name, list(shape), dtype))
                kcT = sb2("kcT", [128, 4, 4128], BF16)
                w1 = sb2("w1", [128, 16, 256], BF16)
                w2 = sb2("w2", [128, 2, 128], BF16)
                posT = sb2("posT", [128, 16], BF16)
                hidT = sb2("hidT", [128, 2, 4, 256], BF16)
                pb = sb2("pb", [128, 2], F32)
                ub = sb2("ub", [128, 512], F32)
                tb = sb2("tb", [128, 512], F32)
                sg = sb2("sg", [128, 512], F32)
                for st in range(2):
                    for g in range(4):
                        sc.dma("sp", lambda st=st, g=g: nc.sync.dma_start(out=kcT[0:64, g, 0:4096], in_=kT4[4 * st + g, :, :]),
                               reads=all_kT4, writes=[("kcT", g)])
                        sc.dma("sp", lambda st=st, g=g: nc.sync.dma_start(out=kcT[0:64, g, 4096:4128], in_=kT4[4 * st + g, :, 0:32]),
                               reads=all_kT4, writes=[("kcTw", g)])
                        sc.dma("sp", lambda st=st, g=g: nc.sync.dma_start(out=kcT[64:128, g, 0:4095], in_=kT4[4 * st + g, :, 1:4096]),
                               reads=all_kT4, writes=[("kcT2", g)])
                        sc.dma("sp", lambda st=st, g=g: nc.sync.dma_start(out=kcT[64:128, g, 4095:4127], in_=kT4[4 * st + g, :, 0:32]),
                               reads=all_kT4, writes=[("kcT2w", g)])
                    w1d = w1k_d if st == 0 else w1v_d
                    w2d = w2k_d if st == 0 else w2v_d
                    posd = posk_d if st == 0 else posv_d
                    for q in range(4):
                        sc.dma("pool", lambda q=q, w1d=w1d: nc.gpsimd.dma_start(
                            out=w1[:, 4 * q:4 * q + 4, :], in_=w1d[:, 1024 * q:1024 * (q + 1)].rearrange("p (l n) -> p l n", n=256)),
                            writes=[("w1", q)])
                    sc.dma("pool", lambda w2d=w2d: nc.gpsimd.dma_start(out=w2[:], in_=w2d[:, :].rearrange("p (a n) -> p a n", a=2)),
                           writes=["w2"])
                    sc.dma("pool", lambda posd=posd: nc.gpsimd.dma_start(out=posT[:], in_=posd[:, :]), writes=["posT"])
                    w1keys = [("w1", q) for q in range(4)]
                    kckeys = [(nm, g) for g in range(4) for nm in ("kcT", "kcTw", "kcT2", "kcT2w")]
                    for hc in range(2):
                        bi = nextbank()

                        def mpb(bi=bi, hc=hc):
                            inst = None
                            for l in range(16):
                                inst = nc.tensor.matmul(banks[bi][:, 0:1], lhsT=w1[:, l, hc * 128:(hc + 1) * 128],
                                                        rhs=posT[:, l:l + 1], start=(l == 0), stop=(l == 15))
                            return inst
                        sc.op("pe", mpb, reads=w1keys + ["posT"], writes=[bk(bi)])
                        sc.op("act", lambda bi=bi, hc=hc: nc.scalar.copy(out=pb[:, hc:hc + 1], in_=banks[bi][:, 0:1]),
                              reads=[bk(bi)], writes=[("pb", hc)])
                        for gp in range(2):
                            bh = nextbank()

                            def mh(bh=bh, hc=hc, gp=gp):
                                inst = None
                                for l in range(16):
                                    inst = nc.tensor.matmul(banks[bh][:, :], lhsT=w1[:, l, hc * 128:(hc + 1) * 128],
                                                            rhs=kcT[:, 2 * gp:2 * gp + 2, 2 * l:2 * l + 4096:16],
                                                            start=(l == 0), stop=(l == 15))
                                return inst
                            sc.op("pe", mh, reads=w1keys + kckeys, writes=[bk(bh)])
                            sc.op("act", lambda bh=bh, hc=hc: nc.scalar.activation(
                                out=ub[:], in_=banks[bh][:, :], func=AF.Identity, bias=pb[:, hc:hc + 1]),
                                reads=[bk(bh), ("pb", hc)], writes=["ub"])
                            sc.op("dve", lambda: nc.vector.tensor_tensor(out=tb[:], in0=ub[:], in1=ub[:], op=ALU.mult),
                                  reads=["ub"], writes=["tb"])
                            sc.op("dve", lambda: nc.vector.tensor_scalar(out=tb[:], in0=tb[:], scalar1=0.044715, scalar2=1.0,
                                                                         op0=ALU.mult, op1=ALU.add),
                                  reads=["tb"], writes=["tb"])
                            sc.op("dve", lambda: nc.vector.tensor_tensor(out=tb[:], in0=tb[:], in1=ub[:], op=ALU.mult),
                                  reads=["tb", "ub"], writes=["tb"])
                            sc.op("act", lambda: nc.scalar.activation(out=sg[:], in_=tb[:], func=AF.Sigmoid, scale=1.5957691216057308),
                                  reads=["tb"], writes=["sg"])
                            sc.op("dve", lambda hc=hc, gp=gp: nc.vector.tensor_tensor(
                                out=hidT[:, hc, 2 * gp:2 * gp + 2, :], in0=ub[:, :].rearrange("p (a m) -> p a m", a=2),
                                in1=sg[:, :].rearrange("p (a m) -> p a m", a=2), op=ALU.mult),
                                reads=["ub", "sg"], writes=[("hidT", hc, gp)])
                    hkeys = [("hidT", hc, gp) for hc in range(2) for gp in range(2)]
                    if st == 0:
                        for gp in range(2):
                            bo = nextbank()

                            def mk(bo=bo, gp=gp):
                                inst = None
                                for hc in range(2):
                                    inst = nc.tensor.matmul(banks[bo][:, :], lhsT=w2[:, hc, :],
                                                            rhs=hidT[:, hc, 2 * gp:2 * gp + 2, :], start=(hc == 0), stop=(hc == 1))
                                return inst
                            sc.op("pe", mk, reads=hkeys + ["w2"], writes=[bk(bo)])
                            sc.op("act", lambda bo=bo, gp=gp: nc.scalar.copy(
                                out=kcmpT[:, 2 * gp:2 * gp + 2, :], in_=banks[bo][:, :].rearrange("p (a m) -> p a m", a=2)),
                                reads=[bk(bo)], writes=[("kcmpT", gp)])
                    else:
                        for nt in range(2):
                            bo = nextbank()

                            def mv(bo=bo, nt=nt):
                                inst = None
                                for g in range(4):
                                    for hc in range(2):
                                        inst = nc.tensor.matmul(banks[bo][:, g * 64:(g + 1) * 64],
                                                                lhsT=hidT[:, hc, g, nt * 128:(nt + 1) * 128],
                                                                rhs=w2[:, hc, 0:64], start=(g == 0 and hc == 0), stop=(hc == 1),
                                                                skip_group_check=True)
                                return inst
                            sc.op("pe", mv, reads=hkeys + ["w2"], writes=[bk(bo)])
                            sc.op("act", lambda bo=bo, nt=nt: nc.scalar.copy(
                                out=vcmp[:, nt, :, :], in_=banks[bo][:, 0:256].rearrange("p (g d) -> p g d", g=4)),
                                reads=[bk(bo)], writes=[("vcmp", nt)])
                sc.flush()

            Ksel = sb("Ksel", [128, 4, 4096], BF16)
            KwT = sb("KwT", [128, 4, 4096], BF16)
            Vall = sb("Vall", [128, 32, 520], BF16)
            cmask = sb("cmask_sb", [128, 2, 2048], BF16)
            fmul = sb("fmul_sb", [128, 16, 64], F32)
            fadd = sb("fadd_sb", [128, 16, 64], F32)
            tri = sb("tri_sb", [128, 128], BF16)
            upp = sb("upp_sb", [128, 128], BF16)
            woutb = sb("woutb_sb", [128, 8, 1024], BF16)
            gfin = sb("gfin_sb", [128, 1024], F32)
            Qsel = [sb("Qsel%d" % i, [128, 16, 128], BF16) for i in range(2)]
            Qwin = [sb("Qwin%d" % i, [128, 16, 128], BF16) for i in range(2)]
            Qc = [sb("Qc%d" % i, [128, 16, 128], BF16) for i in range(2)]
            gts = [sb("gts%d" % i, [128, 16, 3], F32) for i in range(2)]
            szs = sb("szs", [128, 1024], F32)
            x1s = sb("x1s", [128, 1024], F32)
            oacc2 = [sb("oacc%d" % i, [128, 16, 64], F32) for i in range(2)]
            otmp = [sb("otmp%d" % i, [128, 4, 64], F32) for i in range(2)]
            NPT = 4
            PTb = [sb("PTb%d" % i, [128, 512], BF16) for i in range(NPT)]
            rs = [sb("rs%d" % i, [128, 4], F32) for i in range(3)]
            rc = [sb("rc%d" % i, [128, 4], F32) for i in range(3)]
            wg = [sb("wg%d" % i, [128, 4], F32) for i in range(3)]
            impb = sb("impb", [128, 64], F32)
            scr = sb("scr", [128, 64], F32)
            scr2 = sb("scr2", [128, 64], F32)
            m8a = sb("m8a", [128, 8], F32)
            m8b = sb("m8b", [128, 8], F32)
            thr = sb("thr", [128, 1], F32)
            negb4 = [sb("negb%d" % i, [128, 128], BF16) for i in range(4)]
            ybf = sb("ybf", [128, 1024], BF16)
            yT = sb("cyT", [128, 8, 128], BF16)
            x2 = sb("x2", [128, 1024], F32)
            fss = sb("fss", [128, 16], F32)
            fsq = sb("fsq", [128, 16], F32)
            frs = sb("frs", [128, 16], F32)
            junk = sb("junk", [128, 1024], BF16)
            osb = sb("osb", [128, 1024], F32)
            identF = sb("identF", [128, 128], F32)
            accT = [sb("accT%d" % i, [128, 512], F32) for i in range(2)]
            sc.dma("sp", lambda: nc.sync.dma_start(out=identF[:], in_=ident_d[:, :]), writes=["identF"])
            for i in range(2):
                sc.op("pool", lambda i=i: nc.gpsimd.memset(accT[i][:], 0.0), writes=[("accT", i)])

            for g in range(4):
                sc.dma("sp", lambda g=g: nc.sync.dma_start(out=Ksel[0:64, g, :], in_=kT4[8 + g, :, :]),
                       reads=all_kT4, writes=[("Ksel", g)])
                sc.op("pool", lambda g=g: nc.gpsimd.memset(KwT[64:128, g, :], 0.0), writes=[("KwTz", g)])
                sc.dma("sp", lambda g=g: nc.sync.dma_start(out=KwT[0:64, g, :], in_=kT4[12 + g, :, :]),
                       reads=all_kT4, writes=[("KwT", g)])
                for hh in range(2):
                    sc.dma("pool", lambda g=g, hh=hh: nc.gpsimd.dma_start(
                        out=Ksel[64:128, g, 2048 * hh:2048 * (hh + 1)], in_=E_d[:, 2048 * hh:2048 * (hh + 1)]),
                        writes=[("KselE", g, hh)])
                    sc.dma("pool", lambda g=g, hh=hh: nc.gpsimd.dma_start(
                        out=KwT[64:65, g, 2048 * hh:2048 * (hh + 1)], in_=kbias_d[:, 2048 * hh:2048 * (hh + 1)]),
                        reads=[("KwTz", g)], writes=[("KwTb", g, hh)])
            for ch in range(8):
                sc.dma("sp", lambda ch=ch: nc.sync.dma_start(
                    out=Vall[:, 4 * ch:4 * ch + 4, :], in_=vtok[512 * ch:512 * (ch + 1), :].rearrange("(i p) f -> p i f", p=128)),
                    reads=[("vtok", ch)], writes=[("Vall", ch)])
            for hh in range(2):
                sc.dma("pool", lambda hh=hh: nc.gpsimd.dma_start(out=cmask[:, hh, :], in_=cmask_d[:, 2048 * hh:2048 * (hh + 1)]),
                       writes=[("cmask", hh)])
            sc.dma("sp", lambda: nc.sync.dma_start(out=fmul[:], in_=fmul_d[:, :].rearrange("p (t j) -> p t j", j=64)), writes=["fmul"])
            sc.dma("sp", lambda: nc.sync.dma_start(out=fadd[:], in_=fadd_d[:, :].rearrange("p (t j) -> p t j", j=64)), writes=["fadd"])
            sc.dma("pool", lambda: nc.gpsimd.dma_start(out=tri[:], in_=tri_d[:, :]), writes=["tri"])
            sc.dma("pool", lambda: nc.gpsimd.dma_start(out=upp[:], in_=upp_d[:, :]), writes=["upp"])
            for q in range(8):
                sc.dma("pool", lambda q=q: nc.gpsimd.dma_start(out=woutb[:, q, :], in_=woutb_d[:, 1024 * q:1024 * (q + 1)]),
                       writes=[("woutb", q)])
            sc.dma("sp", lambda: nc.sync.dma_start(out=gfin[:], in_=gfin_d[:, :]), writes=["gfin"])
            for i in range(2):
                sc.op("pool", lambda i=i: nc.gpsimd.memset(Qwin[i][64:128, :, :], 0.0), writes=[("Qwin1", i)])
                sc.op("pool", lambda i=i: nc.gpsimd.memset(Qwin[i][64:65, :, :], 1.0), reads=[("Qwin1", i)], writes=[("Qwin1", i)])
                sc.op("pool", lambda i=i: nc.gpsimd.memset(Qc[i][64:128, :, :], 0.0), writes=[("Qc0", i)])
            for i in range(4):
                sc.op("pool", lambda i=i: nc.gpsimd.memset(negb4[i][:, 0:64], 0.0), writes=[("negb0", i)])
            ksel_keys = lambda g: [("Ksel", g), ("KselE", g, 0), ("KselE", g, 1)]
            kw_keys = lambda g: [("KwT", g), ("KwTz", g), ("KwTb", g, 0), ("KwTb", g, 1)]
            sbank = [0]
            ptc = [0]

            def emit_score(u):
                bS = sbank[0] % 3
                sbank[0] += 1
                pi = ptc[0] % NPT
                ptc[0] += 1
                lhsT, rhs, rkeys, mask, mkeys = u["score"]
                sc.op("pe", lambda: nc.tensor.matmul(banks[bS][:, :], lhsT=lhsT, rhs=rhs, start=True, stop=True),
                      reads=rkeys, writes=[bk(bS)])
                sc.op("act", lambda: nc.scalar.activation(out=PTb[pi][:], in_=banks[bS][:, :], func=AF.Exp, scale=0.125),
                      reads=[bk(bS)], writes=[("PT", pi)])
                if mask is not None:
                    sc.op("dve", lambda: nc.vector.tensor_tensor(
                        out=PTb[pi][:, :].rearrange("p (h q) -> p h q", h=4),
                        in0=PTb[pi][:, :].rearrange("p (h q) -> p h q", h=4),
                        in1=mask, op=ALU.mult), reads=[("PT", pi)] + list(mkeys), writes=[("PT", pi)])
                return pi

            def emit_pv(u, pi):
                for pv_ in u["pvs"]:
                    if pv_[0] == "vstat":
                        _, accb, lhsT_, rkeys, first, last = pv_
                        sc.op("pe", lambda accb=accb, lhsT_=lhsT_, first=first, last=last: nc.tensor.matmul(
                            banks[accb][0:65, :], lhsT=lhsT_, rhs=PTb[pi][:, :], start=first, stop=last),
                            reads=[("PT", pi)] + rkeys, writes=[bk(accb)])
                        continue
                    (accb, col0, width, rhs_, rkeys, first, last) = pv_

                    def f(accb=accb, col0=col0, width=width, rhs_=rhs_, first=first, last=last):
                        inst = None
                        for h in range(4):
                            inst = nc.tensor.matmul(banks[accb][:, col0 + h * width:col0 + (h + 1) * width],
                                                    lhsT=PTb[pi][:, h * 128:(h + 1) * 128], rhs=rhs_,
                                                    start=(first and h == 0), stop=last, skip_group_check=True)
                        return inst
                    sc.op("pe", f, reads=[("PT", pi)] + rkeys, writes=[bk(accb)])

            def untranspose(accb, ai, then):
                sc.op("dve", lambda: nc.vector.tensor_copy(out=accT[ai][0:65, :], in_=banks[accb][0:65, :]),
                      reads=[bk(accb), ("accT", ai)], writes=[("accT", ai)])

                def tail():
                    def tr4():
                        inst = None
                        for h in range(4):
                            inst = nc.tensor.transpose(out=banks[accb][:, h * 128:(h + 1) * 128],
                                                       in_=accT[ai][:, h * 128:(h + 1) * 128], identity=identF[:])
                        return inst
                    sc.op("pe", tr4, reads=[("accT", ai), "identF"], writes=[bk(accb)])
                    then()
                deferred.append([_DBG.get("dunt", 4), tail])

            def branch_out(accb, g, gidx, gt, slot, stride=65, ob=0):
                oacc = oacc2[ob]
                av = banks[accb][:, 0:4 * stride].rearrange("p (h c) -> p h c", c=stride)
                r_, c_, w_ = rs[slot], rc[slot], wg[slot]
                sc.op("dve", lambda: nc.vector.tensor_scalar(out=r_[:], in0=av[:, :, 64], scalar1=1e-30, scalar2=None,
                                                             op0=ALU.max), reads=[bk(accb)], writes=[("rs", slot)])
                sc.op("dve", lambda: nc.vector.reciprocal(out=c_[:], in_=r_[:]), reads=[("rs", slot)], writes=[("rc", slot)])
                sc.op("dve", lambda: nc.vector.tensor_tensor(out=w_[:], in0=c_[:], in1=gt[:, 4 * g:4 * g + 4, gidx], op=ALU.mult),
                      reads=[("rc", slot), "gts"], writes=[("wg", slot)])
                ot = otmp[slot - 1]
                sc.op("dve", lambda: nc.vector.tensor_tensor(
                    out=ot[:], in0=av[:, :, 0:64], in1=w_[:, :, None].broadcast_to([128, 4, 64]), op=ALU.mult),
                    reads=[bk(accb), ("wg", slot)], writes=[("otmp", slot)])
                sc.op("pool", lambda: nc.gpsimd.tensor_tensor(
                    out=oacc[:, 4 * g:4 * g + 4, :], in0=oacc[:, 4 * g:4 * g + 4, :], in1=ot[:], op=ALU.add),
                    reads=[("otmp", slot), ("oacc", ob, g)], writes=[("oacc", ob, g)])

            def select_chain(qs, g, qb):
                gt = gts[qb]
                oacc = oacc2[qb]
                negb = negb4[g]
                iv = banks[3][:, 0:256].rearrange("p (h c) -> p h c", c=64)
                r_, c_, w_ = rs[0], rc[0], wg[0]
                sc.op("dve", lambda: nc.vector.reduce_sum(out=r_[:], in_=iv, axis=mybir.AxisListType.X),
                      reads=[bk(3)], writes=[("rs", 0)])
                sc.op("dve", lambda: nc.vector.tensor_scalar(out=r_[:], in0=r_[:], scalar1=0.5, scalar2=1e-30,
                                                             op0=ALU.mult, op1=ALU.max), reads=[("rs", 0)], writes=[("rs", 0)])
                sc.op("dve", lambda: nc.vector.reciprocal(out=c_[:], in_=r_[:]), reads=[("rs", 0)], writes=[("rc", 0)])
                sc.op("dve", lambda: nc.vector.tensor_scalar(out=impb[:], in0=iv[:, 0, 0:64], scalar1=c_[:, 0:1],
                                                             scalar2=None, op0=ALU.mult),
                      reads=[bk(3), ("rc", 0)], writes=["impb"])
                for h in range(1, 4):
                    sc.op("dve", lambda h=h: nc.vector.scalar_tensor_tensor(
                        out=impb[:], in0=iv[:, h, 0:64], scalar=c_[:, h:h + 1], in1=impb[:], op0=ALU.mult, op1=ALU.add),
                        reads=[bk(3), ("rc", 0), "impb"], writes=["impb"])
                sc.op("dve", lambda: nc.vector.tensor_tensor(out=w_[:], in0=c_[:], in1=gt[:, 4 * g:4 * g + 4, 0], op=ALU.mult),
                      reads=[("rc", 0), "gts"], writes=[("wg", 0)])
                sc.op("dve", lambda: nc.vector.tensor_tensor(
                    out=oacc[:, 4 * g:4 * g + 4, :], in0=banks[3][:, 256:512].rearrange("p (h c) -> p h c", c=64),
                    in1=w_[:, :, None].broadcast_to([128, 4, 64]), op=ALU.mult),
                    reads=[bk(3), ("wg", 0)], writes=[("oacc", qb, g)])
                sc.op("dve", lambda: nc.vector.tensor_tensor(out=scr[:], in0=impb[:], in1=fmul[:, qs, :], op=ALU.mult),
                      reads=["impb", "fmul"], writes=["scr"])
                sc.op("dve", lambda: nc.vector.tensor_tensor(out=scr[:], in0=scr[:], in1=fadd[:, qs, :], op=ALU.add),
                      reads=["scr", "fadd"], writes=["scr"])
                sc.op("dve", lambda: nc.vector.max(out=m8a[:], in_=scr[:]), reads=["scr"], writes=["m8a"])
                sc.op("dve", lambda: nc.vector.match_replace(out=scr2[:], in_to_replace=m8a[:], in_values=scr[:], imm_value=-2e9),
                      reads=["scr", "m8a"], writes=["scr2"])
                sc.op("dve", lambda: nc.vector.max(out=m8b[:], in_=scr2[:]), reads=["scr2"], writes=["m8b"])
                sc.op("dve", lambda: nc.vector.tensor_scalar(out=thr[:], in0=m8b[:, 7:8], scalar1=-1e8, scalar2=None, op0=ALU.max),
                      reads=["m8b"], writes=["thr"])
                sc.op("dve", lambda: nc.vector.tensor_scalar(out=negb[:, 64:128], in0=scr[:], scalar1=thr[:, 0:1], scalar2=NEG,
                                                             op0=ALU.is_lt, op1=ALU.mult),
                      reads=["scr", "thr"], writes=[("negb", g)])

                def tail():
                    sc.op("pe", lambda: nc.tensor.transpose(out=bfview(7)[:, 0:128], in_=negb[:, :], identity=ident[:]),
                          reads=[("negb", g), ("negb0", g), "ident"], writes=[bk(7)])
                    sc.op("dve", lambda: nc.vector.tensor_copy(
                        out=Qsel[qb][64:128, 4 * g:4 * g + 4, :], in_=bfview(7)[64:128, None, 0:128].broadcast_to([64, 4, 128])),
                        reads=[bk(7)], writes=[("QselB", qb, g)])
                deferred.append([_DBG.get("dsel", 14), tail])

            def load_q(qs):
                qb = qs % 2
                q0 = qs * 128
                sc.dma("sp", lambda: nc.sync.dma_start(
                    out=Qsel[qb][0:64, :, :], in_=qT[16:32, :, q0:q0 + 128].rearrange("a d t -> d a t")),
                    reads=[("qT", qs // 4)], writes=[("QselQ", qb)])
                sc.dma("sp", lambda: nc.sync.dma_start(
                    out=Qwin[qb][0:64, :, :], in_=qT[16:32, :, q0:q0 + 128].rearrange("a d t -> d a t")),
                    reads=[("qT", qs // 4)], writes=[("QwinQ", qb)])
                sc.dma("sp", lambda: nc.sync.dma_start(
                    out=Qc[qb][0:64, :, :], in_=qT[0:16, :, q0:q0 + 128].rearrange("a d t -> d a t")),
                    reads=[("qT", qs // 4)], writes=[("Qc", qb)])
                sc.dma("sp", lambda: nc.sync.dma_start(
                    out=gts[qb][:, :, :], in_=gated[q0:q0 + 128, :].rearrange("p (h c) -> p h c", c=3)),
                    reads=[("gated", qs)], writes=["gts"])

            def load_tail(qs):
                q0 = qs * 128
                sc.dma("sp", lambda: nc.sync.dma_start(out=szs[:], in_=szd[q0:q0 + 128, :]),
                       reads=[("szd", qs)], writes=["szs"])
                sc.dma("sp", lambda: nc.sync.dma_start(out=x1s[:], in_=x1d[q0:q0 + 128, :]),
                       reads=[("x1d", qs)], writes=["x1s"])

            mhalf = sb("mhalf", [128, 1], F32)
            sc.op("pool", lambda: nc.gpsimd.memset(mhalf[:], -0.5), writes=["mhalf"])

            def epilogue(qs):
                q0 = qs * 128
                okeys = [("oacc", qs % 2, g) for g in range(4)]
                oacc = oacc2[qs % 2]
                sc.op("dve", lambda: nc.vector.tensor_tensor(out=ybf[:], in0=oacc[:, :, :].rearrange("p h d -> p (h d)"),
                                                             in1=szs[:], op=ALU.mult),
                      reads=okeys + ["szs"], writes=["ybf"])

                def p1():
                    def try_():
                        inst = None
                        for kc in range(8):
                            inst = nc.tensor.transpose(out=bfview(7)[:, kc * 128:(kc + 1) * 128],
                                                       in_=ybf[:, kc * 128:(kc + 1) * 128], identity=ident[:])
                        return inst
                    sc.op("pe", try_, reads=["ybf", "ident"], writes=[bk(7)])

                def p2():
                    sc.op("dve", lambda: nc.vector.tensor_copy(out=yT[:, :, :], in_=bfview(7)[:, :].rearrange("p (k t) -> p k t", k=8)),
                          reads=[bk(7)], writes=["cyT"])

                def p3():
                    for nh in range(2):
                        def mo(nh=nh):
                            inst = None
                            for kc in range(8):
                                inst = nc.tensor.matmul(banks[4 + nh][:, :], lhsT=yT[:, kc, :],
                                                        rhs=woutb[:, kc, nh * 512:(nh + 1) * 512],
                                                        start=(kc == 0), stop=(kc == 7))
                            return inst
                        sc.op("pe", mo, reads=["cyT"] + [("woutb", q) for q in range(8)], writes=[bk(4 + nh)])

                def p4():
                    for nh in range(2):
                        sc.op("dve", lambda nh=nh: nc.vector.tensor_tensor(
                            out=x2[:, nh * 512:(nh + 1) * 512], in0=banks[4 + nh][:, :], in1=x1s[:, nh * 512:(nh + 1) * 512],
                            op=ALU.add), reads=[bk(4 + nh), "x1s"], writes=[("x2", nh)])
                    sc.op("dve", lambda: nc.vector.scalar_tensor_tensor(
                        out=osb[:], in0=x2[:], scalar=1.0, in1=x2[:], op0=ALU.mult, op1=ALU.mult, accum_out=fss[:, qs:qs + 1]),
                        reads=[("x2", 0), ("x2", 1)], writes=["osb", "fss"])
                    sc.op("pool", lambda: nc.gpsimd.tensor_scalar(out=fsq[:, qs:qs + 1], in0=fss[:, qs:qs + 1], scalar1=1.0 / D,
                                                                  scalar2=EPS, op0=ALU.mult, op1=ALU.add),
                          reads=["fss"], writes=["fsq"])
                    sc.op("pool", lambda: nc.gpsimd.tensor_tensor(out=frs[:, qs:qs + 1], in0=fsq[:, qs:qs + 1], in1=mhalf[:],
                                                                  op=ALU.pow), reads=["fsq", "mhalf"], writes=["frs"])

                def p5():
                    sc.op("dve", lambda: nc.vector.scalar_tensor_tensor(
                        out=osb[:], in0=x2[:], scalar=frs[:, qs:qs + 1], in1=gfin[:], op0=ALU.mult, op1=ALU.mult),
                        reads=[("x2", 0), ("x2", 1), "frs", "gfin", "osb"], writes=["osb"])
                    sc.dma("sp", lambda: nc.sync.dma_start(out=outd[q0:q0 + 128, :], in_=osb[:]),
                           reads=["osb"], writes=[("out", qs)])
                for dly, fn in zip(_DBG.get("depi", (3, 5, 7, 10, 13)), (p1, p2, p3, p4, p5)):
                    deferred.append([dly, fn])

            stream = []

            def cmp_units(qs, g):
                qb = qs % 2
                q0 = qs * 128
                for nt in range(2):
                    stream.append(("unit", {
                        "score": (kcmpT[:, g, nt * 128:(nt + 1) * 128], Qc[qb][:, 4 * g:4 * g + 4, :],
                                  [("kcmpT", g // 2), ("Qc", qb), ("Qc0", qb)],
                                  cmask[:, nt:nt + 1, q0:q0 + 128].broadcast_to([128, 4, 128]), [("cmask", nt)]),
                        "pvs": [(3, 0, 64, AI[:, nt, :], ["AI"], nt == 0, nt == 1),
                                (3, 256, 64, vcmp[:, nt, g, :], [("vcmp", nt)], False, nt == 1)],
                        "post": ((lambda qs=qs, g=g, qb=qb: select_chain(qs, g, qb)) if nt == 1 else None)}))

            load_q(0)
            for g in range(4):
                cmp_units(0, g)
            for qs in range(16):
                qb = qs % 2
                q0 = qs * 128
                for g in range(4):
                    for k in (1, 2, 3, 0, 4):
                        kt = (qs - 4 + k) % 32
                        m = None
                        if k == 0:
                            m = upp[:, None, :].broadcast_to([128, 4, 128])
                        elif k == 4:
                            m = tri[:, None, :].broadcast_to([128, 4, 128])
                        stream.append(("unit", {
                            "score": (KwT[:, g, kt * 128:(kt + 1) * 128], Qwin[qb][:, 4 * g:4 * g + 4, :],
                                      kw_keys(g) + [("QwinQ", qb), ("Qwin1", qb)], m, ["tri", "upp"]),
                            "pvs": [(6, 0, 65, Vall[:, kt, (4 + g) * 65:(5 + g) * 65], [("Vall", kt // 4)], k == 1, k == 4)],
                            "post": ((lambda g=g, qb=qb: branch_out(6, g, 2, gts[qb], 2, ob=qb)) if k == 4 else None)}))
                    if g == 0 and qs + 1 < 16:
                        stream.append(("call", lambda qs=qs: load_q(qs + 1)))
                    if g == 3:
                        stream.append(("call", lambda qs=qs: load_tail(qs)))
                for g in range(4):
                    if qs + 1 < 16:
                        cmp_units(qs + 1, g)
                    klist = list(range(16, 32)) + list(range(0, qs + 1))
                    for idx, kt in enumerate(klist):
                        last = idx == len(klist) - 1
                        post = None
                        sb_ = 4 + g % 2
                        if last:
                            if g < 3:
                                post = (lambda g=g, qb=qb, sb_=sb_: untranspose(
                                    sb_, g % 2, lambda: branch_out(sb_, g, 1, gts[qb], 1, stride=128, ob=qb)))
                            else:
                                post = (lambda g=g, qb=qb, qs=qs, sb_=sb_: untranspose(
                                    sb_, g % 2, lambda: (branch_out(sb_, g, 1, gts[qb], 1, stride=128, ob=qb), epilogue(qs))))
                        stream.append(("unit", {
                            "score": (Ksel[:, g, kt * 128:(kt + 1) * 128], Qsel[qb][:, 4 * g:4 * g + 4, :],
                                      ksel_keys(g) + [("QselQ", qb), ("QselB", qb, g)],
                                      (tri[:, None, :].broadcast_to([128, 4, 128]) if kt == qs else None), ["tri"]),
                            "pvs": [("vstat", sb_, Vall[:, kt, g * 65:(g + 1) * 65], [("Vall", kt // 4)], idx == 0, last)],
                            "post": post}))

            LA = _DBG.get('LA', 2)
            deferred = []
            pend = []

            def retire():
                u, pi = pend.pop(0)
                emit_pv(u, pi)
                if u["post"] is not None:
                    u["post"]()
                for d_ in list(deferred):
                    d_[0] -= 1
                    if d_[0] <= 0:
                        deferred.remove(d_)
                        d_[1]()
            for kind, item in stream:
                if kind == "call":
                    item()
                    continue
                pend.append((item, emit_score(item)))
                if len(pend) > LA:
                    retire()
            while pend:
                retire()
            while deferred:
                deferred.sort(key=lambda x: x[0])
                deferred.pop(0)[1]()
            sc.flush()

    sc.finish()
    return nc, sc


def _shared_layout(inputs):
    f = np.float32
    a_w_in = np.asarray(inputs["a_w_in"], f)[0]
    w = a_w_in.reshape(8, 128, 4, 16, 128).transpose(3, 1, 0, 2, 4).reshape(16, 128, 4096)
    a_gT = np.asarray(inputs["a_norm"], f)[0].reshape(8, 128).T
    a_cw = np.asarray(inputs["a_conv_w"], f)[0].reshape(3, 16, 128).transpose(2, 1, 0).reshape(128, 48)
    w_out = np.asarray(inputs["a_w_out"], f)[0].reshape(16, 128, 1024).transpose(1, 0, 2).reshape(128, 16 * 1024)
    wkv = np.asarray(inputs["w_kv"], f).reshape(1024, 6, 256)[:, [0, 1, 2, 4, 3, 5], :].reshape(1024, 1536)
    wkv_r = wkv.reshape(8, 128, 1536).transpose(1, 0, 2).reshape(128, 8 * 1536)
    bw = np.asarray(inputs["b_w_in"], f)[0]
    bw = np.concatenate([bw[:, 0:1024], bw[:, 1072:2096], bw[:, 1024:1072]], axis=1)
    wbin_r = bw.reshape(8, 128, 2096).transpose(1, 0, 2).reshape(128, 8 * 2096)
    woutb = np.asarray(inputs["b_w_out"], f)[0].reshape(8, 128, 1024).transpose(1, 0, 2).reshape(128, 8 * 1024)

    def w1l(w1):
        return np.asarray(w1, f).reshape(16, 128, 256).transpose(1, 0, 2).reshape(128, 16 * 256)

    def w2l(w2):
        w = np.zeros((128, 2, 128), f)
        w[:, :, 0:64] = np.asarray(w2, f).reshape(2, 128, 64).transpose(1, 0, 2)
        return w.reshape(128, 256)
    tri = (np.arange(128)[:, None] <= np.arange(128)[None, :]).astype(f)
    upp = (np.arange(128)[:, None] > np.arange(128)[None, :]).astype(f)
    sh = {
        "w_in_r": w, "a_gT": a_gT, "a_cw": a_cw, "w_out_r": w_out, "ident": np.eye(128, dtype=f),
        "wkv_r": wkv_r, "wbin_r": wbin_r,
        "gkvT": np.asarray(inputs["kv_norm"], f).reshape(8, 128).T,
        "gbT": np.asarray(inputs["b_norm"], f)[0].reshape(8, 128).T,
        "w1k": w1l(inputs["cmp_w1_k"]), "w1v": w1l(inputs["cmp_w1_v"]),
        "w2k": w2l(inputs["cmp_w2_k"]), "w2v": w2l(inputs["cmp_w2_v"]),
        "poskT": np.asarray(inputs["cmp_pos_k"], f).reshape(16, 128).T, "posvT": np.asarray(inputs["cmp_pos_v"], f).reshape(16, 128).T,
        "tri": tri, "upp": upp, "woutb": woutb,
        "gfin": np.broadcast_to(np.asarray(inputs["final_norm"], f)[None, :], (128, 1024)),
    }
    return {k: np.ascontiguousarray(v, dtype=f) for k, v in sh.items()}


def _parity_consts(par):
    f = np.float32
    pos = (np.arange(S) + 2048 * par) % S
    inv = (np.float32(500000.0) ** (-np.arange(0, 16, 2, dtype=f) / np.float32(16))).astype(f)
    ang = pos.astype(f)[:, None] * inv[None, :]
    cosT = np.cos(ang).astype(f).reshape(32, 128, 8).transpose(1, 0, 2).reshape(128, 256)
    sinT = np.sin(ang).astype(f).reshape(32, 128, 8).transpose(1, 0, 2).reshape(128, 256)
    E = (pos[None, :] // 64 == np.arange(64)[:, None]).astype(f)
    kbias = np.zeros((1, S), f)
    if par == 0:
        kbias[0, 3584:] = NEG
    tq = np.arange(2048) + 2048 * par
    m = np.arange(256)
    a = (16 * m + 2048 * par) % S
    valid_m = (a + 31) < S
    n = a // 16
    cm = (valid_m[:, None] & ((16 * n + 31)[:, None] <= tq[None, :])).astype(f)
    cmask = cm.reshape(2, 128, 2048).transpose(1, 0, 2).reshape(128, 4096)
    A = np.zeros((256, 65), f)
    for mi in range(256):
        if not valid_m[mi] or n[mi] > 254:
            continue
        j, r = divmod(int(n[mi]), 4)
        A[mi, j] += 2.0 if r < 3 else 1.0
        if r == 3 and j + 1 < 64:
            A[mi, j + 1] += 1.0
    A[:, 64] = 1.0
    Aaug = A.reshape(2, 128, 65).transpose(1, 0, 2).reshape(128, 130)
    cur = tq // 64
    jj = np.arange(64)[None, :]
    validb = jj <= cur[:, None]
    forced = (jj == 0) | (validb & (jj > cur[:, None] - 2))
    fmul = (validb & ~forced).astype(f)
    fadd = np.where(forced, f(1e4), np.where(validb, f(0.0), f(-1e9))).astype(f)
    fmul = fmul.reshape(16, 128, 64).transpose(1, 0, 2).reshape(128, 1024)
    fadd = fadd.reshape(16, 128, 64).transpose(1, 0, 2).reshape(128, 1024)
    c = {"cosT": cosT, "sinT": sinT, "Emat": E, "kbias": kbias, "cmask": cmask, "Aaug": Aaug, "fmul": fmul, "fadd": fadd}
    return {k: np.ascontiguousarray(v, dtype=f) for k, v in c.items()}


def make_in_maps(inputs, cores=range(NCORES)):
    sh = _shared_layout(inputs)
    pc = [_parity_consts(0), _parity_consts(1)]
    x = np.asarray(inputs["x"], np.float32)
    maps = []
    for c in cores:
        b, par = c // 2, c % 2
        m = dict(sh)
        m.update(pc[par])
        m["xb"] = np.ascontiguousarray(np.roll(x[b], -2048 * par, axis=0))
        xh = np.zeros((4, D), np.float32)
        if par == 0:
            xh[2:4] = x[b, 2046:2048]
        else:
            xh[0:2] = x[b, 2046:2048]
        m["xh"] = xh
        maps.append(m)
    return maps


_CACHE = {}


def kernel(**inputs):
    if "nc" not in _CACHE:
        _CACHE["nc"] = build_program()[0]
    nc = _CACHE["nc"]
    in_maps = make_in_maps(inputs)
    res = run_bass_kernel_spmd(nc, in_maps, core_ids=list(range(NCORES)))
    out = np.empty((4, S, D), np.float32)
    for c in range(NCORES):
        b, par = c // 2, c % 2
        out[b, 2048 * par:2048 * (par + 1)] = res.results[c]["out"]
    return out
```
